# Optimizing a Trainium2 kernel written in Bass

```python
import math
import jax, jax.numpy as jnp
from jax import lax
import numpy as np

D_MODEL = 1024
BATCH = 2
SEQ = 8192
DEPTH = 2
DEC_BATCH = 32
DEC_SEQ = 2048
PAST_LEN = 128

GRID_W = 64
N_MIXERS = 2
N_LAYERS_A = (DEPTH + 1) // 2
N_LAYERS_B = DEPTH // 2
RMS_EPS = 1e-6
L2_EPS = 1e-6
NEG_INF = -1e30

NA_HEADS = 16
NA_HEAD_DIM = D_MODEL // NA_HEADS
NA_WIDTH = NA_HEADS * NA_HEAD_DIM
NA_WIN_H = 8
NA_WIN_W = 16
NA_QBLOCK_W = 16
NA_KBLOCK_W = NA_QBLOCK_W + NA_WIN_W
NA_N_CBLOCKS = GRID_W // NA_QBLOCK_W
NA_IN = 4 * NA_WIDTH

GDN_HEADS = 8
GDN_DK = 128
GDN_DV = 256
GDN_KW = GDN_HEADS * GDN_DK
GDN_VW = GDN_HEADS * GDN_DV
GDN_CONV = 5
GDN_CONV_CH = 2 * GDN_KW + GDN_VW
GDN_CHUNK = 64
GDN_IN = 2 * GDN_KW + 2 * GDN_VW + 4 * GDN_HEADS

kernel_name = "hybrid_natten_gdn_encoder"


def _rmsnorm(x, w):
    xf = x.astype(jnp.float32)
    y = xf * lax.rsqrt(jnp.mean(jnp.square(xf), axis=-1, keepdims=True) + RMS_EPS)
    return (y * w.astype(jnp.float32)).astype(x.dtype)


def _l2norm(x):
    xf = x.astype(jnp.float32)
    return xf * lax.rsqrt(jnp.sum(jnp.square(xf), axis=-1, keepdims=True) + L2_EPS)


def _na_column_tables():
    j = np.arange(NA_N_CBLOCKS)
    slab_start = np.clip(j * NA_QBLOCK_W - NA_WIN_W // 2, 0, GRID_W - NA_KBLOCK_W)
    key_col = slab_start[:, None] + np.arange(NA_KBLOCK_W)[None, :]
    q_col = j[:, None] * NA_QBLOCK_W + np.arange(NA_QBLOCK_W)[None, :]
    win_start = np.clip(q_col - NA_WIN_W // 2, 0, GRID_W - NA_WIN_W)
    rel = key_col[:, None, :] - win_start[:, :, None]
    mask = (rel >= 0) & (rel < NA_WIN_W)
    dc = key_col[:, None, :] - q_col[:, :, None]
    dc_idx = np.clip(dc + NA_WIN_W - 1, 0, 2 * NA_WIN_W - 2)
    return key_col, mask, dc_idx


def _neighbourhood_attention(h, w_in, rpb, w_out):
    b, t, _ = h.shape
    rows = t // GRID_W
    kh = min(NA_WIN_H, rows)
    proj = h @ w_in
    q, k, v, g = jnp.split(proj, 4, axis=-1)
    grid = lambda z: z.reshape(b, rows, GRID_W, NA_HEADS, NA_HEAD_DIM)
    q = grid(q) * (NA_HEAD_DIM ** -0.5)
    k = grid(k)
    v = grid(v)
    key_col, col_mask, dc_idx = _na_column_tables()
    rpb_cols = rpb.astype(jnp.float32)[:, :, dc_idx]
    mask = col_mask[:, :, None, :]

    def one_row(r):
        rs = jnp.clip(r - kh // 2, 0, rows - kh)
        q_r = lax.dynamic_index_in_dim(q, r, axis=1, keepdims=False)
        q_r = q_r.reshape(b, NA_N_CBLOCKS, NA_QBLOCK_W, NA_HEADS, NA_HEAD_DIM)
        k_s = lax.dynamic_slice_in_dim(k, rs, kh, axis=1)[:, :, key_col]
        v_s = lax.dynamic_slice_in_dim(v, rs, kh, axis=1)[:, :, key_col]
        s = jnp.einsum('bjqhd,bijkhd->bhjqik', q_r, k_s).astype(jnp.float32)
        dr = rs + jnp.arange(kh) - r + (NA_WIN_H - 1)
        bias = jnp.take(rpb_cols, dr, axis=1).transpose(0, 2, 3, 1, 4)
        s = jnp.where(mask, s + bias, NEG_INF)
        p = jax.nn.softmax(s.reshape(b, NA_HEADS, NA_N_CBLOCKS, NA_QBLOCK_W, kh * NA_KBLOCK_W), axis=-1)
        p = p.reshape(s.shape).astype(v.dtype)
        o = jnp.einsum('bhjqik,bijkhd->bjqhd', p, v_s)
        return o.reshape(b, GRID_W, NA_WIDTH)

    o = lax.map(one_row, jnp.arange(rows))
    o = o.transpose(1, 0, 2, 3).reshape(b, t, NA_WIDTH)
    return (o * jax.nn.silu(g)) @ w_out


def _centred_depthwise_conv(x, w):
    pad = GDN_CONV // 2
    return lax.conv_general_dilated(x, w[:, None, :], window_strides=(1,), padding=[(pad, pad)],
                                    dimension_numbers=('NWC', 'WIO', 'NWC'),
                                    feature_group_count=x.shape[-1])


def _chunk_gated_delta(q, k, v, g, beta):
    b, t, h, dk = q.shape
    dv = v.shape[-1]
    c = GDN_CHUNK
    n = t // c
    ch = lambda z: z.transpose(0, 2, 1, 3).reshape(b, h, n, c, z.shape[-1])
    q, k, v = ch(q), ch(k), ch(v)
    beta = beta.transpose(0, 2, 1).reshape(b, h, n, c)
    g = jnp.cumsum(g.transpose(0, 2, 1).reshape(b, h, n, c), axis=-1)
    tril = np.tril(np.ones((c, c), dtype=bool))
    tril_strict = np.tril(np.ones((c, c), dtype=bool), -1)
    diff = g[..., :, None] - g[..., None, :]
    decay = jnp.where(tril, jnp.exp(jnp.where(tril, diff, 0.0)), 0.0)
    k_beta = k * beta[..., None]
    a_kk = jnp.where(tril_strict, jnp.einsum('bhncd,bhnsd->bhncs', k_beta, k) * decay, 0.0)
    lhs = jnp.eye(c, dtype=jnp.float32) + a_kk
    w = lax.linalg.triangular_solve(lhs, k_beta * jnp.exp(g)[..., None], left_side=True,
                                    lower=True, unit_diagonal=True)
    u = lax.linalg.triangular_solve(lhs, v * beta[..., None], left_side=True,
                                    lower=True, unit_diagonal=True)
    a_qk = jnp.einsum('bhncd,bhnsd->bhncs', q, k) * decay

    def step(state, xs):
        q_i, k_i, u_i, w_i, g_i, a_i = xs
        v_new = u_i - jnp.einsum('bhcd,bhde->bhce', w_i, state)
        o = jnp.einsum('bhcd,bhde->bhce', q_i * jnp.exp(g_i)[..., None], state) \
            + jnp.einsum('bhcs,bhse->bhce', a_i, v_new)
        g_last = g_i[..., -1]
        state = state * jnp.exp(g_last)[..., None, None] + jnp.einsum(
            'bhcd,bhce->bhde', k_i * jnp.exp(g_last[..., None] - g_i)[..., None], v_new)
        return state, o

    mv = lambda z: jnp.moveaxis(z, 2, 0)
    s0 = jnp.zeros((b, h, dk, dv), jnp.float32)
    _, o = lax.scan(step, s0, (mv(q), mv(k), mv(u), mv(w), mv(g), mv(a_qk)))
    return jnp.moveaxis(o, 0, 2).reshape(b, h, t, dv).transpose(0, 2, 1, 3)


def _gated_deltanet(h, w_in, conv_w, a_log, dt_bias, norm_w, w_out):
    b, t, _ = h.shape
    proj = h @ w_in
    o1 = GDN_CONV_CH
    o2 = o1 + GDN_VW
    o3 = o2 + 2 * GDN_HEADS
    qkv = jax.nn.silu(_centred_depthwise_conv(proj[..., :o1], conv_w))
    z = proj[..., o1:o2].reshape(b, t, GDN_HEADS, GDN_DV)
    beta = jax.nn.sigmoid(proj[..., o2:o3].astype(jnp.float32)).reshape(b, t, 2, GDN_HEADS)
    a_in = proj[..., o3:].astype(jnp.float32).reshape(b, t, 2, GDN_HEADS)
    g = -jnp.exp(a_log.astype(jnp.float32)) * jax.nn.softplus(a_in + dt_bias.astype(jnp.float32))
    q = _l2norm(qkv[..., :GDN_KW].reshape(b, t, GDN_HEADS, GDN_DK)) * (GDN_DK ** -0.5)
    k = _l2norm(qkv[..., GDN_KW:2 * GDN_KW].reshape(b, t, GDN_HEADS, GDN_DK))
    v = qkv[..., 2 * GDN_KW:].reshape(b, t, GDN_HEADS, GDN_DV).astype(jnp.float32)
    fl = lambda x: jnp.flip(x, axis=1)
    o_fwd = _chunk_gated_delta(q, k, v, g[:, :, 0], beta[:, :, 0])
    o_bwd = _chunk_gated_delta(fl(q), fl(k), fl(v), fl(g[:, :, 1]), fl(beta[:, :, 1]))
    o = (o_fwd + fl(o_bwd)).astype(h.dtype)
    o = _rmsnorm(o, norm_w) * jax.nn.silu(z)
    return o.reshape(b, t, GDN_VW) @ w_out


def _trunk(x, ln_w, na_w_in, na_rpb, na_w_out, gdn_w_in, gdn_conv_w, gdn_a_log, gdn_dt_bias,
           gdn_norm_w, gdn_w_out, final_norm_w):
    h = x
    for i in range(DEPTH):
        hn = _rmsnorm(h, ln_w[i])
        li = i // N_MIXERS
        if i % N_MIXERS == 0:
            h = h + _neighbourhood_attention(hn, na_w_in[li], na_rpb[li], na_w_out[li])
        else:
            h = h + _gated_deltanet(hn, gdn_w_in[li], gdn_conv_w[li], gdn_a_log[li], gdn_dt_bias[li],
                                    gdn_norm_w[li], gdn_w_out[li])
    return _rmsnorm(h, final_norm_w)


def setup_inputs(seed: int = 0) -> dict:
    key = jax.random.key(seed)
    ks = jax.random.split(key, 13)
    f32 = jnp.float32
    nrm = lambda k, shape, scale: jax.random.normal(k, shape, f32) * scale
    x_prompt = nrm(ks[0], (BATCH, SEQ, D_MODEL), 1.0)
    x_sample = nrm(ks[1], (DEC_BATCH, DEC_SEQ, D_MODEL), 1.0)
    ln_w = 1.0 + nrm(ks[2], (DEPTH, D_MODEL), 0.02)
    na_w_in = nrm(ks[3], (N_LAYERS_A, D_MODEL, NA_IN), D_MODEL ** -0.5)
    na_rpb = nrm(ks[4], (N_LAYERS_A, NA_HEADS, 2 * NA_WIN_H - 1, 2 * NA_WIN_W - 1), 0.05)
    na_w_out = nrm(ks[5], (N_LAYERS_A, NA_WIDTH, D_MODEL), NA_WIDTH ** -0.5)
    gdn_w_in = nrm(ks[6], (N_LAYERS_B, D_MODEL, GDN_IN), D_MODEL ** -0.5)
    gdn_conv_w = nrm(ks[7], (N_LAYERS_B, GDN_CONV, GDN_CONV_CH), GDN_CONV ** -0.5)
    gdn_a_log = jnp.log(jax.random.uniform(ks[8], (N_LAYERS_B, 2, GDN_HEADS), f32, 1.0, 16.0))
    dt = jnp.exp(jax.random.uniform(ks[9], (N_LAYERS_B, 2, GDN_HEADS), f32,
                                    math.log(1e-3), math.log(1e-1)))
    gdn_dt_bias = dt + jnp.log(-jnp.expm1(-dt))
    gdn_norm_w = 1.0 + nrm(ks[10], (N_LAYERS_B, GDN_DV), 0.02)
    gdn_w_out = nrm(ks[11], (N_LAYERS_B, GDN_VW, D_MODEL), GDN_VW ** -0.5)
    final_norm_w = 1.0 + nrm(ks[12], (D_MODEL,), 0.02)
    return {"x_prompt": x_prompt, "x_sample": x_sample, "ln_w": ln_w, "na_w_in": na_w_in,
            "na_rpb": na_rpb, "na_w_out": na_w_out, "gdn_w_in": gdn_w_in, "gdn_conv_w": gdn_conv_w,
            "gdn_a_log": gdn_a_log, "gdn_dt_bias": gdn_dt_bias, "gdn_norm_w": gdn_norm_w,
            "gdn_w_out": gdn_w_out, "final_norm_w": final_norm_w}


def reference(x_prompt, x_sample, ln_w, na_w_in, na_rpb, na_w_out, gdn_w_in, gdn_conv_w, gdn_a_log,
              gdn_dt_bias, gdn_norm_w, gdn_w_out, final_norm_w):
    y_prompt = _trunk(x_prompt, ln_w, na_w_in, na_rpb, na_w_out, gdn_w_in, gdn_conv_w, gdn_a_log,
                      gdn_dt_bias, gdn_norm_w, gdn_w_out, final_norm_w)
    y_sample = _trunk(x_sample, ln_w, na_w_in, na_rpb, na_w_out, gdn_w_in, gdn_conv_w, gdn_a_log,
                      gdn_dt_bias, gdn_norm_w, gdn_w_out, final_norm_w)
    return (y_prompt, y_sample)
```

```python
import contextlib
import numpy as np
import concourse.bass as bass
import concourse.mybir as mybir
from concourse.bass_utils import run_bass_kernel_spmd

F32 = mybir.dt.float32
BF16 = mybir.dt.bfloat16
F32R = mybir.dt.float32r
ALU = mybir.AluOpType
AF = mybir.ActivationFunctionType

D = 1024
NH_A = 16
GW = 64
NH_G = 8
DK = 128
DV = 256
NSEG = 5
GRP = 4
BIG = 1.0e5
RMS_EPS = 1e-6
L2_EPS = 1e-6

NSLOT = 32
DMA_QS = ("sp", "pool")


class Buf:
    __slots__ = ("w", "r")

    def __init__(self):
        self.w = {}
        self.r = {}


class PBuf(Buf):
    __slots__ = ("bank",)

    def __init__(self):
        Buf.__init__(self)
        self.bank = Buf()


class _Rec:
    def __getattr__(self, name):
        def f(*a, **k):
            self.call = (name, a, k)
            return self
        return f


class Prog:
    def __init__(self, nc):
        self.nc = nc
        self.streams = {e: [] for e in ("pe", "act", "dve", "pool", "sp")}
        self.count = {e: 0 for e in self.streams}
        self.waited = {e: {} for e in self.streams}
        self.dma_n = {q: 0 for q in DMA_QS}
        self.semh = {}
        self.ninst = 0
        self.rr = 0
        self.only_sp = False

    def _deps(self, eng, reads, writes):
        deps = {}
        for b in reads:
            for k, v in b.w.items():
                if deps.get(k, 0) < v:
                    deps[k] = v
        for b in writes:
            for k, v in b.w.items():
                if deps.get(k, 0) < v:
                    deps[k] = v
            for k, v in b.r.items():
                if deps.get(k, 0) < v:
                    deps[k] = v
        for b in list(reads) + list(writes):
            if isinstance(b, PBuf):
                for k, v in b.bank.w.items():
                    if k != eng and deps.get(k, 0) < v:
                        deps[k] = v
        wt = self.waited[eng]
        out = []
        for k, v in deps.items():
            if k == "pe" and eng == "pe":
                continue
            if wt.get(k, 0) >= v:
                continue
            wt[k] = v
            out.append((k, v))
        return out

    def op(self, eng, fn, reads=(), writes=()):
        rec = _Rec()
        fn(rec)
        cname, cargs, ckw = rec.call

        def fn(e, cname=cname, cargs=cargs, ckw=ckw):
            return getattr(e, cname)(*cargs, **ckw)
        waits = self._deps(eng, reads, writes)
        self.count[eng] += 1
        n = self.count[eng]
        self.streams[eng].append((waits, fn, (eng, 1)))
        for b in reads:
            b.r[eng] = n
            if isinstance(b, PBuf):
                b.bank.w[eng] = n
        for b in writes:
            b.w[eng] = n
            b.r = {}
            if isinstance(b, PBuf):
                b.bank.w[eng] = n
        self.ninst += 1

    def dma(self, q, out, in_, reads=(), writes=()):
        if q is None:
            q = "sp"
            self.rr += 1
        waits = self._deps(q, reads, writes)
        n = self.dma_n[q]
        self.dma_n[q] += 1
        slot = n % NSLOT
        val = 16 * (n // NSLOT + 1)
        key = "d_%s_%d" % (q, slot)
        if n >= NSLOT and self.waited[q].get(key, 0) < val - 16:
            self.waited[q][key] = val - 16
            waits.append((key, val - 16))

        def fn(e, out=out, in_=in_):
            return e.dma_start(out=out, in_=in_)
        self.streams[q].append((waits, fn, (key, 16)))
        for b in reads:
            b.r[key] = val
        for b in writes:
            b.w[key] = val
            b.r = {}
        self.ninst += 1

    def alloc_sems(self, st):
        keys = ["pe", "act", "dve", "pool"]
        for q in DMA_QS:
            for s_ in range(NSLOT):
                keys.append("d_%s_%d" % (q, s_))
        for k in keys:
            self.semh[k] = st.enter_context(self.nc.semaphore("s_" + k))

    def _all_events(self):
        fin = []
        for e in ("pe", "act", "dve", "pool"):
            if self.count[e]:
                fin.append((e, self.count[e]))
        for q in DMA_QS:
            n = self.dma_n[q]
            for s_ in range(NSLOT):
                cnt = (n - s_ + NSLOT - 1) // NSLOT if n > s_ else 0
                if cnt:
                    fin.append(("d_%s_%d" % (q, s_), 16 * cnt))
        return fin

    def flush(self):
        nc = self.nc
        fin = self._all_events()
        semh = self.semh
        streams = self.streams
        self.streams = {e: [] for e in streams}
        with nc.Block() as block:
            def run(engobj, name):
                for waits, fn, (ik, iv) in streams[name]:
                    for k, v in waits:
                        engobj.wait_ge(semh[k], v)
                    ins = fn(engobj)
                    ins.then_inc(semh[ik], iv)
                wt = self.waited[name]
                for k, v in fin:
                    if wt.get(k, 0) >= v:
                        continue
                    wt[k] = v
                    engobj.wait_ge(semh[k], v)

            @block.sync
            def _(e):
                run(e, "sp")

            @block.tensor
            def _(e):
                run(e, "pe")

            @block.scalar
            def _(e):
                run(e, "act")

            @block.vector
            def _(e):
                run(e, "dve")

            @block.gpsimd
            def _(e):
                run(e, "pool")


class Rot:
    def __init__(self, items):
        self.items = items
        self.i = 0

    def next(self):
        it = self.items[self.i % len(self.items)]
        self.i += 1
        return it


def na_chunk_range(i, nb, nbs):
    p = i % nbs
    lo, hi = i - 2, i + 2
    if p == 0:
        hi = i + 3
    if p == nbs - 1:
        lo = i - 3
    return max(lo, 0), min(hi, nb - 1)


def na_special_pos(p, nbs):
    sp = sorted(set([0, 1, nbs - 2, nbs - 1]))
    if nbs <= 4:
        sp = list(range(nbs))
    return sp.index(p) if p in sp else None


def na_nspecial(nbs):
    return min(nbs, 4)


def na_table_id(seg, p, nbs):
    spi = na_special_pos(p, nbs)
    if spi is None:
        return 0
    nsp = na_nspecial(nbs)
    if seg == GRP:
        kind = 2
    else:
        top = p < nbs // 2 if nbs > 2 else (p == 0)
        if top:
            kind = 0 if seg == 0 else 1
        else:
            kind = 0 if seg == GRP - 1 else 1
    return 1 + kind * nsp + spi


def build_na_bias_tables(rpb, seg_t, joined):
    rs_rows = seg_t // GW
    nbs = rs_rows // 2
    nsp = na_nspecial(nbs)
    ntab = 1 + 3 * nsp
    tabs = np.full((ntab, 128, NH_A, 6, 128), -BIG, dtype=np.float32)
    done = set()

    def fill(tid, i, nb, seq_of_row):
        if tid in done:
            return
        done.add(tid)
        lo, hi = na_chunk_range(i, nb, nbs)
        a = np.arange(2)[:, None, None, None]
        kc = np.arange(GW)[None, :, None, None]
        b = np.arange(2)[None, None, :, None]
        qc = np.arange(GW)[None, None, None, :]
        for s, m in enumerate(range(lo, hi + 1)):
            kr = 2 * m + a
            qr = 2 * i + b
            s_lo_q, s_r_q = seq_of_row(qr)
            s_lo_k, _ = seq_of_row(kr)
            rs = np.clip(qr - s_lo_q - 4, 0, s_r_q - 8)
            okr = (s_lo_k == s_lo_q) & (kr - s_lo_q >= rs) & (kr - s_lo_q < rs + 8)
            ws = np.clip(qc - 8, 0, GW - 16)
            okc = (kc >= ws) & (kc < ws + 16)
            ok = np.broadcast_to(okr & okc, (2, GW, 2, GW))
            dr = np.broadcast_to(np.clip(kr - qr + 7, 0, 14), (2, GW, 2, GW))
            dc = np.broadcast_to(np.clip(kc - qc + 15, 0, 30), (2, GW, 2, GW))
            vals = rpb[:, dr, dc]
            vals = np.where(ok[None], vals, np.float32(-BIG)).astype(np.float32)
            tabs[tid, :, :, s, :] = vals.reshape(NH_A, 128, 128).transpose(1, 0, 2)

    nbg = GRP * nbs

    def seq_group(row):
        if joined:
            return np.zeros_like(row), np.full_like(row, GRP * rs_rows)
        return (row // rs_rows) * rs_rows, np.full_like(row, rs_rows)

    def seq_single(row):
        return np.zeros_like(row), np.full_like(row, rs_rows)

    for seg in range(GRP):
        for p in range(nbs):
            tid = na_table_id(seg, p, nbs)
            fill(tid, seg * nbs + p, nbg, seq_group)
    for p in range(nbs):
        fill(na_table_id(GRP, p, nbs), p, nbs, seq_single)
    return tabs


def build_program(seg_t, debug=()):
    NT = NSEG * seg_t
    NCH = NT // 128
    NCS = seg_t // 128
    NBS = NCS
    nsp = na_nspecial(NBS)
    NTAB = 1 + 3 * nsp
    assert seg_t % 512 == 0

    nc = bass.Bass("TRN2", target_bir_lowering=False)

    def din(name, shape, dt=F32):
        return nc.dram_tensor(name, list(shape), dt, kind="ExternalInput").ap()

    def dscr(name, shape, dt):
        kind = "ExternalOutput" if name in debug else "Internal"
        return nc.dram_tensor(name, list(shape), dt, kind=kind).ap()

    x_in = din("x", [NT, D])
    ln_w_bc = din("ln_w_bc", [2, D])
    fin_w = din("fin_w", [1, D])
    na_w_in = din("na_w_in", [D, 4 * D])
    na_bias = din("na_bias", [NTAB, 128, NH_A, 6 * 128])
    na_w_out = din("na_w_out", [D, D])
    g_w_in = din("g_w_in", [D, 6176])
    g_cw = din("g_cw", [128, 32, 5])
    g_alog = din("g_alog", [1, 16])
    g_dtb = din("g_dtb", [1, 16])
    g_nw = din("g_nw", [1, DV])
    g_w_out = din("g_w_out", [2 * D, D])
    consts = din("consts", [128, 8, 128])
    flag_in = din("flag", [128, 1])
    y_out = nc.dram_tensor("y", [NT, D], F32, kind="ExternalOutput").ap()

    qT_s = dscr("qT_s", [NCH, 128, 8, 128], BF16)
    kT_s = dscr("kT_s", [NCH, 128, 8, 128], BF16)
    v_s = dscr("v_s", [NT, D], BF16)
    sg_s = dscr("sg_s", [NT, D], BF16)
    eb_s = dscr("eb_s", [NTAB, 128, NH_A, 768], BF16)
    h1_s = dscr("h1_s", [NT, D], F32)
    gq_s = dscr("gq_s", [NCH, 128, 8, 128], BF16)
    gk_s = dscr("gk_s", [NCH, 128, 8, 128], BF16)
    gkt_s = dscr("gkt_s", [NT, 8, 128], BF16)
    gv_s = dscr("gv_s", [NT, 8, 256], BF16)
    z_s = dscr("z_s", [NT, 2 * D], BF16)
    bg_s = dscr("bg_s", [NT, 32], F32)
    gcr_s = dscr("gcr_s", [2, 8, NT], F32)
    o_s = [dscr("of_s", [NT, 8, 256], F32), dscr("ob_s", [NT, 8, 256], F32)]

    P = Prog(nc)
    B_x = Buf()
    B_scr = {n: Buf() for n in ("qT", "kT", "v", "sg", "eb", "h1", "gq", "gk", "gkt", "gv", "z", "bg", "gcr",
                                "of", "ob")}

    with contextlib.ExitStack() as st0:
        P.alloc_sems(st0)

        def mk(st, name, shape, dt, psum=False):
            if psum:
                esz = 2 if dt == BF16 else 4
                per_bank = 2048 // esz
                assert len(shape) == 2
                ncols = ((shape[1] + per_bank - 1) // per_bank) * per_bank
                t_ = st.enter_context(nc.psum_tensor(name, [shape[0], ncols], dt))
                return t_[:, 0:shape[1]], PBuf()
            return st.enter_context(nc.sbuf_tensor(name, list(shape), dt)), Buf()

        cst, Bc = mk(st0, "cst", [128, 8, 128], F32)
        identb, Bidb = mk(st0, "identb", [128, 128], BF16)
        flagt, Bflag = mk(st0, "flagt", [128, 1], F32)
        P.dma("sp", cst[:], consts[:, :, :], writes=[Bc])
        P.dma("sp", flagt[:], flag_in[:, :], writes=[Bflag])
        P.op("dve", lambda e: e.tensor_copy(identb[:], cst[:, 0, :]), reads=[Bc], writes=[Bidb])
        identf = cst[:, 0, :]
        onesf = cst[:, 1, :]
        trif = cst[:, 2, :]
        trib = cst[:, 3, :]
        maskD = [cst[:, 4, :], cst[:, 5, :]]
        maskS = [cst[:, 6, :], cst[:, 7, :]]

        def load_weight_bf16(st, name, src, kchunks, ncols, stage_cols=2048):
            wt, Bw = mk(st, name, [128, kchunks, ncols], BF16)
            engs = ["act", "dve", "pool"]
            k = 0
            with contextlib.ExitStack() as stl:
                stg = [mk(stl, "%s_stg%d" % (name, i), [128, stage_cols], F32) for i in range(2)]
                it = 0
                for c in range(kchunks):
                    for c0 in range(0, ncols, stage_cols):
                        w_ = min(stage_cols, ncols - c0)
                        s_t, s_b = stg[it % 2]
                        it += 1
                        P.dma(None, s_t[:, 0:w_], src[c * 128:(c + 1) * 128, c0:c0 + w_], writes=[s_b])
                        en = engs[k % 3]
                        k += 1
                        if en == "act":
                            P.op("act", lambda e, s_t=s_t, c=c, c0=c0, w_=w_: e.activation(
                                wt[:, c, c0:c0 + w_], s_t[:, 0:w_], AF.Copy), reads=[s_b], writes=[Bw])
                        else:
                            P.op(en, lambda e, s_t=s_t, c=c, c0=c0, w_=w_: e.tensor_copy(
                                wt[:, c, c0:c0 + w_], s_t[:, 0:w_]), reads=[s_b], writes=[Bw])
                P.flush()
            return wt, Bw

        def rmsnorm_rows(jk, xt_ap, np_, lnw_bc, xn_ap, ss, rs, reads, Bss, Bjk, Bxn, Blnw, flag=None):
            P.op("act", lambda e: e.activation(jk[0:np_, :], xt_ap, AF.Square, scale=1.0 / 32.0,
                                               accum_out=ss[0:np_, 0:1]), reads=reads, writes=[Bjk, Bss])
            P.op("act", lambda e: e.activation(rs[0:np_, 0:1], ss[0:np_, 0:1], AF.Sqrt, bias=RMS_EPS),
                 reads=[Bss], writes=[Bss])
            P.op("dve", lambda e: e.reciprocal(rs[0:np_, 0:1], rs[0:np_, 0:1]), reads=[Bss], writes=[Bss])
            if flag is not None:
                P.op("dve", lambda e: e.tensor_tensor(rs[0:np_, 0:1], rs[0:np_, 0:1], flag[0:np_, 0:1], ALU.mult),
                     reads=[Bss, Bflag], writes=[Bss])
            P.op("dve", lambda e: e.scalar_tensor_tensor(xn_ap, xt_ap, rs[0:np_, 0:1], lnw_bc[0:np_, :],
                                                         ALU.mult, ALU.mult),
                 reads=list(reads) + [Bss, Blnw], writes=[Bxn])

        P.flush()

        with contextlib.ExitStack() as st:
            stg = [mk(st, "eb_stg%d" % i, [128, 4, 768], F32) for i in range(2)]
            ebo = [mk(st, "eb_o%d" % i, [128, 4, 768], BF16) for i in range(2)]
            it = 0
            for t in range(NTAB):
                for hq in range(4):
                    s_t, s_b = stg[it % 2]
                    o_t, o_b = ebo[it % 2]
                    it += 1
                    P.dma(None, s_t[:], na_bias[t, :, hq * 4:(hq + 1) * 4, :], writes=[s_b])
                    P.op("act", lambda e, s_t=s_t, o_t=o_t: e.activation(o_t[:], s_t[:], AF.Exp),
                         reads=[s_b], writes=[o_b])
                    P.dma(None, eb_s[t, :, hq * 4:(hq + 1) * 4, :], o_t[:], reads=[o_b], writes=[B_scr["eb"]])
            P.flush()

        with contextlib.ExitStack() as st:
            Wa, BWa = load_weight_bf16(st, "Wa", na_w_in, 8, 4 * D)
            lnw, Blnw = mk(st, "lnwA", [128, D], F32)
            P.dma("sp", lnw[:], ln_w_bc[0:1, :].partition_broadcast(128), writes=[Blnw])
            xts = Rot([mk(st, "xtA%d" % i, [128, 4, D], F32) for i in range(2)])
            xn, Bxn = mk(st, "xnA", [128, 4, D], BF16)
            jk, Bjk = mk(st, "jkA", [128, D], BF16)
            sss = Rot([mk(st, "ssA%d" % i, [128, 2], F32) for i in range(4)])
            hnTs = Rot([mk(st, "hnTA%d" % i, [128, 8, 512], BF16) for i in range(2)])
            pTs = Rot([mk(st, "pTA%d" % i, [128, 512], BF16, psum=True) for i in range(2)])
            pqs = Rot([mk(st, "pqA%d" % i, [128, 512], F32, psum=True) for i in range(4)])
            qos = Rot([mk(st, "qoA%d" % i, [128, 512], BF16) for i in range(4)])
            for g in range(NT // 512):
                t0 = g * 512
                xt, Bxt = xts.next()
                P.dma(None, xt[:], x_in[t0:t0 + 512, :].rearrange("(j p) d -> p j d", p=128), reads=[B_x],
                      writes=[Bxt])
                for j in range(4):
                    ss, Bss = sss.next()
                    rmsnorm_rows(jk, xt[:, j, :], 128, lnw, xn[:, j, :], ss[:, 0:1], ss[:, 1:2], [Bxt], Bss,
                                 Bjk, Bxn, Blnw)
                hnT, BhnT = hnTs.next()
                for c in range(8):
                    pT, BpT = pTs.next()
                    for j in range(4):
                        P.op("pe", lambda e, pT=pT, j=j, c=c: e.transpose(
                            pT[:, j * 128:(j + 1) * 128], xn[:, j, c * 128:(c + 1) * 128], identb[:]),
                            reads=[Bxn, Bidb], writes=[BpT])
                    en = "dve" if c % 2 == 0 else "pool"
                    if en == "dve":
                        P.op("dve", lambda e, pT=pT, hnT=hnT, c=c: e.tensor_copy(hnT[:, c, :], pT[:]),
                             reads=[BpT], writes=[BhnT])
                    else:
                        P.op("act", lambda e, pT=pT, hnT=hnT, c=c: e.activation(hnT[:, c, :], pT[:], AF.Copy),
                             reads=[BpT], writes=[BhnT])
                for fo in range(16):
                    pq, Bpq = pqs.next()
                    for c in range(8):
                        P.op("pe", lambda e, pq=pq, hnT=hnT, c=c, fo=fo: e.matmul(
                            pq[:], Wa[:, c, fo * 128:(fo + 1) * 128], hnT[:, c, :], start=(c == 0), stop=(c == 7)),
                            reads=[BWa, BhnT], writes=[Bpq])
                    qo, Bqo = qos.next()
                    sc = 0.125 if fo < 8 else 1.0
                    P.op("act", lambda e, pq=pq, qo=qo, sc=sc: e.activation(qo[:], pq[:], AF.Copy, scale=sc),
                         reads=[Bpq], writes=[Bqo])
                    dst = qT_s if fo < 8 else kT_s
                    bd = B_scr["qT"] if fo < 8 else B_scr["kT"]
                    ch0 = t0 // 128
                    P.dma(None, dst[ch0:ch0 + 4, :, fo % 8, :].rearrange("j p t -> p j t"),
                          qo[:].rearrange("p (j t) -> p j t", j=4), reads=[Bqo], writes=[bd])
                for j in range(4):
                    for fg in range(4):
                        pq, Bpq = pqs.next()
                        for c in range(8):
                            P.op("pe", lambda e, pq=pq, hnT=hnT, c=c, fg=fg, j=j: e.matmul(
                                pq[:], hnT[:, c, j * 128:(j + 1) * 128],
                                Wa[:, c, 2048 + fg * 512:2048 + (fg + 1) * 512], start=(c == 0), stop=(c == 7)),
                                reads=[BWa, BhnT], writes=[Bpq])
                        qo, Bqo = qos.next()
                        if fg < 2:
                            P.op("dve", lambda e, pq=pq, qo=qo: e.tensor_copy(qo[:], pq[:]), reads=[Bpq], writes=[Bqo])
                            P.dma(None, v_s[t0 + j * 128:t0 + (j + 1) * 128, fg * 512:(fg + 1) * 512], qo[:],
                                  reads=[Bqo], writes=[B_scr["v"]])
                        else:
                            P.op("act", lambda e, pq=pq, qo=qo: e.activation(qo[:], pq[:], AF.Silu),
                                 reads=[Bpq], writes=[Bqo])
                            P.dma(None, sg_s[t0 + j * 128:t0 + (j + 1) * 128, (fg - 2) * 512:(fg - 1) * 512], qo[:],
                                  reads=[Bqo], writes=[B_scr["sg"]])
            P.flush()

        if "stopA" in debug:
            return nc

        with contextlib.ExitStack() as st:
            P.only_sp = "onlysp" in debug
            Wo, BWo = load_weight_bf16(st, "Wo", na_w_out, 8, D, stage_cols=1024)
            ebg, Bebg = mk(st, "ebg", [128, NH_A, 768], BF16)
            P.dma("sp", ebg[:], eb_s[0, :, :, :], reads=[B_scr["eb"]], writes=[Bebg])
            ebsp = Rot([mk(st, "ebsp%d" % i, [128, NH_A, 768], BF16) for i in range(2)])
            NR = 8
            kring = [mk(st, "kring%d" % i, [128, 8, 128], BF16) for i in range(NR)]
            vring = [mk(st, "vring%d" % i, [128, NH_A, 128], BF16) for i in range(NR)]
            for i in range(NR):
                P.op("pool", lambda e, i=i: e.memset(vring[i][0][:, :, 64:128], 1.0), writes=[vring[i][1]])
            qts = Rot([mk(st, "qtB%d" % i, [128, 8, 128], BF16) for i in range(2)])
            sgs = Rot([mk(st, "sgB%d" % i, [128, D], BF16) for i in range(2)])
            xbs = Rot([mk(st, "xB%d" % i, [128, D], F32) for i in range(3)])
            Es = Rot([mk(st, "EB%d" % i, [128, 768], BF16) for i in range(4)])
            Pms = Rot([mk(st, "PmB%d" % i, [128, 768], BF16) for i in range(4)])
            ogs = Rot([mk(st, "ogB%d" % i, [128, D], BF16) for i in range(2)])
            ogTs = Rot([mk(st, "ogTB%d" % i, [128, 8, 128], BF16) for i in range(2)])
            h1os = Rot([mk(st, "h1oB%d" % i, [128, D], F32) for i in range(2)])
            rcs = Rot([mk(st, "rcB%d" % i, [128, 1], F32) for i in range(4)])
            onrm = Rot([mk(st, "onB%d" % i, [128, 64], BF16) for i in range(4)])
            psts = Rot([mk(st, "pstB%d" % i, [128, 1024], F32, psum=True) for i in range(3)])
            pOs = Rot([mk(st, "pOB%d" % i, [128, 128], F32, psum=True) for i in range(1)])
            po_t, Bpo_t = mk(st, "poB", [128, 512], F32, psum=True)
            pT2, BpT2 = po_t.bitcast(BF16), Bpo_t
            pos = Rot([(po_t, Bpo_t)])

            bunits = []
            for (seg0, nsegs) in ((0, GRP), (GRP, 1)):
                nb = nsegs * NBS
                for i in range(nb):
                    bunits.append((seg0, nb, i))
            bctx = {}
            loaded = set()

            def B0(u):
                seg0, nb, i = bunits[u]
                c_ = {}
                bctx[u] = c_
                tokbase = seg0 * seg_t
                chbase = tokbase // 128
                seg = seg0 + i // NBS
                p = i % NBS
                lo, hi = na_chunk_range(i, nb, NBS)
                tid = na_table_id(seg, p, NBS)
                for m in range(lo, hi + 1):
                    if (chbase + m) in loaded:
                        continue
                    loaded.add(chbase + m)
                    kt, Bkt = kring[(chbase + m) % NR]
                    vt, Bvt = vring[(chbase + m) % NR]
                    P.dma(None, kt[:], kT_s[chbase + m, :, :, :], reads=[B_scr["kT"]], writes=[Bkt])
                    tk = tokbase + m * 128
                    P.dma(None, vt[:, :, 0:64], v_s[tk:tk + 128, :].rearrange("p (h d) -> p h d", h=NH_A),
                          reads=[B_scr["v"]], writes=[Bvt])
                tq = tokbase + i * 128
                qt, Bqt = qts.next()
                sg, Bsg = sgs.next()
                P.dma(None, qt[:], qT_s[chbase + i, :, :, :], reads=[B_scr["qT"]], writes=[Bqt])
                P.dma(None, sg[:], sg_s[tq:tq + 128, :], reads=[B_scr["sg"]], writes=[Bsg])
                if tid == 0:
                    ebt, Bebt = ebg, Bebg
                else:
                    ebt, Bebt = ebsp.next()
                    P.dma(None, ebt[:], eb_s[tid, :, :, :], reads=[B_scr["eb"]], writes=[Bebt])
                c_.update(qt=qt, Bqt=Bqt, sg=sg, Bsg=Bsg, ebt=ebt, Bebt=Bebt, lo=lo, ns=hi - lo + 1, chbase=chbase, tq=tq)

            def B1(u):
                c_ = bctx[u]
                qt, Bqt, sg, Bsg, ebt, Bebt = c_["qt"], c_["Bqt"], c_["sg"], c_["Bsg"], c_["ebt"], c_["Bebt"]
                lo, ns, chbase = c_["lo"], c_["ns"], c_["chbase"]
                og, Bog = ogs.next()
                c_["og"], c_["Bog"] = og, Bog

                def s_stage(h):
                    hp, po = h // 2, (h % 2) * 64
                    pst, Bpst = psts.next()
                    for s in range(ns):
                        kt, Bkt = kring[(chbase + lo + s) % NR]
                        P.op("pe", lambda e: e.matmul(
                            pst[:, s * 128:(s + 1) * 128], kt[po:po + 64, hp, :], qt[po:po + 64, hp, :],
                            start=True, stop=True), reads=[Bkt, Bqt], writes=[Bpst])
                    E, BE = Es.next()
                    P.op("act", lambda e: e.activation(E[:, 0:ns * 128], pst[:, 0:ns * 128], AF.Exp),
                         reads=[Bpst], writes=[BE])
                    Pm, BPm = Pms.next()
                    P.op("pool" if h % 2 == 0 else "dve", lambda e: e.tensor_tensor(
                        Pm[:, 0:ns * 128], E[:, 0:ns * 128], ebt[:, h, 0:ns * 128], ALU.mult),
                        reads=[BE, Bebt], writes=[BPm])
                    return Pm, BPm

                def pv_stage(h, Pm, BPm):
                    pO, BpO = pOs.next()
                    for s in range(ns):
                        vt, Bvt = vring[(chbase + lo + s) % NR]
                        P.op("pe", lambda e: e.matmul(
                            pO[:, 0:72], Pm[:, s * 128:(s + 1) * 128], vt[:, h, 0:72],
                            start=(s == 0), stop=(s == ns - 1)), reads=[BPm, Bvt], writes=[BpO])
                    rc, Brc = rcs.next()
                    P.op("dve", lambda e: e.reciprocal(rc[:], pO[:, 64:65]), reads=[BpO], writes=[Brc])
                    on_, Bon_ = onrm.next()
                    P.op("dve", lambda e: e.tensor_scalar(on_[:], pO[:, 0:64], rc[:, 0:1], None, ALU.mult),
                         reads=[BpO, Brc], writes=[Bon_])
                    P.op("dve", lambda e: e.tensor_tensor(
                        og[:, h * 64:(h + 1) * 64], on_[:], sg[:, h * 64:(h + 1) * 64], ALU.mult),
                        reads=[Bon_, Bsg], writes=[Bog])

                pend = []
                for h in range(NH_A):
                    pend.append((h, s_stage(h)))
                    if len(pend) > 2:
                        h_, (Pm_, BPm_) = pend.pop(0)
                        pv_stage(h_, Pm_, BPm_)
                for h_, (Pm_, BPm_) in pend:
                    pv_stage(h_, Pm_, BPm_)

            def B2(u):
                c_ = bctx[u]
                og, Bog = c_["og"], c_["Bog"]
                tq = c_["tq"]
                xb, Bxb = xbs.next()
                c_["xb"], c_["Bxb"] = xb, Bxb
                P.dma(None, xb[:], x_in[tq:tq + 128, :], reads=[B_x], writes=[Bxb])
                for c in range(8):
                    P.op("pe", lambda e: e.transpose(
                        pT2[:, c * 128:(c + 1) * 128], og[:, c * 128:(c + 1) * 128], identb[:]),
                        reads=[Bog, Bidb], writes=[BpT2])
                ogT, BogT = ogTs.next()
                c_["ogT"], c_["BogT"] = ogT, BogT
                P.op("act", lambda e: e.activation(ogT[:].rearrange("p c t -> p (c t)"), pT2[:], AF.Copy),
                     reads=[BpT2], writes=[BogT])

            def B3(u):
                c_ = bctx.pop(u)
                ogT, BogT, xb, Bxb, tq = c_["ogT"], c_["BogT"], c_["xb"], c_["Bxb"], c_["tq"]
                h1o, Bh1o = h1os.next()
                for half in range(2):
                    po_, Bpo = pos.next()
                    for c in range(8):
                        P.op("pe", lambda e: e.matmul(
                            po_[:], ogT[:, c, :], Wo[:, c, half * 512:(half + 1) * 512],
                            start=(c == 0), stop=(c == 7)), reads=[BogT, BWo], writes=[Bpo])
                    P.op("dve", lambda e: e.tensor_tensor(
                        h1o[:, half * 512:(half + 1) * 512], po_[:], xb[:, half * 512:(half + 1) * 512], ALU.add),
                        reads=[Bpo, Bxb], writes=[Bh1o])
                P.dma(None, h1_s[tq:tq + 128, :], h1o[:], reads=[Bh1o], writes=[B_scr["h1"]])

            nBU = len(bunits)
            for t in range(nBU + 3):
                for fn_, off_ in ((B3, 3), (B2, 2), (B1, 1), (B0, 0)):
                    if 0 <= t - off_ < nBU:
                        fn_(t - off_)
            P.flush()

        if "stopB" in debug:
            return nc

        with contextlib.ExitStack() as st:
            Wg, BWg = load_weight_bf16(st, "Wg", g_w_in, 8, 6176, stage_cols=1544)
            lnw, Blnw = mk(st, "lnwC", [128, D], F32)
            P.dma("sp", lnw[:], ln_w_bc[1:2, :].partition_broadcast(128), writes=[Blnw])
            cw, Bcw = mk(st, "cwC", [128, 32, 5], F32)
            P.dma("sp", cw[:], g_cw[:, :, :], writes=[Bcw])
            dtb, Bdtb = mk(st, "dtbC", [128, 16], F32)
            nA, BnA = mk(st, "nAC", [128, 16], F32)
            P.dma("sp", dtb[:], g_dtb[0:1, :].partition_broadcast(128), writes=[Bdtb])
            P.dma("sp", nA[:], g_alog[0:1, :].partition_broadcast(128), writes=[BnA])
            P.op("act", lambda e: e.activation(nA[:], nA[:], AF.Exp), reads=[BnA], writes=[BnA])
            P.op("dve", lambda e: e.tensor_scalar(nA[:], nA[:], -1.0, None, ALU.mult), reads=[BnA], writes=[BnA])

            W_ = seg_t + 4
            hnT, BhnT = mk(st, "hnTC", [128, 8, W_], BF16)
            xts = Rot([mk(st, "xtC%d" % i, [128, D], F32) for i in range(2)])
            xns = Rot([mk(st, "xnC%d" % i, [128, D], BF16) for i in range(2)])
            sss = Rot([mk(st, "ssC%d" % i, [128, 2], F32) for i in range(4)])
            pTs = Rot([mk(st, "pTC%d" % i, [128, 1024], BF16, psum=True) for i in range(2)])
            pps = Rot([mk(st, "ppC%d" % i, [128, 512], F32, psum=True) for i in range(4)])
            psm, Bpsm = mk(st, "psmC", [128, 512], F32, psum=True)
            HT = seg_t // 2
            NW = min(512, HT)
            NN = HT // NW
            NJ = HT // 128
            projs = Rot([mk(st, "projC%d" % i, [128, HT + 4], F32) for i in range(3)])
            accs = Rot([mk(st, "accC%d" % i, [128, HT], F32) for i in range(3)])
            ys = Rot([mk(st, "yC%d" % i, [128, HT], F32) for i in range(3)])
            ynb = Rot([mk(st, "ynC%d" % i, [128, HT], BF16) for i in range(5)])
            tks = Rot([mk(st, "tkC%d" % i, [128, 8, 128], BF16) for i in range(2)])
            zos = Rot([mk(st, "zoC%d" % i, [128, 512], BF16) for i in range(2)])
            bgt = Rot([mk(st, "bgC%d" % i, [128, 32], F32) for i in range(2)])
            gts = Rot([mk(st, "gtC%d" % i, [128, 16], F32) for i in range(2)])
            bgall = mk(st, "bgallC", [128, NCS, 32], F32)
            gtall = mk(st, "gtallC", [128, 2, NCS, 8], F32)
            grall = mk(st, "grallC", [NCS * 8, 2, 128], F32)

            for seg in range(NSEG):
                T0 = seg * seg_t
                has_prev = (seg not in (0, GRP))
                has_next = (seg not in (GRP - 1, GRP))
                s1ctx = {}

                def S1a(j):
                    xt, Bxt = xts.next()
                    xn, Bxn = xns.next()
                    ss, Bss = sss.next()
                    s1ctx[j] = (xn, Bxn)
                    if j < NCS:
                        P.dma(None, xt[:], h1_s[T0 + j * 128:T0 + (j + 1) * 128, :], reads=[B_scr["h1"]], writes=[Bxt])
                        rmsnorm_rows(xn, xt[:, :], 128, lnw, xn[:, :], ss[:, 0:1], ss[:, 1:2], [Bxt], Bss, Bxn, Bxn, Blnw)
                    else:
                        P.op("pool", lambda e: e.memset(xt[0:32, :], 0.0), writes=[Bxt])
                        if has_prev:
                            P.dma(None, xt[0:2, :], h1_s[T0 - 2:T0, :], reads=[B_scr["h1"]], writes=[Bxt])
                        if has_next:
                            P.dma(None, xt[2:4, :], h1_s[T0 + seg_t:T0 + seg_t + 2, :], reads=[B_scr["h1"]], writes=[Bxt])
                        rmsnorm_rows(xn, xt[0:4, :], 4, lnw, xn[0:4, :], ss[:, 0:1], ss[:, 1:2], [Bxt], Bss, Bxn, Bxn,
                                     Blnw, flag=flagt)

                def S1b(j):
                    xn, Bxn = s1ctx.pop(j)
                    np_ = 128 if j < NCS else 4
                    pT, BpT = pTs.next()
                    for c in range(8):
                        P.op("pe", lambda e: e.transpose(
                            pT[:, c * 128:c * 128 + np_], xn[0:np_, c * 128:(c + 1) * 128], identb[0:np_, 0:np_]),
                            reads=[Bxn, Bidb], writes=[BpT])
                    if j < NCS:
                        P.op("act", lambda e: e.activation(
                            hnT[:, :, 2 + j * 128:2 + (j + 1) * 128], pT[:].rearrange("p (c t) -> p c t", c=8), AF.Copy),
                            reads=[BpT], writes=[BhnT])
                    else:
                        pv_ = pT[:].rearrange("p (c t) -> p c t", c=8)
                        P.op("dve", lambda e: e.tensor_copy(hnT[:, :, 0:2], pv_[:, :, 0:2]), reads=[BpT], writes=[BhnT])
                        P.op("dve", lambda e: e.tensor_copy(hnT[:, :, seg_t + 2:seg_t + 4], pv_[:, :, 2:4]),
                             reads=[BpT], writes=[BhnT])

                for t in range(NCS + 2):
                    if 0 <= t - 1 <= NCS:
                        S1b(t - 1)
                    if t <= NCS:
                        S1a(t)
                cunits = [(hf, cc) for hf in range(2) for cc in range(32)]
                cctx = {}

                def PJ(u):
                    hf, cc = cunits[u]
                    c_ = {}
                    cctx[u] = c_
                    col0 = hf * HT
                    proj, Bproj = projs.next()
                    c_["proj"], c_["Bproj"] = proj, Bproj
                    for (c0, d0) in ((col0, 0), (col0 + HT + 2, 2)):
                        for c in range(8):
                            P.op("pe", lambda e: e.matmul(
                                psm[:, d0:d0 + 2], Wg[:, c, cc * 128:(cc + 1) * 128], hnT[:, c, c0:c0 + 2],
                                start=(c == 0), stop=(c == 7)), reads=[BWg, BhnT], writes=[Bpsm])
                    P.op("act", lambda e: e.activation(proj[:, 0:2], psm[:, 0:2], AF.Copy), reads=[Bpsm], writes=[Bproj])
                    P.op("act", lambda e: e.activation(proj[:, HT + 2:HT + 4], psm[:, 2:4], AF.Copy),
                         reads=[Bpsm], writes=[Bproj])
                    for n in range(NN):
                        pp, Bpp = pps.next()
                        for c in range(8):
                            P.op("pe", lambda e: e.matmul(
                                pp[:, 0:NW], Wg[:, c, cc * 128:(cc + 1) * 128],
                                hnT[:, c, col0 + 2 + n * NW:col0 + 2 + (n + 1) * NW],
                                start=(c == 0), stop=(c == 7)), reads=[BWg, BhnT], writes=[Bpp])
                        P.op("act", lambda e: e.activation(proj[:, 2 + n * NW:2 + (n + 1) * NW], pp[:, 0:NW], AF.Copy),
                             reads=[Bpp], writes=[Bproj])

                def CV(u):
                    hf, cc = cunits[u]
                    c_ = cctx[u]
                    proj, Bproj = c_["proj"], c_["Bproj"]
                    acc, Bacc = accs.next()
                    c_["acc"], c_["Bacc"] = acc, Bacc
                    P.op("dve", lambda e: e.tensor_scalar(acc[:], proj[:, 0:HT], cw[:, cc, 0:1], None, ALU.mult),
                         reads=[Bproj, Bcw], writes=[Bacc])
                    for k in range(1, 5):
                        P.op("dve", lambda e: e.scalar_tensor_tensor(
                            acc[:], proj[:, k:k + HT], cw[:, cc, k:k + 1], acc[:], ALU.mult, ALU.add),
                            reads=[Bproj, Bcw, Bacc], writes=[Bacc])
                    if cc < 16:
                        y, By = ys.next()
                        c_["y"], c_["By"] = y, By
                        P.op("act", lambda e: e.activation(y[:], acc[:], AF.Silu), reads=[Bacc], writes=[By])
                    else:
                        yn, Byn = ynb.next()
                        c_["yn"], c_["Byn"] = yn, Byn
                        P.op("act", lambda e: e.activation(yn[:], acc[:], AF.Silu), reads=[Bacc], writes=[Byn])

                def NM1(u):
                    hf, cc = cunits[u]
                    if cc >= 16:
                        return
                    c_ = cctx[u]
                    acc, Bacc, y, By = c_["acc"], c_["Bacc"], c_["y"], c_["By"]
                    P.op("pool", lambda e: e.tensor_tensor(acc[:], y[:], y[:], ALU.mult), reads=[By], writes=[Bacc])
                    for n in range(NN):
                        pp, Bpp = pps.next()
                        P.op("pe", lambda e: e.matmul(pp[:, 0:NW], onesf, acc[:, n * NW:(n + 1) * NW], start=True, stop=True),
                             reads=[Bc, Bacc], writes=[Bpp])
                        P.op("act", lambda e: e.activation(acc[:, n * NW:(n + 1) * NW], pp[:, 0:NW], AF.Sqrt, bias=L2_EPS),
                             reads=[Bpp], writes=[Bacc])

                def NM2(u):
                    hf, cc = cunits[u]
                    if cc >= 16:
                        return
                    c_ = cctx[u]
                    acc, Bacc, y, By = c_["acc"], c_["Bacc"], c_["y"], c_["By"]
                    P.op("dve", lambda e: e.reciprocal(acc[:], acc[:]), reads=[Bacc], writes=[Bacc])
                    yn, Byn = ynb.next()
                    c_["yn"], c_["Byn"] = yn, Byn
                    sc = float(DK ** -0.5) if cc < 8 else 1.0
                    P.op("dve", lambda e: e.scalar_tensor_tensor(yn[:], y[:], sc, acc[:], ALU.mult, ALU.mult),
                         reads=[By, Bacc], writes=[Byn])
                    dst, bd = (gq_s, B_scr["gq"]) if cc < 8 else (gk_s, B_scr["gk"])
                    ch0 = (T0 + hf * HT) // 128
                    P.dma(None, dst[ch0:ch0 + NJ, :, cc % 8, :].rearrange("j p t -> p j t"),
                          yn[:].rearrange("p (j t) -> p j t", j=NJ), reads=[Byn], writes=[bd])

                def TR(u):
                    hf, cc = cunits[u]
                    c_ = cctx.pop(u)
                    if cc < 8:
                        return
                    yn, Byn = c_["yn"], c_["Byn"]
                    pT, BpT = pTs.next()
                    for jj in range(NJ):
                        P.op("pe", lambda e: e.transpose(pT[:, jj * 128:(jj + 1) * 128], yn[:, jj * 128:(jj + 1) * 128], identb[:]),
                             reads=[Byn, Bidb], writes=[BpT])
                    tk, Btk = tks.next()
                    P.op("act", lambda e: e.activation(
                        tk[:, 0:NJ, :], pT[:, 0:NJ * 128].rearrange("p (j t) -> p j t", j=NJ), AF.Copy),
                        reads=[BpT], writes=[Btk])
                    tA = T0 + hf * HT
                    if cc < 16:
                        dd = gkt_s[tA:tA + HT, cc - 8, :].rearrange("(j p) t -> p j t", p=128)
                        bd = B_scr["gkt"]
                    else:
                        hh_, half_ = (cc - 16) // 2, (cc - 16) % 2
                        dd = gv_s[tA:tA + HT, hh_, half_ * 128:(half_ + 1) * 128].rearrange("(j p) t -> p j t", p=128)
                        bd = B_scr["gv"]
                    P.dma(None, dd, tk[:, 0:NJ, :], reads=[Btk], writes=[bd])

                nCU = len(cunits)
                stages_c = ((TR, 4), (NM2, 3), (NM1, 2), (CV, 1), (PJ, 0))
                for t in range(nCU + 4):
                    for fn_, off_ in stages_c:
                        if 0 <= t - off_ < nCU:
                            fn_(t - off_)
                psm_v = psm[:, 0:NCS * 32].rearrange("p (j c) -> p j c", j=NCS)
                for j in range(NCS):
                    tA = T0 + j * 128
                    for n in range(4):
                        pp, Bpp = pps.next()
                        for c in range(8):
                            P.op("pe", lambda e: e.matmul(
                                pp[:], hnT[:, c, 2 + j * 128:2 + (j + 1) * 128],
                                Wg[:, c, 4096 + n * 512:4096 + (n + 1) * 512], start=(c == 0), stop=(c == 7)),
                                reads=[BWg, BhnT], writes=[Bpp])
                        zo, Bzo = zos.next()
                        P.op("act", lambda e: e.activation(zo[:], pp[:], AF.Silu), reads=[Bpp], writes=[Bzo])
                        P.dma(None, z_s[tA:tA + 128, n * 512:(n + 1) * 512], zo[:], reads=[Bzo], writes=[B_scr["z"]])
                    for c in range(8):
                        P.op("pe", lambda e: e.matmul(
                            psm[:, j * 32:(j + 1) * 32], hnT[:, c, 2 + j * 128:2 + (j + 1) * 128], Wg[:, c, 6144:6176],
                            start=(c == 0), stop=(c == 7)), reads=[BWg, BhnT], writes=[Bpsm])
                bga, Bbga = bgall
                gta, Bgta = gtall
                P.op("act", lambda e: e.activation(bga[:, :, 0:16], psm_v[:, :, 0:16], AF.Sigmoid), reads=[Bpsm], writes=[Bbga])
                P.op("dve", lambda e: e.tensor_tensor(
                    gta[:].rearrange("p d j h -> p j d h"), psm_v[:, :, 16:32].rearrange("p j (d h) -> p j d h", d=2),
                    dtb[:].rearrange("p (d h) -> p d h", d=2).unsqueeze(1).to_broadcast([128, NCS, 2, 8]), ALU.add),
                     reads=[Bpsm, Bdtb], writes=[Bgta])
                P.op("act", lambda e: e.activation(gta[:], gta[:], AF.Exp), reads=[Bgta], writes=[Bgta])
                P.op("act", lambda e: e.activation(gta[:], gta[:], AF.Ln, bias=1.0), reads=[Bgta], writes=[Bgta])
                P.op("dve", lambda e: e.tensor_tensor(
                    gta[:], gta[:], nA[:].rearrange("p (d h) -> p d h", d=2).unsqueeze(2).to_broadcast([128, 2, NCS, 8]), ALU.mult),
                     reads=[Bgta, BnA], writes=[Bgta])
                pp, Bpp = pps.next()
                P.op("pe", lambda e: e.matmul(pp[:, 0:NCS * 8], trif, gta[:, 0, :, :].rearrange("p j h -> p (j h)"), start=True, stop=True),
                     reads=[Bc, Bgta], writes=[Bpp])
                P.op("pe", lambda e: e.matmul(pp[:, NCS * 8:2 * NCS * 8], trib, gta[:, 1, :, :].rearrange("p j h -> p (j h)"), start=True, stop=True),
                     reads=[Bc, Bgta], writes=[Bpp])
                P.op("dve", lambda e: e.tensor_copy(
                    bga[:, :, 16:32].rearrange("p j (d h) -> p d j h", d=2),
                    pp[:, 0:2 * NCS * 8].rearrange("p (d j h) -> p d j h", d=2, j=NCS)), reads=[Bpp], writes=[Bbga])
                P.dma(None, bg_s[T0:T0 + seg_t, :].rearrange("(j p) c -> p j c", p=128), bga[:], reads=[Bbga],
                      writes=[B_scr["bg"]])
                pr_, Bpr_ = pps.next()
                P.op("pe", lambda e: e.matmul(pr_[0:NCS * 8, 0:128], gta[:, 0, :, :].rearrange("p j h -> p (j h)"), trif, start=True, stop=True),
                     reads=[Bc, Bgta], writes=[Bpr_])
                P.op("pe", lambda e: e.matmul(pr_[0:NCS * 8, 128:256], gta[:, 1, :, :].rearrange("p j h -> p (j h)"), trib, start=True, stop=True),
                     reads=[Bc, Bgta], writes=[Bpr_])
                gra, Bgra = grall
                P.op("dve", lambda e: e.tensor_copy(
                    gra[:], pr_[0:NCS * 8, 0:256].rearrange("p (d t) -> p d t", d=2)), reads=[Bpr_], writes=[Bgra])
                for j in range(NCS):
                    tA = T0 + j * 128
                    P.dma(None, gcr_s[:, :, tA:tA + 128].rearrange("d h t -> h d t"), gra[j * 8:(j + 1) * 8, :, :],
                          reads=[Bgra], writes=[B_scr["gcr"]])
            P.flush()

        if "stopC" in debug:
            return nc

        with contextlib.ExitStack() as st:
            HB = 4
            S32 = [mk(st, "S32_%d" % d, [128, 8, 256], F32) for d in range(2)]
            Sbf = [mk(st, "Sbf_%d" % d, [128, 8, 256], BF16) for d in range(2)]
            S32B = [[Buf() for _ in range(8)] for d in range(2)]
            SbfB = [[Buf() for _ in range(8)] for d in range(2)]
            kTs = Rot([mk(st, "kTD%d" % i, [128, 8, 128], BF16) for i in range(2)])
            qTs = Rot([mk(st, "qTD%d" % i, [128, 8, 128], BF16) for i in range(2)])
            kts = Rot([mk(st, "ktD%d" % i, [128, 8, 128], BF16) for i in range(2)])
            vts = Rot([mk(st, "vtD%d" % i, [128, 8, 256], BF16) for i in range(2)])
            bgs = Rot([mk(st, "bgD%d" % i, [128, 32], F32) for i in range(3)])
            GRs = Rot([mk(st, "GRD%d" % i, [128, 8, 128], F32) for i in range(2)])
            smalls = Rot([mk(st, "smD%d" % i, [128, 4, 8], F32) for i in range(2)])
            tmps = Rot([mk(st, "tmpD%d" % i, [128, 8, 128], F32) for i in range(2)])
            DTs = Rot([mk(st, "DTD%d" % i, [128, 8, 128], F32) for i in range(1)])
            DSs = Rot([mk(st, "DSD%d" % i, [128, 8, 128], F32) for i in range(1)])
            ERs = Rot([mk(st, "ERD%d" % i, [128, 8, 128], F32) for i in range(2)])
            Aqs = Rot([mk(st, "AqD%d" % i, [128, 8, 128], BF16) for i in range(2)])
            qgs = Rot([mk(st, "qgD%d" % i, [128, 8, 128], BF16) for i in range(2)])
            rws = Rot([mk(st, "rwD%d" % i, [128, 8, 128], F32) for i in range(2)])
            kds = Rot([mk(st, "kdD%d" % i, [128, 8, 128], BF16) for i in range(2)])
            vbs = Rot([mk(st, "vbD%d" % i, [128, 8, 256], F32) for i in range(2)])
            Ab = Rot([mk(st, "AbD%d" % i, [128, HB, 128], F32) for i in range(4)])
            Nb = Rot([mk(st, "NbD%d" % i, [128, HB, 128], F32) for i in range(4)])
            ZP0 = Rot([mk(st, "ZP0D%d" % i, [128, HB, 2, 128], F32) for i in range(4)])
            ZPm = Rot([mk(st, "ZPmD%d" % i, [128, HB, 2, 128], F32) for i in range(4)])
            Yb = Rot([mk(st, "YbD%d" % i, [128, HB, 128], F32) for i in range(4)])
            P32 = Rot([mk(st, "P32D%d" % i, [128, HB, 128], F32) for i in range(4)])
            nWs = Rot([mk(st, "nWD%d" % i, [128, HB, 128], BF16) for i in range(2)])
            vns = Rot([mk(st, "vnD%d" % i, [128, 256], BF16) for i in range(4)])
            oos = Rot([mk(st, "ooD%d" % i, [128, 8, 256], F32) for i in range(2)])
            pM = [mk(st, "pMD%d" % i, [128, HB * 128], F32, psum=True) for i in range(2)]
            pZs = [mk(st, "pZD%d" % i, [128, 512], F32, psum=True) for i in range(2)]
            pYs = [mk(st, "pYD%d" % i, [128, 512], F32, psum=True) for i in range(2)]
            pPs = [mk(st, "pPD%d" % i, [128, HB * 128], F32, psum=True) for i in range(2)]

            units = []
            for (seg0, nsegs) in ((0, GRP), (GRP, 1)):
                nchg = nsegs * NCS
                chb = seg0 * NCS
                for step in range(nchg):
                    for d in range(2):
                        cl = step if d == 0 else nchg - 1 - step
                        first = (step == 0)
                        carry = (not first) and ((cl % NCS == 0) if d == 0 else (cl % NCS == NCS - 1))
                        units.append((chb + cl, d, first, carry))
            ctxs = {}
            h4 = lambda ap: ap.rearrange("p (h t) -> p h t", h=HB)

            def L_stage(u):
                ch, d, first, carry = units[u]
                c = {}
                ctxs[u] = c
                tA = ch * 128
                c["kT"], c["BkT"] = kTs.next()
                c["qT"], c["BqT"] = qTs.next()
                c["kt"], c["Bkt"] = kts.next()
                c["vt"], c["Bvt"] = vts.next()
                c["bg"], c["Bbg"] = bgs.next()
                c["GR"], c["BGR"] = GRs.next()
                P.dma(None, c["kT"][:], gk_s[ch, :, :, :], reads=[B_scr["gk"]], writes=[c["BkT"]])
                P.dma(None, c["qT"][:], gq_s[ch, :, :, :], reads=[B_scr["gq"]], writes=[c["BqT"]])
                P.dma(None, c["kt"][:], gkt_s[tA:tA + 128, :, :], reads=[B_scr["gkt"]], writes=[c["Bkt"]])
                P.dma(None, c["vt"][:], gv_s[tA:tA + 128, :, :], reads=[B_scr["gv"]], writes=[c["Bvt"]])
                P.dma(None, c["bg"][:], bg_s[tA:tA + 128, :], reads=[B_scr["bg"]], writes=[c["Bbg"]])
                for h in range(8):
                    P.dma(None, c["GR"][:, h, :], gcr_s[d, h:h + 1, tA:tA + 128].partition_broadcast(128),
                          reads=[B_scr["gcr"]], writes=[c["BGR"]])

            def Q_stage(u):
                ch, d, first, carry = units[u]
                c = ctxs[u]
                last = 127 if d == 0 else 0
                kT, BkT, qT, BqT, kt, Bkt, vt, Bvt = c["kT"], c["BkT"], c["qT"], c["BqT"], c["kt"], c["Bkt"], c["vt"], c["Bvt"]
                bg, Bbg, GR, BGR = c["bg"], c["Bbg"], c["GR"], c["BGR"]
                beta = bg[:, d * 8:(d + 1) * 8]
                gc = bg[:, 16 + d * 8:16 + (d + 1) * 8]
                sm, Bsm = smalls.next()
                P.op("act", lambda e: e.activation(sm[:, 0, :], gc, AF.Exp), reads=[Bbg], writes=[Bsm])
                P.op("dve", lambda e: e.tensor_tensor(sm[:, 0, :], sm[:, 0, :], beta, ALU.mult), reads=[Bsm, Bbg], writes=[Bsm])
                P.op("dve", lambda e: e.tensor_tensor(sm[:, 1, :], GR[:, :, last], gc, ALU.subtract),
                     reads=[BGR, Bbg], writes=[Bsm])
                P.op("act", lambda e: e.activation(sm[:, 1, :], sm[:, 1, :], AF.Exp), reads=[Bsm], writes=[Bsm])
                yield
                tmp, Btmp = tmps.next()
                DT, BDT = DTs.next()
                DS, BDS = DSs.next()
                ER, BER = ERs.next()
                c["ER"], c["BER"] = ER, BER
                for h in range(8):
                    P.op("dve", lambda e, h=h: e.scalar_tensor_tensor(
                        tmp[:, h, :], GR[:, h, :], gc[:, h:h + 1], maskD[d], ALU.subtract, ALU.add),
                        reads=[BGR, Bbg, Bc], writes=[Btmp])
                    if h % 2 == 1:
                        yield
                    if h % 4 == 3:
                        P.op("act", lambda e: e.activation(DT[:, h - 3:h + 1, :], tmp[:, h - 3:h + 1, :], AF.Exp),
                             reads=[Btmp], writes=[BDT])
                tmp2, Btmp2 = tmps.next()
                for h in range(8):
                    P.op("dve", lambda e, h=h: e.scalar_tensor_tensor(
                        tmp2[:, h, :], GR[:, h, :], gc[:, h:h + 1], maskS[d], ALU.subtract, ALU.subtract),
                        reads=[BGR, Bbg, Bc], writes=[Btmp2])
                    if h % 2 == 1:
                        yield
                    if h % 4 == 3:
                        P.op("act", lambda e: e.activation(DS[:, h - 3:h + 1, :], tmp2[:, h - 3:h + 1, :], AF.Exp, scale=-1.0),
                             reads=[Btmp2], writes=[BDS])
                for hq in range(2):
                    P.op("act", lambda e: e.activation(ER[:, hq * 4:hq * 4 + 4, :], GR[:, hq * 4:hq * 4 + 4, :], AF.Exp),
                         reads=[BGR], writes=[BER])
                    yield
                qg, Bqg = qgs.next()
                rw, Brw = rws.next()
                kd, Bkd = kds.next()
                vb, Bvb = vbs.next()
                c.update(qg=qg, Bqg=Bqg, rw=rw, Brw=Brw, kd=kd, Bkd=Bkd, vb=vb, Bvb=Bvb)
                P.op("pool", lambda e: e.tensor_tensor(qg[:], qT[:], ER[:], ALU.mult), reads=[BqT, BER], writes=[Bqg])
                P.op("pool", lambda e: e.tensor_tensor(rw[:], kt[:], sm[:, 0, :].unsqueeze(2).to_broadcast([128, 8, 128]),
                                                       ALU.mult), reads=[Bkt, Bsm], writes=[Brw])
                P.op("pool", lambda e: e.tensor_tensor(kd[:], kt[:], sm[:, 1, :].unsqueeze(2).to_broadcast([128, 8, 128]),
                                                       ALU.mult), reads=[Bkt, Bsm], writes=[Bkd])
                P.op("pool", lambda e: e.tensor_tensor(vb[:].bitcast(F32R), vt[:], beta.unsqueeze(2).to_broadcast([128, 8, 256]),
                                                       ALU.mult), reads=[Bvt, Bbg], writes=[Bvb])
                yield
                Aq, BAq = Aqs.next()
                c["Aq"], c["BAq"] = Aq, BAq
                c["hb"] = []
                for hb in range(2):
                    h0 = hb * HB
                    pk, Bpk = pM[0]
                    for hh in range(HB):
                        P.op("pe", lambda e, hh=hh: e.matmul(
                            pk[:, hh * 128:(hh + 1) * 128], kT[:, h0 + hh, :], kT[:, h0 + hh, :], start=True, stop=True),
                            reads=[BkT], writes=[Bpk])
                    A_, BA = Ab.next()
                    yield
                    for hh in range(HB):
                        P.op("dve", lambda e, hh=hh: e.scalar_tensor_tensor(
                            A_[:, hh, :], pk[:, hh * 128:(hh + 1) * 128], beta[:, h0 + hh:h0 + hh + 1], DS[:, h0 + hh, :],
                            ALU.mult, ALU.mult), reads=[Bpk, Bbg, BDS], writes=[BA])
                        if hh % 2 == 1:
                            yield
                    pq, Bpq = pM[1]
                    for hh in range(HB):
                        P.op("pe", lambda e, hh=hh: e.matmul(
                            pq[:, hh * 128:(hh + 1) * 128], kT[:, h0 + hh, :], qT[:, h0 + hh, :], start=True, stop=True),
                            reads=[BkT, BqT], writes=[Bpq])
                    yield
                    P.op("dve", lambda e: e.tensor_tensor(
                        Aq[:, h0:h0 + HB, :], h4(pq[:]), DT[:, h0:h0 + HB, :], ALU.mult), reads=[Bpq, BDT], writes=[BAq])
                    yield
                    for hh in range(HB):
                        P.op("pe", lambda e, hh=hh: e.transpose(pk[:, hh * 128:(hh + 1) * 128], A_[:, hh, :], identf),
                             reads=[BA, Bc], writes=[Bpk])
                    yield
                    N_, BN = Nb.next()
                    P.op("act", lambda e: e.activation(N_[:], h4(pk[:]), AF.Copy), reads=[Bpk], writes=[BN])
                    zp0, Bzp0 = ZP0.next()
                    P.op("dve", lambda e: e.tensor_tensor(
                        zp0[:, :, 1, :].bitcast(F32R), identf.unsqueeze(1).to_broadcast([128, HB, 128]), h4(pk[:]), ALU.subtract),
                        reads=[Bc, Bpk], writes=[Bzp0])
                    c["hb"].append(dict(A=A_, BA=BA, N=N_, BN=BN, zp=zp0, Bzp=Bzp0))
                    yield

            def M_stage(u):
                ch, d, first, carry = units[u]
                c = ctxs[u]
                tA = ch * 128
                last = 127 if d == 0 else 0
                S32t = S32[d][0]
                Sbft = Sbf[d][0]
                BS32h = S32B[d]
                BSbfh = SbfB[d]
                ER, BER = c["ER"], c["BER"]
                qg, Bqg, rw, Brw, kd, Bkd, vb, Bvb, Aq, BAq = (c["qg"], c["Bqg"], c["rw"], c["Brw"], c["kd"], c["Bkd"],
                                                              c["vb"], c["Bvb"], c["Aq"], c["BAq"])
                stt = []
                for hb in range(2):
                    x = c["hb"][hb]
                    stt.append(dict(N=x["N"], BN=x["BN"], Yc=x["A"], BYc=x["BA"], zp=x["zp"], Bzp=x["Bzp"]))
                r32 = lambda ap: ap.bitcast(F32R)
                for lvl in range(0, 7):
                    zpns = [None, None]
                    for hb in range(2):
                        s_ = stt[hb]
                        pRa, BpRa = pZs[hb]
                        pRb, BpRb = pYs[hb]
                        Yc, BYc, zp, Bzp = s_["Yc"], s_["BYc"], s_["zp"], s_["Bzp"]
                        for hh in range(HB):
                            pr, Bpr = (pRa, BpRa) if hh < 2 else (pRb, BpRb)
                            o0 = (hh % 2) * 256
                            if lvl == 0:
                                N_, BN = s_["N"], s_["BN"]
                                P.op("pe", lambda e: e.matmul(pr[:, o0:o0 + 128], Yc[:, hh, :], N_[:, hh, :],
                                                              start=True, stop=True), reads=[BYc, BN], writes=[Bpr])
                            elif lvl < 6:
                                P.op("pe", lambda e: e.matmul(
                                    pr[:, o0:o0 + 256], r32(Yc[:, hh, :]),
                                    r32(zp[:, hh, :, :].rearrange("p z t -> p (z t)")), start=True, stop=True),
                                    reads=[BYc, Bzp], writes=[Bpr])
                            else:
                                P.op("pe", lambda e: e.matmul(pr[:, o0:o0 + 128], r32(Yc[:, hh, :]), r32(zp[:, hh, 1, :]),
                                                              start=True, stop=True), reads=[BYc, Bzp], writes=[Bpr])
                    yield
                    for hb in range(2):
                        s_ = stt[hb]
                        pRa, BpRa = pZs[hb]
                        pRb, BpRb = pYs[hb]
                        zp, Bzp = s_["zp"], s_["Bzp"]
                        if lvl == 0:
                            zpn, Bzpn = zp, Bzp
                        elif lvl < 6:
                            zpn, Bzpn = ZPm.next()
                        else:
                            zpn, Bzpn = P32.next()
                        zpns[hb] = (zpn, Bzpn)
                        for half, (pr, Bpr) in enumerate(((pRa, BpRa), (pRb, BpRb))):
                            prv = pr[:].rearrange("p (h z t) -> p h z t", h=2, z=2)
                            hs = slice(2 * half, 2 * half + 2)
                            if lvl < 6:
                                P.op("act", lambda e: e.activation(zpn[:, hs, 0, :].bitcast(F32R), prv[:, :, 0, :], AF.Copy),
                                     reads=[Bpr], writes=[Bzpn])
                            if 1 <= lvl < 6:
                                P.op("dve", lambda e: e.tensor_tensor(
                                    zpn[:, hs, 1, :].bitcast(F32R), prv[:, :, 1, :], zp[:, hs, 1, :], ALU.add),
                                    reads=[Bpr, Bzp], writes=[Bzpn])
                            if lvl == 6:
                                P.op("dve", lambda e: e.tensor_tensor(
                                    zpn[:, hs, :].bitcast(F32R), prv[:, :, 0, :], zp[:, hs, 1, :], ALU.add),
                                    reads=[Bpr, Bzp], writes=[Bzpn])
                        if lvl == 6:
                            s_["Pf"], s_["BPf"] = zpn, Bzpn
                    yield
                    if lvl == 6:
                        continue
                    for hb in range(2):
                        pTr, BpTr = pPs[hb]
                        zpn, Bzpn = zpns[hb]
                        for hh in range(HB):
                            P.op("pe", lambda e: e.transpose(pTr[:, hh * 128:(hh + 1) * 128], zpn[:, hh, 0, :], identf),
                                 reads=[Bzpn, Bc], writes=[BpTr])
                    yield
                    for hb in range(2):
                        s_ = stt[hb]
                        pTr, BpTr = pPs[hb]
                        Yn, BYn = Yb.next()
                        if hb == 0:
                            P.op("dve", lambda e: e.tensor_copy(Yn[:].bitcast(F32R), h4(pTr[:])), reads=[BpTr], writes=[BYn])
                        else:
                            P.op("act", lambda e: e.activation(Yn[:].bitcast(F32R), h4(pTr[:]), AF.Copy),
                                 reads=[BpTr], writes=[BYn])
                        s_["Yc"], s_["BYc"] = Yn, BYn
                        s_["zp"], s_["Bzp"] = zpns[hb]
                    yield
                nWl = []
                for hb in range(2):
                    h0 = hb * HB
                    pZ, BpZ = pZs[hb]
                    Pf, BPf = stt[hb]["Pf"], stt[hb]["BPf"]
                    for hh in range(HB):
                        P.op("pe", lambda e, hh=hh: e.matmul(
                            pZ[:, hh * 128:(hh + 1) * 128], rw[:, h0 + hh, :], Pf[:, hh, :], start=True, stop=True),
                            reads=[Brw, BPf], writes=[BpZ])
                    nW, BnW = nWs.next()
                    P.op("act", lambda e: e.activation(nW[:], h4(pZ[:]), AF.Copy, scale=-1.0), reads=[BpZ], writes=[BnW])
                    nWl.append((nW, BnW))
                    yield
                if first:
                    P.op("pool", lambda e: e.memset(S32t[:], 0.0), writes=BS32h)
                    P.op("pool", lambda e: e.memset(Sbft[:], 0.0), writes=BSbfh)
                elif carry:
                    P.op("dve", lambda e: e.tensor_scalar(S32t[:], S32t[:], flagt[:, 0:1], None, ALU.mult),
                         reads=BS32h + [Bflag], writes=BS32h)
                    P.op("pool", lambda e: e.tensor_scalar(Sbft[:], Sbft[:], flagt[:, 0:1], None, ALU.mult),
                         reads=BSbfh + [Bflag], writes=BSbfh)
                oo, Boo = oos.next()
                for hh in range(HB):
                    for hb in range(2):
                        h = hb * HB + hh
                        Pf, BPf = stt[hb]["Pf"], stt[hb]["BPf"]
                        nW, BnW = nWl[hb]
                        pv, Bpv = pZs[hb][0][:, 0:256], pZs[hb][1]
                        po_, Bpo = pYs[hb][0][:, 0:256], pYs[hb][1]
                        ps_, Bps = pPs[hb][0][:, 0:256], pPs[hb][1]
                        P.op("pe", lambda e: e.matmul(pv, Pf[:, hh, :].bitcast(F32R), vb[:, h, :].bitcast(F32R),
                                                      start=True, stop=False), reads=[BPf, Bvb], writes=[Bpv])
                        P.op("pe", lambda e: e.matmul(pv, nW[:, hh, :], Sbft[:, h, :], start=False, stop=True),
                             reads=[BnW, BSbfh[h]], writes=[Bpv])
                        vn, Bvn = vns.next()
                        P.op("act", lambda e: e.activation(vn[:], pv, AF.Copy), reads=[Bpv], writes=[Bvn])
                        P.op("pe", lambda e: e.matmul(po_, qg[:, h, :], Sbft[:, h, :], start=True, stop=False),
                             reads=[Bqg, BSbfh[h]], writes=[Bpo])
                        P.op("pe", lambda e: e.matmul(po_, Aq[:, h, :], vn[:], start=False, stop=True),
                             reads=[BAq, Bvn], writes=[Bpo])
                        P.op("pe", lambda e: e.matmul(ps_, kd[:, h, :], vn[:], start=True, stop=True),
                             reads=[Bkd, Bvn], writes=[Bps])
                        P.op("act", lambda e: e.activation(oo[:, h, :], po_, AF.Copy), reads=[Bpo], writes=[Boo])
                        P.op("dve", lambda e: e.scalar_tensor_tensor(
                            S32t[:, h, :], S32t[:, h, :], ER[:, h, last:last + 1], ps_, ALU.mult, ALU.add),
                            reads=[BS32h[h], BER, Bps], writes=[BS32h[h]])
                        P.op("pool", lambda e: e.tensor_copy(Sbft[:, h, :], S32t[:, h, :]), reads=[BS32h[h]], writes=[BSbfh[h]])
                        yield
                P.dma(None, o_s[d][tA:tA + 128, :, :], oo[:], reads=[Boo], writes=[B_scr["of" if d == 0 else "ob"]])
                del ctxs[u]

            nU = len(units)
            for t in range(-2, nU):
                if 0 <= t + 2 < nU:
                    L_stage(t + 2)
                gens = []
                if 0 <= t < nU:
                    gens.append(M_stage(t))
                if 0 <= t + 1 < nU:
                    gens.append(Q_stage(t + 1))
                while gens:
                    for g_ in list(gens):
                        try:
                            next(g_)
                        except StopIteration:
                            gens.remove(g_)
            P.flush()

        if "stopD" in debug:
            return nc

        with contextlib.ExitStack() as st:
            Wo2, BWo2 = load_weight_bf16(st, "Wo2", g_w_out, 16, D, stage_cols=1024)
            nwb, Bnwb = mk(st, "nwE", [128, DV], F32)
            P.dma("sp", nwb[:], g_nw[0:1, :].partition_broadcast(128), writes=[Bnwb])
            fwb, Bfwb = mk(st, "fwE", [128, D], F32)
            P.dma("sp", fwb[:], fin_w[0:1, :].partition_broadcast(128), writes=[Bfwb])
            ofs = Rot([mk(st, "ofE%d" % i, [128, 8, 256], F32) for i in range(4)])
            obs = Rot([mk(st, "obE%d" % i, [128, 8, 256], F32) for i in range(3)])
            zts = Rot([mk(st, "zE%d" % i, [128, 8, 256], BF16) for i in range(4)])
            h1s = Rot([mk(st, "h1E%d" % i, [128, D], F32) for i in range(3)])
            jk, Bjk = mk(st, "jkE", [128, 256], BF16)
            jk2, Bjk2 = mk(st, "jk2E", [128, D], BF16)
            sss = Rot([mk(st, "ssE%d" % i, [128, 16], F32) for i in range(4)])
            ons = Rot([mk(st, "onE%d" % i, [128, 8, 256], BF16) for i in range(3)])
            onTs = Rot([mk(st, "onTE%d" % i, [128, 16, 128], BF16) for i in range(3)])
            h2s = Rot([mk(st, "h2E%d" % i, [128, D], F32) for i in range(3)])
            yos = Rot([mk(st, "yoE%d" % i, [128, D], F32) for i in range(2)])
            s2s = Rot([mk(st, "s2E%d" % i, [128, 2], F32) for i in range(2)])
            pTs = Rot([mk(st, "pTE%d" % i, [128, 1024], BF16, psum=True) for i in range(4)])
            pos = Rot([mk(st, "poE%d" % i, [128, 512], F32, psum=True) for i in range(4)])
            ectx = {}

            def E0(j):
                tA = j * 128
                c_ = {}
                ectx[j] = c_
                of, Bof = ofs.next()
                ob, Bob = obs.next()
                zt, Bzt = zts.next()
                c_.update(of=of, Bof=Bof, ob=ob, Bob=Bob, zt=zt, Bzt=Bzt)
                P.dma(None, of[:], o_s[0][tA:tA + 128, :, :], reads=[B_scr["of"]], writes=[Bof])
                P.dma(None, ob[:], o_s[1][tA:tA + 128, :, :], reads=[B_scr["ob"]], writes=[Bob])
                P.dma(None, zt[:], z_s[tA:tA + 128, :].rearrange("p (h d) -> p h d", h=8), reads=[B_scr["z"]], writes=[Bzt])

            def E1(j):
                c_ = ectx[j]
                of, Bof, ob, Bob = c_["of"], c_["Bof"], c_["ob"], c_["Bob"]
                P.op("pool", lambda e: e.tensor_tensor(of[:], of[:], ob[:], ALU.add), reads=[Bof, Bob], writes=[Bof])
                ss, Bss = sss.next()
                c_["ss"], c_["Bss"] = ss, Bss
                for h in range(8):
                    P.op("act", lambda e: e.activation(
                        jk[:], of[:, h, :], AF.Square, scale=1.0 / 16.0, accum_out=ss[:, h:h + 1]),
                        reads=[Bof], writes=[Bjk, Bss])
                P.op("act", lambda e: e.activation(ss[:, 8:16], ss[:, 0:8], AF.Sqrt, bias=RMS_EPS), reads=[Bss], writes=[Bss])
                P.op("dve", lambda e: e.reciprocal(ss[:, 8:16], ss[:, 8:16]), reads=[Bss], writes=[Bss])

            def E2(j):
                c_ = ectx[j]
                of, Bof, zt, Bzt, ss, Bss = c_["of"], c_["Bof"], c_["zt"], c_["Bzt"], c_["ss"], c_["Bss"]
                on, Bon = ons.next()
                c_["on"], c_["Bon"] = on, Bon
                for h in range(8):
                    P.op("dve", lambda e: e.scalar_tensor_tensor(
                        of[:, h, :], of[:, h, :], ss[:, 8 + h:9 + h], nwb[:], ALU.mult, ALU.mult),
                        reads=[Bof, Bss, Bnwb], writes=[Bof])
                P.op("pool", lambda e: e.tensor_tensor(on[:], of[:], zt[:], ALU.mult), reads=[Bof, Bzt], writes=[Bon])

            def E3(j):
                c_ = ectx[j]
                on, Bon = c_["on"], c_["Bon"]
                tA = j * 128
                h1, Bh1 = h1s.next()
                c_["h1"], c_["Bh1"] = h1, Bh1
                P.dma(None, h1[:], h1_s[tA:tA + 128, :], reads=[B_scr["h1"]], writes=[Bh1])
                onT, BonT = onTs.next()
                c_["onT"], c_["BonT"] = onT, BonT
                for half in range(2):
                    pT, BpT = pTs.next()
                    for c in range(8):
                        cc = half * 8 + c
                        P.op("pe", lambda e: e.transpose(
                            pT[:, c * 128:(c + 1) * 128], on[:, cc // 2, (cc % 2) * 128:(cc % 2 + 1) * 128], identb[:]),
                            reads=[Bon, Bidb], writes=[BpT])
                    P.op("act", lambda e: e.activation(
                        onT[:, half * 8:(half + 1) * 8, :], pT[:].rearrange("p (c t) -> p c t", c=8), AF.Copy),
                        reads=[BpT], writes=[BonT])

            def E4(j):
                c_ = ectx[j]
                onT, BonT, h1, Bh1 = c_["onT"], c_["BonT"], c_["h1"], c_["Bh1"]
                h2, Bh2 = h2s.next()
                c_["h2"], c_["Bh2"] = h2, Bh2
                for half in range(2):
                    po_, Bpo = pos.next()
                    for c in range(16):
                        P.op("pe", lambda e: e.matmul(
                            po_[:], onT[:, c, :], Wo2[:, c, half * 512:(half + 1) * 512], start=(c == 0), stop=(c == 15)),
                            reads=[BonT, BWo2], writes=[Bpo])
                    P.op("dve", lambda e: e.tensor_tensor(
                        h2[:, half * 512:(half + 1) * 512], po_[:], h1[:, half * 512:(half + 1) * 512], ALU.add),
                        reads=[Bpo, Bh1], writes=[Bh2])

            def E5(j):
                c_ = ectx.pop(j)
                h2, Bh2 = c_["h2"], c_["Bh2"]
                tA = j * 128
                yo, Byo = yos.next()
                s2, Bs2 = s2s.next()
                rmsnorm_rows(jk2, h2[:, :], 128, fwb, yo[:, :], s2[:, 0:1], s2[:, 1:2], [Bh2], Bs2, Bjk2, Byo, Bfwb)
                P.dma(None, y_out[tA:tA + 128, :], yo[:], reads=[Byo], writes=[B_x])

            nE = NT // 128
            for t in range(nE + 5):
                for fn_, off_ in ((E5, 5), (E4, 4), (E3, 3), (E2, 2), (E1, 1), (E0, 0)):
                    if 0 <= t - off_ < nE:
                        fn_(t - off_)
            P.flush()
    return nc


def make_consts():
    c = np.zeros((128, 8, 128), np.float32)
    p = np.arange(128)[:, None]
    f = np.arange(128)[None, :]
    c[:, 0, :] = (p == f)
    c[:, 1, :] = 1.0
    c[:, 2, :] = (p <= f)
    c[:, 3, :] = (p >= f)
    c[:, 4, :] = np.where(f >= p, 0.0, -BIG)
    c[:, 5, :] = np.where(f <= p, 0.0, -BIG)
    c[:, 6, :] = np.where(f < p, 0.0, -BIG)
    c[:, 7, :] = np.where(f > p, 0.0, -BIG)
    return c


def shared_inputs(ln_w, na_w_in, na_w_out, gdn_w_in, gdn_conv_w, gdn_a_log, gdn_dt_bias, gdn_norm_w, gdn_w_out,
                  final_norm_w):
    f = lambda a: np.ascontiguousarray(np.asarray(a, dtype=np.float32))
    return {
        "ln_w_bc": f(ln_w),
        "fin_w": f(final_norm_w).reshape(1, D),
        "na_w_in": f(na_w_in[0]),
        "na_w_out": f(na_w_out[0]),
        "g_w_in": f(gdn_w_in[0]),
        "g_cw": f(np.asarray(gdn_conv_w[0]).T.reshape(32, 128, 5).transpose(1, 0, 2)),
        "g_alog": f(gdn_a_log[0]).reshape(1, 16),
        "g_dtb": f(gdn_dt_bias[0]).reshape(1, 16),
        "g_nw": f(gdn_norm_w[0]).reshape(1, DV),
        "g_w_out": f(gdn_w_out[0]),
        "consts": make_consts(),
    }


_PROG_CACHE = {}


def kernel(x_prompt, x_sample, ln_w, na_w_in, na_rpb, na_w_out, gdn_w_in, gdn_conv_w, gdn_a_log, gdn_dt_bias,
           gdn_norm_w, gdn_w_out, final_norm_w):
    seg_t = 2048
    x_prompt = np.asarray(x_prompt, np.float32)
    x_sample = np.asarray(x_sample, np.float32)
    shared = shared_inputs(ln_w, na_w_in, na_w_out, gdn_w_in, gdn_conv_w, gdn_a_log, gdn_dt_bias, gdn_norm_w,
                           gdn_w_out, final_norm_w)
    rpb = np.asarray(na_rpb, np.float32)[0]
    tabs_j = build_na_bias_tables(rpb, seg_t, True).reshape(-1, 128, NH_A, 768)
    tabs_s = build_na_bias_tables(rpb, seg_t, False).reshape(-1, 128, NH_A, 768)
    in_maps = []
    for c in range(8):
        if c < 2:
            xc = np.concatenate([x_prompt[c], x_sample[c]], axis=0)
            m = dict(shared, x=np.ascontiguousarray(xc), na_bias=tabs_j, flag=np.ones((128, 1), np.float32))
        else:
            s0 = 2 + 5 * (c - 2)
            xc = x_sample[s0:s0 + 5].reshape(5 * seg_t, D)
            m = dict(shared, x=np.ascontiguousarray(xc), na_bias=tabs_s, flag=np.zeros((128, 1), np.float32))
        in_maps.append(m)
    if seg_t not in _PROG_CACHE:
        _PROG_CACHE[seg_t] = build_program(seg_t)
    nc = _PROG_CACHE[seg_t]
    res = run_bass_kernel_spmd(nc, in_maps, core_ids=list(range(8)))
    y_prompt = np.empty_like(x_prompt)
    y_sample = np.empty_like(x_sample)
    for c in range(8):
        y = res.results[c]["y"]
        if c < 2:
            y_prompt[c] = y[0:4 * seg_t]
            y_sample[c] = y[4 * seg_t:]
        else:
            s0 = 2 + 5 * (c - 2)
            y_sample[s0:s0 + 5] = y.reshape(5, seg_t, D)
    return (y_prompt, y_sample)
```

```python
import contextlib
import numpy as np
import concourse.bass as bass
import concourse.mybir as mybir
from concourse.bass_utils import run_bass_kernel_spmd

F32 = mybir.dt.float32
BF16 = mybir.dt.bfloat16
F32R = mybir.dt.float32r
ALU = mybir.AluOpType
AF = mybir.ActivationFunctionType

D = 1024
NH_A = 16
GW = 64
NH_G = 8
DK = 128
DV = 256
NSEG = 5
GRP = 4
BIG = 1.0e5
RMS_EPS = 1e-6
L2_EPS = 1e-6

NSLOT = 32
DMA_QS = ("sp", "pool")


class Buf:
    __slots__ = ("w", "r")

    def __init__(self):
        self.w = {}
        self.r = {}


class PBuf(Buf):
    __slots__ = ("bank",)

    def __init__(self):
        Buf.__init__(self)
        self.bank = Buf()


class _Rec:
    def __getattr__(self, name):
        def f(*a, **k):
            self.call = (name, a, k)
            return self
        return f


class Prog:
    def __init__(self, nc):
        self.nc = nc
        self.streams = {e: [] for e in ("pe", "act", "dve", "pool", "sp")}
        self.count = {e: 0 for e in self.streams}
        self.waited = {e: {} for e in self.streams}
        self.dma_n = {q: 0 for q in DMA_QS}
        self.semh = {}
        self.ninst = 0
        self.rr = 0
        self.only_sp = False

    def _deps(self, eng, reads, writes):
        deps = {}
        for b in reads:
            for k, v in b.w.items():
                if deps.get(k, 0) < v:
                    deps[k] = v
        for b in writes:
            for k, v in b.w.items():
                if deps.get(k, 0) < v:
                    deps[k] = v
            for k, v in b.r.items():
                if deps.get(k, 0) < v:
                    deps[k] = v
        for b in list(reads) + list(writes):
            if isinstance(b, PBuf):
                for k, v in b.bank.w.items():
                    if k != eng and deps.get(k, 0) < v:
                        deps[k] = v
        wt = self.waited[eng]
        out = []
        for k, v in deps.items():
            if k == "pe" and eng == "pe":
                continue
            if wt.get(k, 0) >= v:
                continue
            wt[k] = v
            out.append((k, v))
        return out

    def op(self, eng, fn, reads=(), writes=()):
        rec = _Rec()
        fn(rec)
        cname, cargs, ckw = rec.call

        def fn(e, cname=cname, cargs=cargs, ckw=ckw):
            return getattr(e, cname)(*cargs, **ckw)
        waits = self._deps(eng, reads, writes)
        self.count[eng] += 1
        n = self.count[eng]
        self.streams[eng].append((waits, fn, (eng, 1)))
        for b in reads:
            b.r[eng] = n
            if isinstance(b, PBuf):
                b.bank.w[eng] = n
        for b in writes:
            b.w[eng] = n
            b.r = {}
            if isinstance(b, PBuf):
                b.bank.w[eng] = n
        self.ninst += 1

    def dma(self, q, out, in_, reads=(), writes=()):
        if q is None:
            q = "sp"
            self.rr += 1
        waits = self._deps(q, reads, writes)
        n = self.dma_n[q]
        self.dma_n[q] += 1
        slot = n % NSLOT
        val = 16 * (n // NSLOT + 1)
        key = "d_%s_%d" % (q, slot)
        if n >= NSLOT and self.waited[q].get(key, 0) < val - 16:
            self.waited[q][key] = val - 16
            waits.append((key, val - 16))

        def fn(e, out=out, in_=in_):
            return e.dma_start(out=out, in_=in_)
        self.streams[q].append((waits, fn, (key, 16)))
        for b in reads:
            b.r[key] = val
        for b in writes:
            b.w[key] = val
            b.r = {}
        self.ninst += 1

    def alloc_sems(self, st):
        keys = ["pe", "act", "dve", "pool"]
        for q in DMA_QS:
            for s_ in range(NSLOT):
                keys.append("d_%s_%d" % (q, s_))
        for k in keys:
            self.semh[k] = st.enter_context(self.nc.semaphore("s_" + k))

    def _all_events(self):
        fin = []
        for e in ("pe", "act", "dve", "pool"):
            if self.count[e]:
                fin.append((e, self.count[e]))
        for q in DMA_QS:
            n = self.dma_n[q]
            for s_ in range(NSLOT):
                cnt = (n - s_ + NSLOT - 1) // NSLOT if n > s_ else 0
                if cnt:
                    fin.append(("d_%s_%d" % (q, s_), 16 * cnt))
        return fin

    def flush(self):
        nc = self.nc
        fin = self._all_events()
        semh = self.semh
        streams = self.streams
        self.streams = {e: [] for e in streams}
        with nc.Block() as block:
            def run(engobj, name):
                for waits, fn, (ik, iv) in streams[name]:
                    for k, v in waits:
                        engobj.wait_ge(semh[k], v)
                    ins = fn(engobj)
                    ins.then_inc(semh[ik], iv)
                wt = self.waited[name]
                for k, v in fin:
                    if wt.get(k, 0) >= v:
                        continue
                    wt[k] = v
                    engobj.wait_ge(semh[k], v)

            @block.sync
            def _(e):
                run(e, "sp")

            @block.tensor
            def _(e):
                run(e, "pe")

            @block.scalar
            def _(e):
                run(e, "act")

            @block.vector
            def _(e):
                run(e, "dve")

            @block.gpsimd
            def _(e):
                run(e, "pool")


class Rot:
    def __init__(self, items):
        self.items = items
        self.i = 0

    def next(self):
        it = self.items[self.i % len(self.items)]
        self.i += 1
        return it


def na_chunk_range(i, nb, nbs):
    p = i % nbs
    lo, hi = i - 2, i + 2
    if p == 0:
        hi = i + 3
    if p == nbs - 1:
        lo = i - 3
    return max(lo, 0), min(hi, nb - 1)


def na_special_pos(p, nbs):
    sp = sorted(set([0, 1, nbs - 2, nbs - 1]))
    if nbs <= 4:
        sp = list(range(nbs))
    return sp.index(p) if p in sp else None


def na_nspecial(nbs):
    return min(nbs, 4)


def na_table_id(seg, p, nbs):
    spi = na_special_pos(p, nbs)
    if spi is None:
        return 0
    nsp = na_nspecial(nbs)
    if seg == GRP:
        kind = 2
    else:
        top = p < nbs // 2 if nbs > 2 else (p == 0)
        if top:
            kind = 0 if seg == 0 else 1
        else:
            kind = 0 if seg == GRP - 1 else 1
    return 1 + kind * nsp + spi


def build_na_bias_tables(rpb, seg_t, joined):
    rs_rows = seg_t // GW
    nbs = rs_rows // 2
    nsp = na_nspecial(nbs)
    ntab = 1 + 3 * nsp
    tabs = np.full((ntab, 128, NH_A, 6, 128), -BIG, dtype=np.float32)
    done = set()

    def fill(tid, i, nb, seq_of_row):
        if tid in done:
            return
        done.add(tid)
        lo, hi = na_chunk_range(i, nb, nbs)
        a = np.arange(2)[:, None, None, None]
        kc = np.arange(GW)[None, :, None, None]
        b = np.arange(2)[None, None, :, None]
        qc = np.arange(GW)[None, None, None, :]
        for s, m in enumerate(range(lo, hi + 1)):
            kr = 2 * m + a
            qr = 2 * i + b
            s_lo_q, s_r_q = seq_of_row(qr)
            s_lo_k, _ = seq_of_row(kr)
            rs = np.clip(qr - s_lo_q - 4, 0, s_r_q - 8)
            okr = (s_lo_k == s_lo_q) & (kr - s_lo_q >= rs) & (kr - s_lo_q < rs + 8)
            ws = np.clip(qc - 8, 0, GW - 16)
            okc = (kc >= ws) & (kc < ws + 16)
            ok = np.broadcast_to(okr & okc, (2, GW, 2, GW))
            dr = np.broadcast_to(np.clip(kr - qr + 7, 0, 14), (2, GW, 2, GW))
            dc = np.broadcast_to(np.clip(kc - qc + 15, 0, 30), (2, GW, 2, GW))
            vals = rpb[:, dr, dc]
            vals = np.where(ok[None], vals, np.float32(-BIG)).astype(np.float32)
            tabs[tid, :, :, s, :] = vals.reshape(NH_A, 128, 128).transpose(1, 0, 2)

    nbg = GRP * nbs

    def seq_group(row):
        if joined:
            return np.zeros_like(row), np.full_like(row, GRP * rs_rows)
        return (row // rs_rows) * rs_rows, np.full_like(row, rs_rows)

    def seq_single(row):
        return np.zeros_like(row), np.full_like(row, rs_rows)

    for seg in range(GRP):
        for p in range(nbs):
            tid = na_table_id(seg, p, nbs)
            fill(tid, seg * nbs + p, nbg, seq_group)
    for p in range(nbs):
        fill(na_table_id(GRP, p, nbs), p, nbs, seq_single)
    return tabs


def build_program(seg_t, debug=()):
    NT = NSEG * seg_t
    NCH = NT // 128
    NCS = seg_t // 128
    NBS = NCS
    nsp = na_nspecial(NBS)
    NTAB = 1 + 3 * nsp
    assert seg_t % 512 == 0

    nc = bass.Bass("TRN2", target_bir_lowering=False)

    def din(name, shape, dt=F32):
        return nc.dram_tensor(name, list(shape), dt, kind="ExternalInput").ap()

    def dscr(name, shape, dt):
        kind = "ExternalOutput" if name in debug else "Internal"
        return nc.dram_tensor(name, list(shape), dt, kind=kind).ap()

    x_in = din("x", [NT, D])
    ln_w_bc = din("ln_w_bc", [2, D])
    fin_w = din("fin_w", [1, D])
    na_w_in = din("na_w_in", [D, 4 * D])
    na_bias = din("na_bias", [NTAB, 128, NH_A, 6 * 128])
    na_w_out = din("na_w_out", [D, D])
    g_w_in = din("g_w_in", [D, 6176])
    g_cw = din("g_cw", [128, 32, 5])
    g_alog = din("g_alog", [1, 16])
    g_dtb = din("g_dtb", [1, 16])
    g_nw = din("g_nw", [1, DV])
    g_w_out = din("g_w_out", [2 * D, D])
    consts = din("consts", [128, 8, 128])
    flag_in = din("flag", [128, 1])
    y_out = nc.dram_tensor("y", [NT, D], F32, kind="ExternalOutput").ap()

    qT_s = dscr("qT_s", [NCH, 128, 8, 128], BF16)
    kT_s = dscr("kT_s", [NCH, 128, 8, 128], BF16)
    v_s = dscr("v_s", [NT, D], BF16)
    sg_s = dscr("sg_s", [NT, D], BF16)
    eb_s = dscr("eb_s", [NTAB, 128, NH_A, 768], BF16)
    h1_s = dscr("h1_s", [NT, D], F32)
    gq_s = dscr("gq_s", [NCH, 128, 8, 128], BF16)
    gk_s = dscr("gk_s", [NCH, 128, 8, 128], BF16)
    gkt_s = dscr("gkt_s", [NT, 8, 128], BF16)
    gv_s = dscr("gv_s", [NT, 8, 256], BF16)
    z_s = dscr("z_s", [NT, 2 * D], BF16)
    bg_s = dscr("bg_s", [NT, 32], F32)
    gcr_s = dscr("gcr_s", [2, 8, NT], F32)
    o_s = [dscr("of_s", [NT, 8, 256], F32), dscr("ob_s", [NT, 8, 256], F32)]

    P = Prog(nc)
    B_x = Buf()
    B_scr = {n: Buf() for n in ("qT", "kT", "v", "sg", "eb", "h1", "gq", "gk", "gkt", "gv", "z", "bg", "gcr",
                                "of", "ob")}

    with contextlib.ExitStack() as st0:
        P.alloc_sems(st0)

        def mk(st, name, shape, dt, psum=False):
            if psum:
                esz = 2 if dt == BF16 else 4
                per_bank = 2048 // esz
                assert len(shape) == 2
                ncols = ((shape[1] + per_bank - 1) // per_bank) * per_bank
                t_ = st.enter_context(nc.psum_tensor(name, [shape[0], ncols], dt))
                return t_[:, 0:shape[1]], PBuf()
            return st.enter_context(nc.sbuf_tensor(name, list(shape), dt)), Buf()

        cst, Bc = mk(st0, "cst", [128, 8, 128], F32)
        identb, Bidb = mk(st0, "identb", [128, 128], BF16)
        flagt, Bflag = mk(st0, "flagt", [128, 1], F32)
        P.dma("sp", cst[:], consts[:, :, :], writes=[Bc])
        P.dma("sp", flagt[:], flag_in[:, :], writes=[Bflag])
        P.op("dve", lambda e: e.tensor_copy(identb[:], cst[:, 0, :]), reads=[Bc], writes=[Bidb])
        identf = cst[:, 0, :]
        onesf = cst[:, 1, :]
        trif = cst[:, 2, :]
        trib = cst[:, 3, :]
        maskD = [cst[:, 4, :], cst[:, 5, :]]
        maskS = [cst[:, 6, :], cst[:, 7, :]]

        def load_weight_bf16(st, name, src, kchunks, ncols, stage_cols=2048):
            wt, Bw = mk(st, name, [128, kchunks, ncols], BF16)
            engs = ["act", "dve", "pool"]
            k = 0
            with contextlib.ExitStack() as stl:
                stg = [mk(stl, "%s_stg%d" % (name, i), [128, stage_cols], F32) for i in range(2)]
                it = 0
                for c in range(kchunks):
                    for c0 in range(0, ncols, stage_cols):
                        w_ = min(stage_cols, ncols - c0)
                        s_t, s_b = stg[it % 2]
                        it += 1
                        P.dma(None, s_t[:, 0:w_], src[c * 128:(c + 1) * 128, c0:c0 + w_], writes=[s_b])
                        en = engs[k % 3]
                        k += 1
                        if en == "act":
                            P.op("act", lambda e, s_t=s_t, c=c, c0=c0, w_=w_: e.activation(
                                wt[:, c, c0:c0 + w_], s_t[:, 0:w_], AF.Copy), reads=[s_b], writes=[Bw])
                        else:
                            P.op(en, lambda e, s_t=s_t, c=c, c0=c0, w_=w_: e.tensor_copy(
                                wt[:, c, c0:c0 + w_], s_t[:, 0:w_]), reads=[s_b], writes=[Bw])
                P.flush()
            return wt, Bw

        def rmsnorm_rows(jk, xt_ap, np_, lnw_bc, xn_ap, ss, rs, reads, Bss, Bjk, Bxn, Blnw, flag=None):
            P.op("act", lambda e: e.activation(jk[0:np_, :], xt_ap, AF.Square, scale=1.0 / 32.0,
                                               accum_out=ss[0:np_, 0:1]), reads=reads, writes=[Bjk, Bss])
            P.op("act", lambda e: e.activation(rs[0:np_, 0:1], ss[0:np_, 0:1], AF.Sqrt, bias=RMS_EPS),
                 reads=[Bss], writes=[Bss])
            P.op("dve", lambda e: e.reciprocal(rs[0:np_, 0:1], rs[0:np_, 0:1]), reads=[Bss], writes=[Bss])
            if flag is not None:
                P.op("dve", lambda e: e.tensor_tensor(rs[0:np_, 0:1], rs[0:np_, 0:1], flag[0:np_, 0:1], ALU.mult),
                     reads=[Bss, Bflag], writes=[Bss])
            P.op("dve", lambda e: e.scalar_tensor_tensor(xn_ap, xt_ap, rs[0:np_, 0:1], lnw_bc[0:np_, :],
                                                         ALU.mult, ALU.mult),
                 reads=list(reads) + [Bss, Blnw], writes=[Bxn])

        P.flush()

        with contextlib.ExitStack() as st:
            stg = [mk(st, "eb_stg%d" % i, [128, 4, 768], F32) for i in range(2)]
            ebo = [mk(st, "eb_o%d" % i, [128, 4, 768], BF16) for i in range(2)]
            it = 0
            for t in range(NTAB):
                for hq in range(4):
                    s_t, s_b = stg[it % 2]
                    o_t, o_b = ebo[it % 2]
                    it += 1
                    P.dma(None, s_t[:], na_bias[t, :, hq * 4:(hq + 1) * 4, :], writes=[s_b])
                    P.op("act", lambda e, s_t=s_t, o_t=o_t: e.activation(o_t[:], s_t[:], AF.Exp),
                         reads=[s_b], writes=[o_b])
                    P.dma(None, eb_s[t, :, hq * 4:(hq + 1) * 4, :], o_t[:], reads=[o_b], writes=[B_scr["eb"]])
            P.flush()

        with contextlib.ExitStack() as st:
            Wa, BWa = load_weight_bf16(st, "Wa", na_w_in, 8, 4 * D)
            lnw, Blnw = mk(st, "lnwA", [128, D], F32)
            P.dma("sp", lnw[:], ln_w_bc[0:1, :].partition_broadcast(128), writes=[Blnw])
            xts = Rot([mk(st, "xtA%d" % i, [128, 4, D], F32) for i in range(2)])
            xn, Bxn = mk(st, "xnA", [128, 4, D], BF16)
            jk, Bjk = mk(st, "jkA", [128, D], BF16)
            sss = Rot([mk(st, "ssA%d" % i, [128, 2], F32) for i in range(4)])
            hnTs = Rot([mk(st, "hnTA%d" % i, [128, 8, 512], BF16) for i in range(2)])
            pTs = Rot([mk(st, "pTA%d" % i, [128, 512], BF16, psum=True) for i in range(2)])
            pqs = Rot([mk(st, "pqA%d" % i, [128, 512], F32, psum=True) for i in range(4)])
            qos = Rot([mk(st, "qoA%d" % i, [128, 512], BF16) for i in range(4)])
            for g in range(NT // 512):
                t0 = g * 512
                xt, Bxt = xts.next()
                P.dma(None, xt[:], x_in[t0:t0 + 512, :].rearrange("(j p) d -> p j d", p=128), reads=[B_x],
                      writes=[Bxt])
                for j in range(4):
                    ss, Bss = sss.next()
                    rmsnorm_rows(jk, xt[:, j, :], 128, lnw, xn[:, j, :], ss[:, 0:1], ss[:, 1:2], [Bxt], Bss,
                                 Bjk, Bxn, Blnw)
                hnT, BhnT = hnTs.next()
                for c in range(8):
                    pT, BpT = pTs.next()
                    for j in range(4):
                        P.op("pe", lambda e, pT=pT, j=j, c=c: e.transpose(
                            pT[:, j * 128:(j + 1) * 128], xn[:, j, c * 128:(c + 1) * 128], identb[:]),
                            reads=[Bxn, Bidb], writes=[BpT])
                    en = "dve" if c % 2 == 0 else "pool"
                    if en == "dve":
                        P.op("dve", lambda e, pT=pT, hnT=hnT, c=c: e.tensor_copy(hnT[:, c, :], pT[:]),
                             reads=[BpT], writes=[BhnT])
                    else:
                        P.op("act", lambda e, pT=pT, hnT=hnT, c=c: e.activation(hnT[:, c, :], pT[:], AF.Copy),
                             reads=[BpT], writes=[BhnT])
                for fo in range(16):
                    pq, Bpq = pqs.next()
                    for c in range(8):
                        P.op("pe", lambda e, pq=pq, hnT=hnT, c=c, fo=fo: e.matmul(
                            pq[:], Wa[:, c, fo * 128:(fo + 1) * 128], hnT[:, c, :], start=(c == 0), stop=(c == 7)),
                            reads=[BWa, BhnT], writes=[Bpq])
                    qo, Bqo = qos.next()
                    sc = 0.125 if fo < 8 else 1.0
                    P.op("act", lambda e, pq=pq, qo=qo, sc=sc: e.activation(qo[:], pq[:], AF.Copy, scale=sc),
                         reads=[Bpq], writes=[Bqo])
                    dst = qT_s if fo < 8 else kT_s
                    bd = B_scr["qT"] if fo < 8 else B_scr["kT"]
                    ch0 = t0 // 128
                    P.dma(None, dst[ch0:ch0 + 4, :, fo % 8, :].rearrange("j p t -> p j t"),
                          qo[:].rearrange("p (j t) -> p j t", j=4), reads=[Bqo], writes=[bd])
                for j in range(4):
                    for fg in range(4):
                        pq, Bpq = pqs.next()
                        for c in range(8):
                            P.op("pe", lambda e, pq=pq, hnT=hnT, c=c, fg=fg, j=j: e.matmul(
                                pq[:], hnT[:, c, j * 128:(j + 1) * 128],
                                Wa[:, c, 2048 + fg * 512:2048 + (fg + 1) * 512], start=(c == 0), stop=(c == 7)),
                                reads=[BWa, BhnT], writes=[Bpq])
                        qo, Bqo = qos.next()
                        if fg < 2:
                            P.op("dve", lambda e, pq=pq, qo=qo: e.tensor_copy(qo[:], pq[:]), reads=[Bpq], writes=[Bqo])
                            P.dma(None, v_s[t0 + j * 128:t0 + (j + 1) * 128, fg * 512:(fg + 1) * 512], qo[:],
                                  reads=[Bqo], writes=[B_scr["v"]])
                        else:
                            P.op("act", lambda e, pq=pq, qo=qo: e.activation(qo[:], pq[:], AF.Silu),
                                 reads=[Bpq], writes=[Bqo])
                            P.dma(None, sg_s[t0 + j * 128:t0 + (j + 1) * 128, (fg - 2) * 512:(fg - 1) * 512], qo[:],
                                  reads=[Bqo], writes=[B_scr["sg"]])
            P.flush()

        if "stopA" in debug:
            return nc

        with contextlib.ExitStack() as st:
            P.only_sp = "onlysp" in debug
            Wo, BWo = load_weight_bf16(st, "Wo", na_w_out, 8, D, stage_cols=1024)
            ebg, Bebg = mk(st, "ebg", [128, NH_A, 768], BF16)
            P.dma("sp", ebg[:], eb_s[0, :, :, :], reads=[B_scr["eb"]], writes=[Bebg])
            ebsp = Rot([mk(st, "ebsp%d" % i, [128, NH_A, 768], BF16) for i in range(2)])
            NR = 8
            kring = [mk(st, "kring%d" % i, [128, 8, 128], BF16) for i in range(NR)]
            vring = [mk(st, "vring%d" % i, [128, NH_A, 128], BF16) for i in range(NR)]
            for i in range(NR):
                P.op("pool", lambda e, i=i: e.memset(vring[i][0][:, :, 64:128], 1.0), writes=[vring[i][1]])
            qts = Rot([mk(st, "qtB%d" % i, [128, 8, 128], BF16) for i in range(2)])
            sgs = Rot([mk(st, "sgB%d" % i, [128, D], BF16) for i in range(2)])
            xbs = Rot([mk(st, "xB%d" % i, [128, D], F32) for i in range(3)])
            Es = Rot([mk(st, "EB%d" % i, [128, 768], BF16) for i in range(4)])
            Pms = Rot([mk(st, "PmB%d" % i, [128, 768], BF16) for i in range(4)])
            ogs = Rot([mk(st, "ogB%d" % i, [128, D], BF16) for i in range(2)])
            ogTs = Rot([mk(st, "ogTB%d" % i, [128, 8, 128], BF16) for i in range(2)])
            h1os = Rot([mk(st, "h1oB%d" % i, [128, D], F32) for i in range(2)])
            rcs = Rot([mk(st, "rcB%d" % i, [128, 1], F32) for i in range(4)])
            onrm = Rot([mk(st, "onB%d" % i, [128, 64], BF16) for i in range(4)])
            psts = Rot([mk(st, "pstB%d" % i, [128, 1024], F32, psum=True) for i in range(3)])
            pOs = Rot([mk(st, "pOB%d" % i, [128, 128], F32, psum=True) for i in range(1)])
            po_t, Bpo_t = mk(st, "poB", [128, 512], F32, psum=True)
            pT2, BpT2 = po_t.bitcast(BF16), Bpo_t
            pos = Rot([(po_t, Bpo_t)])

            bunits = []
            for (seg0, nsegs) in ((0, GRP), (GRP, 1)):
                nb = nsegs * NBS
                for i in range(nb):
                    bunits.append((seg0, nb, i))
            bctx = {}
            loaded = set()

            def B0(u):
                seg0, nb, i = bunits[u]
                c_ = {}
                bctx[u] = c_
                tokbase = seg0 * seg_t
                chbase = tokbase // 128
                seg = seg0 + i // NBS
                p = i % NBS
                lo, hi = na_chunk_range(i, nb, NBS)
                tid = na_table_id(seg, p, NBS)
                for m in range(lo, hi + 1):
                    if (chbase + m) in loaded:
                        continue
                    loaded.add(chbase + m)
                    kt, Bkt = kring[(chbase + m) % NR]
                    vt, Bvt = vring[(chbase + m) % NR]
                    P.dma(None, kt[:], kT_s[chbase + m, :, :, :], reads=[B_scr["kT"]], writes=[Bkt])
                    tk = tokbase + m * 128
                    P.dma(None, vt[:, :, 0:64], v_s[tk:tk + 128, :].rearrange("p (h d) -> p h d", h=NH_A),
                          reads=[B_scr["v"]], writes=[Bvt])
                tq = tokbase + i * 128
                qt, Bqt = qts.next()
                sg, Bsg = sgs.next()
                P.dma(None, qt[:], qT_s[chbase + i, :, :, :], reads=[B_scr["qT"]], writes=[Bqt])
                P.dma(None, sg[:], sg_s[tq:tq + 128, :], reads=[B_scr["sg"]], writes=[Bsg])
                if tid == 0:
                    ebt, Bebt = ebg, Bebg
                else:
                    ebt, Bebt = ebsp.next()
                    P.dma(None, ebt[:], eb_s[tid, :, :, :], reads=[B_scr["eb"]], writes=[Bebt])
                c_.update(qt=qt, Bqt=Bqt, sg=sg, Bsg=Bsg, ebt=ebt, Bebt=Bebt, lo=lo, ns=hi - lo + 1, chbase=chbase, tq=tq)

            def B1(u):
                c_ = bctx[u]
                qt, Bqt, sg, Bsg, ebt, Bebt = c_["qt"], c_["Bqt"], c_["sg"], c_["Bsg"], c_["ebt"], c_["Bebt"]
                lo, ns, chbase = c_["lo"], c_["ns"], c_["chbase"]
                og, Bog = ogs.next()
                c_["og"], c_["Bog"] = og, Bog

                def s_stage(h):
                    hp, po = h // 2, (h % 2) * 64
                    pst, Bpst = psts.next()
                    for s in range(ns):
                        kt, Bkt = kring[(chbase + lo + s) % NR]
                        P.op("pe", lambda e: e.matmul(
                            pst[:, s * 128:(s + 1) * 128], kt[po:po + 64, hp, :], qt[po:po + 64, hp, :],
                            start=True, stop=True), reads=[Bkt, Bqt], writes=[Bpst])
                    E, BE = Es.next()
                    P.op("act", lambda e: e.activation(E[:, 0:ns * 128], pst[:, 0:ns * 128], AF.Exp),
                         reads=[Bpst], writes=[BE])
                    Pm, BPm = Pms.next()
                    P.op("pool" if h % 2 == 0 else "dve", lambda e: e.tensor_tensor(
                        Pm[:, 0:ns * 128], E[:, 0:ns * 128], ebt[:, h, 0:ns * 128], ALU.mult),
                        reads=[BE, Bebt], writes=[BPm])
                    return Pm, BPm

                def pv_stage(h, Pm, BPm):
                    pO, BpO = pOs.next()
                    for s in range(ns):
                        vt, Bvt = vring[(chbase + lo + s) % NR]
                        P.op("pe", lambda e: e.matmul(
                            pO[:, 0:72], Pm[:, s * 128:(s + 1) * 128], vt[:, h, 0:72],
                            start=(s == 0), stop=(s == ns - 1)), reads=[BPm, Bvt], writes=[BpO])
                    rc, Brc = rcs.next()
                    P.op("dve", lambda e: e.reciprocal(rc[:], pO[:, 64:65]), reads=[BpO], writes=[Brc])
                    on_, Bon_ = onrm.next()
                    P.op("dve", lambda e: e.tensor_scalar(on_[:], pO[:, 0:64], rc[:, 0:1], None, ALU.mult),
                         reads=[BpO, Brc], writes=[Bon_])
                    P.op("dve", lambda e: e.tensor_tensor(
                        og[:, h * 64:(h + 1) * 64], on_[:], sg[:, h * 64:(h + 1) * 64], ALU.mult),
                        reads=[Bon_, Bsg], writes=[Bog])

                pend = []
                for h in range(NH_A):
                    pend.append((h, s_stage(h)))
                    if len(pend) > 2:
                        h_, (Pm_, BPm_) = pend.pop(0)
                        pv_stage(h_, Pm_, BPm_)
                for h_, (Pm_, BPm_) in pend:
                    pv_stage(h_, Pm_, BPm_)

            def B2(u):
                c_ = bctx[u]
                og, Bog = c_["og"], c_["Bog"]
                tq = c_["tq"]
                xb, Bxb = xbs.next()
                c_["xb"], c_["Bxb"] = xb, Bxb
                P.dma(None, xb[:], x_in[tq:tq + 128, :], reads=[B_x], writes=[Bxb])
                for c in range(8):
                    P.op("pe", lambda e: e.transpose(
                        pT2[:, c * 128:(c + 1) * 128], og[:, c * 128:(c + 1) * 128], identb[:]),
                        reads=[Bog, Bidb], writes=[BpT2])
                ogT, BogT = ogTs.next()
                c_["ogT"], c_["BogT"] = ogT, BogT
                P.op("act", lambda e: e.activation(ogT[:].rearrange("p c t -> p (c t)"), pT2[:], AF.Copy),
                     reads=[BpT2], writes=[BogT])

            def B3(u):
                c_ = bctx.pop(u)
                ogT, BogT, xb, Bxb, tq = c_["ogT"], c_["BogT"], c_["xb"], c_["Bxb"], c_["tq"]
                h1o, Bh1o = h1os.next()
                for half in range(2):
                    po_, Bpo = pos.next()
                    for c in range(8):
                        P.op("pe", lambda e: e.matmul(
                            po_[:], ogT[:, c, :], Wo[:, c, half * 512:(half + 1) * 512],
                            start=(c == 0), stop=(c == 7)), reads=[BogT, BWo], writes=[Bpo])
                    P.op("dve", lambda e: e.tensor_tensor(
                        h1o[:, half * 512:(half + 1) * 512], po_[:], xb[:, half * 512:(half + 1) * 512], ALU.add),
                        reads=[Bpo, Bxb], writes=[Bh1o])
                P.dma(None, h1_s[tq:tq + 128, :], h1o[:], reads=[Bh1o], writes=[B_scr["h1"]])

            nBU = len(bunits)
            for t in range(nBU + 3):
                for fn_, off_ in ((B3, 3), (B2, 2), (B1, 1), (B0, 0)):
                    if 0 <= t - off_ < nBU:
                        fn_(t - off_)
            P.flush()

        if "stopB" in debug:
            return nc

        with contextlib.ExitStack() as st:
            Wg, BWg = load_weight_bf16(st, "Wg", g_w_in, 8, 6176, stage_cols=1544)
            lnw, Blnw = mk(st, "lnwC", [128, D], F32)
            P.dma("sp", lnw[:], ln_w_bc[1:2, :].partition_broadcast(128), writes=[Blnw])
            cw, Bcw = mk(st, "cwC", [128, 32, 5], F32)
            P.dma("sp", cw[:], g_cw[:, :, :], writes=[Bcw])
            dtb, Bdtb = mk(st, "dtbC", [128, 16], F32)
            nA, BnA = mk(st, "nAC", [128, 16], F32)
            P.dma("sp", dtb[:], g_dtb[0:1, :].partition_broadcast(128), writes=[Bdtb])
            P.dma("sp", nA[:], g_alog[0:1, :].partition_broadcast(128), writes=[BnA])
            P.op("act", lambda e: e.activation(nA[:], nA[:], AF.Exp), reads=[BnA], writes=[BnA])
            P.op("dve", lambda e: e.tensor_scalar(nA[:], nA[:], -1.0, None, ALU.mult), reads=[BnA], writes=[BnA])

            W_ = seg_t + 4
            hnT, BhnT = mk(st, "hnTC", [128, 8, W_], BF16)
            xts = Rot([mk(st, "xtC%d" % i, [128, D], F32) for i in range(2)])
            xns = Rot([mk(st, "xnC%d" % i, [128, D], BF16) for i in range(2)])
            sss = Rot([mk(st, "ssC%d" % i, [128, 2], F32) for i in range(4)])
            pTs = Rot([mk(st, "pTC%d" % i, [128, 1024], BF16, psum=True) for i in range(2)])
            pps = Rot([mk(st, "ppC%d" % i, [128, 512], F32, psum=True) for i in range(4)])
            psm, Bpsm = mk(st, "psmC", [128, 512], F32, psum=True)
            HT = seg_t // 2
            NW = min(512, HT)
            NN = HT // NW
            NJ = HT // 128
            projs = Rot([mk(st, "projC%d" % i, [128, HT + 4], F32) for i in range(3)])
            accs = Rot([mk(st, "accC%d" % i, [128, HT], F32) for i in range(3)])
            ys = Rot([mk(st, "yC%d" % i, [128, HT], F32) for i in range(3)])
            ynb = Rot([mk(st, "ynC%d" % i, [128, HT], BF16) for i in range(5)])
            tks = Rot([mk(st, "tkC%d" % i, [128, 8, 128], BF16) for i in range(2)])
            zos = Rot([mk(st, "zoC%d" % i, [128, 512], BF16) for i in range(2)])
            bgt = Rot([mk(st, "bgC%d" % i, [128, 32], F32) for i in range(2)])
            gts = Rot([mk(st, "gtC%d" % i, [128, 16], F32) for i in range(2)])
            bgall = mk(st, "bgallC", [128, NCS, 32], F32)
            gtall = mk(st, "gtallC", [128, 2, NCS, 8], F32)
            grall = mk(st, "grallC", [NCS * 8, 2, 128], F32)

            for seg in range(NSEG):
                T0 = seg * seg_t
                has_prev = (seg not in (0, GRP))
                has_next = (seg not in (GRP - 1, GRP))
                s1ctx = {}

                def S1a(j):
                    xt, Bxt = xts.next()
                    xn, Bxn = xns.next()
                    ss, Bss = sss.next()
                    s1ctx[j] = (xn, Bxn)
                    if j < NCS:
                        P.dma(None, xt[:], h1_s[T0 + j * 128:T0 + (j + 1) * 128, :], reads=[B_scr["h1"]], writes=[Bxt])
                        rmsnorm_rows(xn, xt[:, :], 128, lnw, xn[:, :], ss[:, 0:1], ss[:, 1:2], [Bxt], Bss, Bxn, Bxn, Blnw)
                    else:
                        P.op("pool", lambda e: e.memset(xt[0:32, :], 0.0), writes=[Bxt])
                        if has_prev:
                            P.dma(None, xt[0:2, :], h1_s[T0 - 2:T0, :], reads=[B_scr["h1"]], writes=[Bxt])
                        if has_next:
                            P.dma(None, xt[2:4, :], h1_s[T0 + seg_t:T0 + seg_t + 2, :], reads=[B_scr["h1"]], writes=[Bxt])
                        rmsnorm_rows(xn, xt[0:4, :], 4, lnw, xn[0:4, :], ss[:, 0:1], ss[:, 1:2], [Bxt], Bss, Bxn, Bxn,
                                     Blnw, flag=flagt)

                def S1b(j):
                    xn, Bxn = s1ctx.pop(j)
                    np_ = 128 if j < NCS else 4
                    pT, BpT = pTs.next()
                    for c in range(8):
                        P.op("pe", lambda e: e.transpose(
                            pT[:, c * 128:c * 128 + np_], xn[0:np_, c * 128:(c + 1) * 128], identb[0:np_, 0:np_]),
                            reads=[Bxn, Bidb], writes=[BpT])
                    if j < NCS:
                        P.op("act", lambda e: e.activation(
                            hnT[:, :, 2 + j * 128:2 + (j + 1) * 128], pT[:].rearrange("p (c t) -> p c t", c=8), AF.Copy),
                            reads=[BpT], writes=[BhnT])
                    else:
                        pv_ = pT[:].rearrange("p (c t) -> p c t", c=8)
                        P.op("dve", lambda e: e.tensor_copy(hnT[:, :, 0:2], pv_[:, :, 0:2]), reads=[BpT], writes=[BhnT])
                        P.op("dve", lambda e: e.tensor_copy(hnT[:, :, seg_t + 2:seg_t + 4], pv_[:, :, 2:4]),
                             reads=[BpT], writes=[BhnT])

                for t in range(NCS + 2):
                    if 0 <= t - 1 <= NCS:
                        S1b(t - 1)
                    if t <= NCS:
                        S1a(t)
                cunits = [(hf, cc) for hf in range(2) for cc in range(32)]
                cctx = {}

                def PJ(u):
                    hf, cc = cunits[u]
                    c_ = {}
                    cctx[u] = c_
                    col0 = hf * HT
                    proj, Bproj = projs.next()
                    c_["proj"], c_["Bproj"] = proj, Bproj
                    for (c0, d0) in ((col0, 0), (col0 + HT + 2, 2)):
                        for c in range(8):
                            P.op("pe", lambda e: e.matmul(
                                psm[:, d0:d0 + 2], Wg[:, c, cc * 128:(cc + 1) * 128], hnT[:, c, c0:c0 + 2],
                                start=(c == 0), stop=(c == 7)), reads=[BWg, BhnT], writes=[Bpsm])
                    P.op("act", lambda e: e.activation(proj[:, 0:2], psm[:, 0:2], AF.Copy), reads=[Bpsm], writes=[Bproj])
                    P.op("act", lambda e: e.activation(proj[:, HT + 2:HT + 4], psm[:, 2:4], AF.Copy),
                         reads=[Bpsm], writes=[Bproj])
                    for n in range(NN):
                        pp, Bpp = pps.next()
                        for c in range(8):
                            P.op("pe", lambda e: e.matmul(
                                pp[:, 0:NW], Wg[:, c, cc * 128:(cc + 1) * 128],
                                hnT[:, c, col0 + 2 + n * NW:col0 + 2 + (n + 1) * NW],
                                start=(c == 0), stop=(c == 7)), reads=[BWg, BhnT], writes=[Bpp])
                        P.op("act", lambda e: e.activation(proj[:, 2 + n * NW:2 + (n + 1) * NW], pp[:, 0:NW], AF.Copy),
                             reads=[Bpp], writes=[Bproj])

                def CV(u):
                    hf, cc = cunits[u]
                    c_ = cctx[u]
                    proj, Bproj = c_["proj"], c_["Bproj"]
                    acc, Bacc = accs.next()
                    c_["acc"], c_["Bacc"] = acc, Bacc
                    P.op("dve", lambda e: e.tensor_scalar(acc[:], proj[:, 0:HT], cw[:, cc, 0:1], None, ALU.mult),
                         reads=[Bproj, Bcw], writes=[Bacc])
                    for k in range(1, 5):
                        P.op("dve", lambda e: e.scalar_tensor_tensor(
                            acc[:], proj[:, k:k + HT], cw[:, cc, k:k + 1], acc[:], ALU.mult, ALU.add),
                            reads=[Bproj, Bcw, Bacc], writes=[Bacc])
                    if cc < 16:
                        y, By = ys.next()
                        c_["y"], c_["By"] = y, By
                        P.op("act", lambda e: e.activation(y[:], acc[:], AF.Silu), reads=[Bacc], writes=[By])
                    else:
                        yn, Byn = ynb.next()
                        c_["yn"], c_["Byn"] = yn, Byn
                        P.op("act", lambda e: e.activation(yn[:], acc[:], AF.Silu), reads=[Bacc], writes=[Byn])

                def NM1(u):
                    hf, cc = cunits[u]
                    if cc >= 16:
                        return
                    c_ = cctx[u]
                    acc, Bacc, y, By = c_["acc"], c_["Bacc"], c_["y"], c_["By"]
                    P.op("pool", lambda e: e.tensor_tensor(acc[:], y[:], y[:], ALU.mult), reads=[By], writes=[Bacc])
                    for n in range(NN):
                        pp, Bpp = pps.next()
                        P.op("pe", lambda e: e.matmul(pp[:, 0:NW], onesf, acc[:, n * NW:(n + 1) * NW], start=True, stop=True),
                             reads=[Bc, Bacc], writes=[Bpp])
                        P.op("act", lambda e: e.activation(acc[:, n * NW:(n + 1) * NW], pp[:, 0:NW], AF.Ln, bias=L2_EPS),
                             reads=[Bpp], writes=[Bacc])

                def NM2(u):
                    hf, cc = cunits[u]
                    if cc >= 16:
                        return
                    c_ = cctx[u]
                    acc, Bacc, y, By = c_["acc"], c_["Bacc"], c_["y"], c_["By"]
                    P.op("act", lambda e: e.activation(acc[:], acc[:], AF.Exp, scale=-0.5), reads=[Bacc], writes=[Bacc])
                    yn, Byn = ynb.next()
                    c_["yn"], c_["Byn"] = yn, Byn
                    sc = float(DK ** -0.5) if cc < 8 else 1.0
                    P.op("dve", lambda e: e.scalar_tensor_tensor(yn[:], y[:], sc, acc[:], ALU.mult, ALU.mult),
                         reads=[By, Bacc], writes=[Byn])
                    dst, bd = (gq_s, B_scr["gq"]) if cc < 8 else (gk_s, B_scr["gk"])
                    ch0 = (T0 + hf * HT) // 128
                    P.dma(None, dst[ch0:ch0 + NJ, :, cc % 8, :].rearrange("j p t -> p j t"),
                          yn[:].rearrange("p (j t) -> p j t", j=NJ), reads=[Byn], writes=[bd])

                def TR(u):
                    hf, cc = cunits[u]
                    c_ = cctx.pop(u)
                    if cc < 8:
                        return
                    yn, Byn = c_["yn"], c_["Byn"]
                    pT, BpT = pTs.next()
                    for jj in range(NJ):
                        P.op("pe", lambda e: e.transpose(pT[:, jj * 128:(jj + 1) * 128], yn[:, jj * 128:(jj + 1) * 128], identb[:]),
                             reads=[Byn, Bidb], writes=[BpT])
                    tk, Btk = tks.next()
                    P.op("act", lambda e: e.activation(
                        tk[:, 0:NJ, :], pT[:, 0:NJ * 128].rearrange("p (j t) -> p j t", j=NJ), AF.Copy),
                        reads=[BpT], writes=[Btk])
                    tA = T0 + hf * HT
                    if cc < 16:
                        dd = gkt_s[tA:tA + HT, cc - 8, :].rearrange("(j p) t -> p j t", p=128)
                        bd = B_scr["gkt"]
                    else:
                        hh_, half_ = (cc - 16) // 2, (cc - 16) % 2
                        dd = gv_s[tA:tA + HT, hh_, half_ * 128:(half_ + 1) * 128].rearrange("(j p) t -> p j t", p=128)
                        bd = B_scr["gv"]
                    P.dma(None, dd, tk[:, 0:NJ, :], reads=[Btk], writes=[bd])

                nCU = len(cunits)
                stages_c = ((TR, 4), (NM2, 3), (NM1, 2), (PJ, 0), (CV, 1))
                for t in range(nCU + 4):
                    for fn_, off_ in stages_c:
                        if 0 <= t - off_ < nCU:
                            fn_(t - off_)
                psm_v = psm[:, 0:NCS * 32].rearrange("p (j c) -> p j c", j=NCS)
                for j in range(NCS):
                    tA = T0 + j * 128
                    for n in range(4):
                        pp, Bpp = pps.next()
                        for c in range(8):
                            P.op("pe", lambda e: e.matmul(
                                pp[:], hnT[:, c, 2 + j * 128:2 + (j + 1) * 128],
                                Wg[:, c, 4096 + n * 512:4096 + (n + 1) * 512], start=(c == 0), stop=(c == 7)),
                                reads=[BWg, BhnT], writes=[Bpp])
                        zo, Bzo = zos.next()
                        P.op("act", lambda e: e.activation(zo[:], pp[:], AF.Silu), reads=[Bpp], writes=[Bzo])
                        P.dma(None, z_s[tA:tA + 128, n * 512:(n + 1) * 512], zo[:], reads=[Bzo], writes=[B_scr["z"]])
                    for c in range(8):
                        P.op("pe", lambda e: e.matmul(
                            psm[:, j * 32:(j + 1) * 32], hnT[:, c, 2 + j * 128:2 + (j + 1) * 128], Wg[:, c, 6144:6176],
                            start=(c == 0), stop=(c == 7)), reads=[BWg, BhnT], writes=[Bpsm])
                bga, Bbga = bgall
                gta, Bgta = gtall
                P.op("act", lambda e: e.activation(bga[:, :, 0:16], psm_v[:, :, 0:16], AF.Sigmoid), reads=[Bpsm], writes=[Bbga])
                P.op("dve", lambda e: e.tensor_tensor(
                    gta[:].rearrange("p d j h -> p j d h"), psm_v[:, :, 16:32].rearrange("p j (d h) -> p j d h", d=2),
                    dtb[:].rearrange("p (d h) -> p d h", d=2).unsqueeze(1).to_broadcast([128, NCS, 2, 8]), ALU.add),
                     reads=[Bpsm, Bdtb], writes=[Bgta])
                P.op("act", lambda e: e.activation(gta[:], gta[:], AF.Exp), reads=[Bgta], writes=[Bgta])
                P.op("act", lambda e: e.activation(gta[:], gta[:], AF.Ln, bias=1.0), reads=[Bgta], writes=[Bgta])
                P.op("dve", lambda e: e.tensor_tensor(
                    gta[:], gta[:], nA[:].rearrange("p (d h) -> p d h", d=2).unsqueeze(2).to_broadcast([128, 2, NCS, 8]), ALU.mult),
                     reads=[Bgta, BnA], writes=[Bgta])
                pp, Bpp = pps.next()
                P.op("pe", lambda e: e.matmul(pp[:, 0:NCS * 8], trif, gta[:, 0, :, :].rearrange("p j h -> p (j h)"), start=True, stop=True),
                     reads=[Bc, Bgta], writes=[Bpp])
                P.op("pe", lambda e: e.matmul(pp[:, NCS * 8:2 * NCS * 8], trib, gta[:, 1, :, :].rearrange("p j h -> p (j h)"), start=True, stop=True),
                     reads=[Bc, Bgta], writes=[Bpp])
                P.op("dve", lambda e: e.tensor_copy(
                    bga[:, :, 16:32].rearrange("p j (d h) -> p d j h", d=2),
                    pp[:, 0:2 * NCS * 8].rearrange("p (d j h) -> p d j h", d=2, j=NCS)), reads=[Bpp], writes=[Bbga])
                P.dma(None, bg_s[T0:T0 + seg_t, :].rearrange("(j p) c -> p j c", p=128), bga[:], reads=[Bbga],
                      writes=[B_scr["bg"]])
                pr_, Bpr_ = pps.next()
                P.op("pe", lambda e: e.matmul(pr_[0:NCS * 8, 0:128], gta[:, 0, :, :].rearrange("p j h -> p (j h)"), trif, start=True, stop=True),
                     reads=[Bc, Bgta], writes=[Bpr_])
                P.op("pe", lambda e: e.matmul(pr_[0:NCS * 8, 128:256], gta[:, 1, :, :].rearrange("p j h -> p (j h)"), trib, start=True, stop=True),
                     reads=[Bc, Bgta], writes=[Bpr_])
                gra, Bgra = grall
                P.op("dve", lambda e: e.tensor_copy(
                    gra[:], pr_[0:NCS * 8, 0:256].rearrange("p (d t) -> p d t", d=2)), reads=[Bpr_], writes=[Bgra])
                for j in range(NCS):
                    tA = T0 + j * 128
                    P.dma(None, gcr_s[:, :, tA:tA + 128].rearrange("d h t -> h d t"), gra[j * 8:(j + 1) * 8, :, :],
                          reads=[Bgra], writes=[B_scr["gcr"]])
            P.flush()

        if "stopC" in debug:
            return nc

        with contextlib.ExitStack() as st:
            HB = 4
            S32 = [mk(st, "S32_%d" % d, [128, 8, 256], F32) for d in range(2)]
            Sbf = [mk(st, "Sbf_%d" % d, [128, 8, 256], BF16) for d in range(2)]
            S32B = [[Buf() for _ in range(8)] for d in range(2)]
            SbfB = [[Buf() for _ in range(8)] for d in range(2)]
            kTs = Rot([mk(st, "kTD%d" % i, [128, 8, 128], BF16) for i in range(2)])
            qTs = Rot([mk(st, "qTD%d" % i, [128, 8, 128], BF16) for i in range(2)])
            kts = Rot([mk(st, "ktD%d" % i, [128, 8, 128], BF16) for i in range(2)])
            vts = Rot([mk(st, "vtD%d" % i, [128, 8, 256], BF16) for i in range(2)])
            bgs = Rot([mk(st, "bgD%d" % i, [128, 32], F32) for i in range(3)])
            GRs = Rot([mk(st, "GRD%d" % i, [128, 8, 128], F32) for i in range(2)])
            smalls = Rot([mk(st, "smD%d" % i, [128, 4, 8], F32) for i in range(2)])
            tmps = Rot([mk(st, "tmpD%d" % i, [128, 8, 128], F32) for i in range(2)])
            DTs = Rot([mk(st, "DTD%d" % i, [128, 8, 128], F32) for i in range(1)])
            DSs = Rot([mk(st, "DSD%d" % i, [128, 8, 128], F32) for i in range(1)])
            ERs = Rot([mk(st, "ERD%d" % i, [128, 8, 128], F32) for i in range(2)])
            Aqs = Rot([mk(st, "AqD%d" % i, [128, 8, 128], BF16) for i in range(2)])
            qgs = Rot([mk(st, "qgD%d" % i, [128, 8, 128], BF16) for i in range(2)])
            rws = Rot([mk(st, "rwD%d" % i, [128, 8, 128], F32) for i in range(2)])
            kds = Rot([mk(st, "kdD%d" % i, [128, 8, 128], BF16) for i in range(2)])
            vbs = Rot([mk(st, "vbD%d" % i, [128, 8, 256], F32) for i in range(2)])
            Ab = Rot([mk(st, "AbD%d" % i, [128, HB, 128], F32) for i in range(4)])
            Nb = Rot([mk(st, "NbD%d" % i, [128, HB, 128], F32) for i in range(4)])
            ZP0 = Rot([mk(st, "ZP0D%d" % i, [128, HB, 2, 128], F32) for i in range(4)])
            ZPm = Rot([mk(st, "ZPmD%d" % i, [128, HB, 2, 128], F32) for i in range(4)])
            Yb = Rot([mk(st, "YbD%d" % i, [128, HB, 128], F32) for i in range(4)])
            P32 = Rot([mk(st, "P32D%d" % i, [128, HB, 128], F32) for i in range(4)])
            nWs = Rot([mk(st, "nWD%d" % i, [128, HB, 128], BF16) for i in range(2)])
            vns = Rot([mk(st, "vnD%d" % i, [128, 256], BF16) for i in range(4)])
            oos = Rot([mk(st, "ooD%d" % i, [128, 8, 256], F32) for i in range(2)])
            pM = [mk(st, "pMD%d" % i, [128, HB * 128], F32, psum=True) for i in range(2)]
            pZs = [mk(st, "pZD%d" % i, [128, 512], F32, psum=True) for i in range(2)]
            pYs = [mk(st, "pYD%d" % i, [128, 512], F32, psum=True) for i in range(2)]
            pPs = [mk(st, "pPD%d" % i, [128, HB * 128], F32, psum=True) for i in range(2)]

            units = []
            for (seg0, nsegs) in ((0, GRP), (GRP, 1)):
                nchg = nsegs * NCS
                chb = seg0 * NCS
                for step in range(nchg):
                    for d in range(2):
                        cl = step if d == 0 else nchg - 1 - step
                        first = (step == 0)
                        carry = (not first) and ((cl % NCS == 0) if d == 0 else (cl % NCS == NCS - 1))
                        units.append((chb + cl, d, first, carry))
            ctxs = {}
            h4 = lambda ap: ap.rearrange("p (h t) -> p h t", h=HB)

            def L_stage(u):
                ch, d, first, carry = units[u]
                c = {}
                ctxs[u] = c
                tA = ch * 128
                c["kT"], c["BkT"] = kTs.next()
                c["qT"], c["BqT"] = qTs.next()
                c["kt"], c["Bkt"] = kts.next()
                c["vt"], c["Bvt"] = vts.next()
                c["bg"], c["Bbg"] = bgs.next()
                c["GR"], c["BGR"] = GRs.next()
                P.dma(None, c["kT"][:], gk_s[ch, :, :, :], reads=[B_scr["gk"]], writes=[c["BkT"]])
                P.dma(None, c["qT"][:], gq_s[ch, :, :, :], reads=[B_scr["gq"]], writes=[c["BqT"]])
                P.dma(None, c["kt"][:], gkt_s[tA:tA + 128, :, :], reads=[B_scr["gkt"]], writes=[c["Bkt"]])
                P.dma(None, c["vt"][:], gv_s[tA:tA + 128, :, :], reads=[B_scr["gv"]], writes=[c["Bvt"]])
                P.dma(None, c["bg"][:], bg_s[tA:tA + 128, :], reads=[B_scr["bg"]], writes=[c["Bbg"]])
                for h in range(8):
                    P.dma(None, c["GR"][:, h, :], gcr_s[d, h:h + 1, tA:tA + 128].partition_broadcast(128),
                          reads=[B_scr["gcr"]], writes=[c["BGR"]])

            def Q_stage(u):
                ch, d, first, carry = units[u]
                c = ctxs[u]
                last = 127 if d == 0 else 0
                kT, BkT, qT, BqT, kt, Bkt, vt, Bvt = c["kT"], c["BkT"], c["qT"], c["BqT"], c["kt"], c["Bkt"], c["vt"], c["Bvt"]
                bg, Bbg, GR, BGR = c["bg"], c["Bbg"], c["GR"], c["BGR"]
                beta = bg[:, d * 8:(d + 1) * 8]
                gc = bg[:, 16 + d * 8:16 + (d + 1) * 8]
                sm, Bsm = smalls.next()
                P.op("act", lambda e: e.activation(sm[:, 0, :], gc, AF.Exp), reads=[Bbg], writes=[Bsm])
                P.op("dve", lambda e: e.tensor_tensor(sm[:, 0, :], sm[:, 0, :], beta, ALU.mult), reads=[Bsm, Bbg], writes=[Bsm])
                P.op("dve", lambda e: e.tensor_tensor(sm[:, 1, :], GR[:, :, last], gc, ALU.subtract),
                     reads=[BGR, Bbg], writes=[Bsm])
                P.op("act", lambda e: e.activation(sm[:, 1, :], sm[:, 1, :], AF.Exp), reads=[Bsm], writes=[Bsm])
                yield
                tmp, Btmp = tmps.next()
                DT, BDT = DTs.next()
                DS, BDS = DSs.next()
                ER, BER = ERs.next()
                c["ER"], c["BER"] = ER, BER
                for h in range(8):
                    P.op("dve", lambda e, h=h: e.scalar_tensor_tensor(
                        tmp[:, h, :], GR[:, h, :], gc[:, h:h + 1], maskD[d], ALU.subtract, ALU.add),
                        reads=[BGR, Bbg, Bc], writes=[Btmp])
                    if h % 2 == 1:
                        yield
                    if h % 4 == 3:
                        P.op("act", lambda e: e.activation(DT[:, h - 3:h + 1, :], tmp[:, h - 3:h + 1, :], AF.Exp),
                             reads=[Btmp], writes=[BDT])
                tmp2, Btmp2 = tmps.next()
                for h in range(8):
                    P.op("dve", lambda e, h=h: e.scalar_tensor_tensor(
                        tmp2[:, h, :], GR[:, h, :], gc[:, h:h + 1], maskS[d], ALU.subtract, ALU.subtract),
                        reads=[BGR, Bbg, Bc], writes=[Btmp2])
                    if h % 2 == 1:
                        yield
                    if h % 4 == 3:
                        P.op("act", lambda e: e.activation(DS[:, h - 3:h + 1, :], tmp2[:, h - 3:h + 1, :], AF.Exp, scale=-1.0),
                             reads=[Btmp2], writes=[BDS])
                for hq in range(2):
                    P.op("act", lambda e: e.activation(ER[:, hq * 4:hq * 4 + 4, :], GR[:, hq * 4:hq * 4 + 4, :], AF.Exp),
                         reads=[BGR], writes=[BER])
                    yield
                qg, Bqg = qgs.next()
                rw, Brw = rws.next()
                kd, Bkd = kds.next()
                vb, Bvb = vbs.next()
                c.update(qg=qg, Bqg=Bqg, rw=rw, Brw=Brw, kd=kd, Bkd=Bkd, vb=vb, Bvb=Bvb)
                P.op("pool", lambda e: e.tensor_tensor(qg[:], qT[:], ER[:], ALU.mult), reads=[BqT, BER], writes=[Bqg])
                P.op("pool", lambda e: e.tensor_tensor(rw[:], kt[:], sm[:, 0, :].unsqueeze(2).to_broadcast([128, 8, 128]),
                                                       ALU.mult), reads=[Bkt, Bsm], writes=[Brw])
                P.op("pool", lambda e: e.tensor_tensor(kd[:], kt[:], sm[:, 1, :].unsqueeze(2).to_broadcast([128, 8, 128]),
                                                       ALU.mult), reads=[Bkt, Bsm], writes=[Bkd])
                P.op("pool", lambda e: e.tensor_tensor(vb[:].bitcast(F32R), vt[:], beta.unsqueeze(2).to_broadcast([128, 8, 256]),
                                                       ALU.mult), reads=[Bvt, Bbg], writes=[Bvb])
                yield
                Aq, BAq = Aqs.next()
                c["Aq"], c["BAq"] = Aq, BAq
                c["hb"] = []
                for hb in range(2):
                    h0 = hb * HB
                    pk, Bpk = pM[0]
                    for hh in range(HB):
                        P.op("pe", lambda e, hh=hh: e.matmul(
                            pk[:, hh * 128:(hh + 1) * 128], kT[:, h0 + hh, :], kT[:, h0 + hh, :], start=True, stop=True),
                            reads=[BkT], writes=[Bpk])
                    A_, BA = Ab.next()
                    yield
                    for hh in range(HB):
                        P.op("dve", lambda e, hh=hh: e.scalar_tensor_tensor(
                            A_[:, hh, :], pk[:, hh * 128:(hh + 1) * 128], beta[:, h0 + hh:h0 + hh + 1], DS[:, h0 + hh, :],
                            ALU.mult, ALU.mult), reads=[Bpk, Bbg, BDS], writes=[BA])
                        if hh % 2 == 1:
                            yield
                    pq, Bpq = pM[1]
                    for hh in range(HB):
                        P.op("pe", lambda e, hh=hh: e.matmul(
                            pq[:, hh * 128:(hh + 1) * 128], kT[:, h0 + hh, :], qT[:, h0 + hh, :], start=True, stop=True),
                            reads=[BkT, BqT], writes=[Bpq])
                    yield
                    P.op("dve", lambda e: e.tensor_tensor(
                        Aq[:, h0:h0 + HB, :], h4(pq[:]), DT[:, h0:h0 + HB, :], ALU.mult), reads=[Bpq, BDT], writes=[BAq])
                    yield
                    for hh in range(HB):
                        P.op("pe", lambda e, hh=hh: e.transpose(pk[:, hh * 128:(hh + 1) * 128], A_[:, hh, :], identf),
                             reads=[BA, Bc], writes=[Bpk])
                    yield
                    N_, BN = Nb.next()
                    P.op("act", lambda e: e.activation(N_[:], h4(pk[:]), AF.Copy), reads=[Bpk], writes=[BN])
                    zp0, Bzp0 = ZP0.next()
                    P.op("dve", lambda e: e.tensor_tensor(
                        zp0[:, :, 1, :].bitcast(F32R), identf.unsqueeze(1).to_broadcast([128, HB, 128]), h4(pk[:]), ALU.subtract),
                        reads=[Bc, Bpk], writes=[Bzp0])
                    c["hb"].append(dict(A=A_, BA=BA, N=N_, BN=BN, zp=zp0, Bzp=Bzp0))
                    yield

            def M_stage(u):
                ch, d, first, carry = units[u]
                c = ctxs[u]
                tA = ch * 128
                last = 127 if d == 0 else 0
                S32t = S32[d][0]
                Sbft = Sbf[d][0]
                BS32h = S32B[d]
                BSbfh = SbfB[d]
                ER, BER = c["ER"], c["BER"]
                qg, Bqg, rw, Brw, kd, Bkd, vb, Bvb, Aq, BAq = (c["qg"], c["Bqg"], c["rw"], c["Brw"], c["kd"], c["Bkd"],
                                                              c["vb"], c["Bvb"], c["Aq"], c["BAq"])
                stt = []
                for hb in range(2):
                    x = c["hb"][hb]
                    stt.append(dict(N=x["N"], BN=x["BN"], Yc=x["A"], BYc=x["BA"], zp=x["zp"], Bzp=x["Bzp"]))
                r32 = lambda ap: ap.bitcast(F32R)
                for lvl in range(0, 7):
                    zpns = [None, None]
                    for hb in range(2):
                        s_ = stt[hb]
                        pRa, BpRa = pZs[hb]
                        pRb, BpRb = pYs[hb]
                        Yc, BYc, zp, Bzp = s_["Yc"], s_["BYc"], s_["zp"], s_["Bzp"]
                        for hh in range(HB):
                            pr, Bpr = (pRa, BpRa) if hh < 2 else (pRb, BpRb)
                            o0 = (hh % 2) * 256
                            if lvl == 0:
                                N_, BN = s_["N"], s_["BN"]
                                P.op("pe", lambda e: e.matmul(pr[:, o0:o0 + 128], Yc[:, hh, :], N_[:, hh, :],
                                                              start=True, stop=True), reads=[BYc, BN], writes=[Bpr])
                            elif lvl < 6:
                                P.op("pe", lambda e: e.matmul(
                                    pr[:, o0:o0 + 256], r32(Yc[:, hh, :]),
                                    r32(zp[:, hh, :, :].rearrange("p z t -> p (z t)")), start=True, stop=True),
                                    reads=[BYc, Bzp], writes=[Bpr])
                            else:
                                P.op("pe", lambda e: e.matmul(pr[:, o0:o0 + 128], r32(Yc[:, hh, :]), r32(zp[:, hh, 1, :]),
                                                              start=True, stop=True), reads=[BYc, Bzp], writes=[Bpr])
                    yield
                    for hb in range(2):
                        s_ = stt[hb]
                        pRa, BpRa = pZs[hb]
                        pRb, BpRb = pYs[hb]
                        zp, Bzp = s_["zp"], s_["Bzp"]
                        if lvl == 0:
                            zpn, Bzpn = zp, Bzp
                        elif lvl < 6:
                            zpn, Bzpn = ZPm.next()
                        else:
                            zpn, Bzpn = P32.next()
                        zpns[hb] = (zpn, Bzpn)
                        for half, (pr, Bpr) in enumerate(((pRa, BpRa), (pRb, BpRb))):
                            prv = pr[:].rearrange("p (h z t) -> p h z t", h=2, z=2)
                            hs = slice(2 * half, 2 * half + 2)
                            if lvl < 6:
                                P.op("act", lambda e: e.activation(zpn[:, hs, 0, :].bitcast(F32R), prv[:, :, 0, :], AF.Copy),
                                     reads=[Bpr], writes=[Bzpn])
                            if 1 <= lvl < 6:
                                P.op("dve", lambda e: e.tensor_tensor(
                                    zpn[:, hs, 1, :].bitcast(F32R), prv[:, :, 1, :], zp[:, hs, 1, :], ALU.add),
                                    reads=[Bpr, Bzp], writes=[Bzpn])
                            if lvl == 6:
                                P.op("dve", lambda e: e.tensor_tensor(
                                    zpn[:, hs, :].bitcast(F32R), prv[:, :, 0, :], zp[:, hs, 1, :], ALU.add),
                                    reads=[Bpr, Bzp], writes=[Bzpn])
                        if lvl == 6:
                            s_["Pf"], s_["BPf"] = zpn, Bzpn
                    yield
                    if lvl == 6:
                        continue
                    for hb in range(2):
                        pTr, BpTr = pPs[hb]
                        zpn, Bzpn = zpns[hb]
                        for hh in range(HB):
                            P.op("pe", lambda e: e.transpose(pTr[:, hh * 128:(hh + 1) * 128], zpn[:, hh, 0, :], identf),
                                 reads=[Bzpn, Bc], writes=[BpTr])
                    yield
                    for hb in range(2):
                        s_ = stt[hb]
                        pTr, BpTr = pPs[hb]
                        Yn, BYn = Yb.next()
                        if hb == 0:
                            P.op("dve", lambda e: e.tensor_copy(Yn[:].bitcast(F32R), h4(pTr[:])), reads=[BpTr], writes=[BYn])
                        else:
                            P.op("act", lambda e: e.activation(Yn[:].bitcast(F32R), h4(pTr[:]), AF.Copy),
                                 reads=[BpTr], writes=[BYn])
                        s_["Yc"], s_["BYc"] = Yn, BYn
                        s_["zp"], s_["Bzp"] = zpns[hb]
                    yield
                nWl = []
                for hb in range(2):
                    h0 = hb * HB
                    pZ, BpZ = pZs[hb]
                    Pf, BPf = stt[hb]["Pf"], stt[hb]["BPf"]
                    for hh in range(HB):
                        P.op("pe", lambda e, hh=hh: e.matmul(
                            pZ[:, hh * 128:(hh + 1) * 128], rw[:, h0 + hh, :], Pf[:, hh, :], start=True, stop=True),
                            reads=[Brw, BPf], writes=[BpZ])
                    nW, BnW = nWs.next()
                    P.op("act", lambda e: e.activation(nW[:], h4(pZ[:]), AF.Copy, scale=-1.0), reads=[BpZ], writes=[BnW])
                    nWl.append((nW, BnW))
                    yield
                if first:
                    P.op("pool", lambda e: e.memset(S32t[:], 0.0), writes=BS32h)
                    P.op("pool", lambda e: e.memset(Sbft[:], 0.0), writes=BSbfh)
                elif carry:
                    P.op("dve", lambda e: e.tensor_scalar(S32t[:], S32t[:], flagt[:, 0:1], None, ALU.mult),
                         reads=BS32h + [Bflag], writes=BS32h)
                    P.op("pool", lambda e: e.tensor_scalar(Sbft[:], Sbft[:], flagt[:, 0:1], None, ALU.mult),
                         reads=BSbfh + [Bflag], writes=BSbfh)
                oo, Boo = oos.next()
                for hh in range(HB):
                    for hb in range(2):
                        h = hb * HB + hh
                        Pf, BPf = stt[hb]["Pf"], stt[hb]["BPf"]
                        nW, BnW = nWl[hb]
                        pv, Bpv = pZs[hb][0][:, 0:256], pZs[hb][1]
                        po_, Bpo = pYs[hb][0][:, 0:256], pYs[hb][1]
                        ps_, Bps = pPs[hb][0][:, 0:256], pPs[hb][1]
                        P.op("pe", lambda e: e.matmul(pv, Pf[:, hh, :].bitcast(F32R), vb[:, h, :].bitcast(F32R),
                                                      start=True, stop=False), reads=[BPf, Bvb], writes=[Bpv])
                        P.op("pe", lambda e: e.matmul(pv, nW[:, hh, :], Sbft[:, h, :], start=False, stop=True),
                             reads=[BnW, BSbfh[h]], writes=[Bpv])
                        vn, Bvn = vns.next()
                        P.op("act", lambda e: e.activation(vn[:], pv, AF.Copy), reads=[Bpv], writes=[Bvn])
                        P.op("pe", lambda e: e.matmul(po_, qg[:, h, :], Sbft[:, h, :], start=True, stop=False),
                             reads=[Bqg, BSbfh[h]], writes=[Bpo])
                        P.op("pe", lambda e: e.matmul(po_, Aq[:, h, :], vn[:], start=False, stop=True),
                             reads=[BAq, Bvn], writes=[Bpo])
                        P.op("pe", lambda e: e.matmul(ps_, kd[:, h, :], vn[:], start=True, stop=True),
                             reads=[Bkd, Bvn], writes=[Bps])
                        P.op("act", lambda e: e.activation(oo[:, h, :], po_, AF.Copy), reads=[Bpo], writes=[Boo])
                        P.op("dve", lambda e: e.scalar_tensor_tensor(
                            S32t[:, h, :], S32t[:, h, :], ER[:, h, last:last + 1], ps_, ALU.mult, ALU.add),
                            reads=[BS32h[h], BER, Bps], writes=[BS32h[h]])
                        P.op("pool", lambda e: e.tensor_copy(Sbft[:, h, :], S32t[:, h, :]), reads=[BS32h[h]], writes=[BSbfh[h]])
                        yield
                P.dma(None, o_s[d][tA:tA + 128, :, :], oo[:], reads=[Boo], writes=[B_scr["of" if d == 0 else "ob"]])
                del ctxs[u]

            nU = len(units)
            for t in range(-2, nU):
                if 0 <= t + 2 < nU:
                    L_stage(t + 2)
                gens = []
                if 0 <= t < nU:
                    gens.append(M_stage(t))
                if 0 <= t + 1 < nU:
                    gens.append(Q_stage(t + 1))
                while gens:
                    for g_ in list(gens):
                        try:
                            next(g_)
                        except StopIteration:
                            gens.remove(g_)
            P.flush()

        if "stopD" in debug:
            return nc

        with contextlib.ExitStack() as st:
            Wo2, BWo2 = load_weight_bf16(st, "Wo2", g_w_out, 16, D, stage_cols=1024)
            nwb, Bnwb = mk(st, "nwE", [128, DV], F32)
            P.dma("sp", nwb[:], g_nw[0:1, :].partition_broadcast(128), writes=[Bnwb])
            fwb, Bfwb = mk(st, "fwE", [128, D], F32)
            P.dma("sp", fwb[:], fin_w[0:1, :].partition_broadcast(128), writes=[Bfwb])
            ofs = Rot([mk(st, "ofE%d" % i, [128, 8, 256], F32) for i in range(4)])
            obs = Rot([mk(st, "obE%d" % i, [128, 8, 256], F32) for i in range(3)])
            zts = Rot([mk(st, "zE%d" % i, [128, 8, 256], BF16) for i in range(4)])
            h1s = Rot([mk(st, "h1E%d" % i, [128, D], F32) for i in range(3)])
            jk, Bjk = mk(st, "jkE", [128, 256], BF16)
            jk2, Bjk2 = mk(st, "jk2E", [128, D], BF16)
            sss = Rot([mk(st, "ssE%d" % i, [128, 16], F32) for i in range(4)])
            ons = Rot([mk(st, "onE%d" % i, [128, 8, 256], BF16) for i in range(3)])
            onTs = Rot([mk(st, "onTE%d" % i, [128, 16, 128], BF16) for i in range(3)])
            h2s = Rot([mk(st, "h2E%d" % i, [128, D], F32) for i in range(3)])
            yos = Rot([mk(st, "yoE%d" % i, [128, D], F32) for i in range(2)])
            s2s = Rot([mk(st, "s2E%d" % i, [128, 2], F32) for i in range(2)])
            pTs = Rot([mk(st, "pTE%d" % i, [128, 1024], BF16, psum=True) for i in range(4)])
            pos = Rot([mk(st, "poE%d" % i, [128, 512], F32, psum=True) for i in range(4)])
            ectx = {}

            def E0(j):
                tA = j * 128
                c_ = {}
                ectx[j] = c_
                of, Bof = ofs.next()
                ob, Bob = obs.next()
                zt, Bzt = zts.next()
                c_.update(of=of, Bof=Bof, ob=ob, Bob=Bob, zt=zt, Bzt=Bzt)
                P.dma(None, of[:], o_s[0][tA:tA + 128, :, :], reads=[B_scr["of"]], writes=[Bof])
                P.dma(None, ob[:], o_s[1][tA:tA + 128, :, :], reads=[B_scr["ob"]], writes=[Bob])
                P.dma(None, zt[:], z_s[tA:tA + 128, :].rearrange("p (h d) -> p h d", h=8), reads=[B_scr["z"]], writes=[Bzt])

            def E1(j):
                c_ = ectx[j]
                of, Bof, ob, Bob = c_["of"], c_["Bof"], c_["ob"], c_["Bob"]
                P.op("pool", lambda e: e.tensor_tensor(of[:], of[:], ob[:], ALU.add), reads=[Bof, Bob], writes=[Bof])
                ss, Bss = sss.next()
                c_["ss"], c_["Bss"] = ss, Bss
                for h in range(8):
                    P.op("act", lambda e: e.activation(
                        jk[:], of[:, h, :], AF.Square, scale=1.0 / 16.0, accum_out=ss[:, h:h + 1]),
                        reads=[Bof], writes=[Bjk, Bss])
                P.op("act", lambda e: e.activation(ss[:, 8:16], ss[:, 0:8], AF.Sqrt, bias=RMS_EPS), reads=[Bss], writes=[Bss])
                P.op("dve", lambda e: e.reciprocal(ss[:, 8:16], ss[:, 8:16]), reads=[Bss], writes=[Bss])

            def E2(j):
                c_ = ectx[j]
                of, Bof, zt, Bzt, ss, Bss = c_["of"], c_["Bof"], c_["zt"], c_["Bzt"], c_["ss"], c_["Bss"]
                on, Bon = ons.next()
                c_["on"], c_["Bon"] = on, Bon
                for h in range(8):
                    P.op("dve", lambda e: e.scalar_tensor_tensor(
                        of[:, h, :], of[:, h, :], ss[:, 8 + h:9 + h], nwb[:], ALU.mult, ALU.mult),
                        reads=[Bof, Bss, Bnwb], writes=[Bof])
                P.op("pool", lambda e: e.tensor_tensor(on[:], of[:], zt[:], ALU.mult), reads=[Bof, Bzt], writes=[Bon])

            def E3(j):
                c_ = ectx[j]
                on, Bon = c_["on"], c_["Bon"]
                tA = j * 128
                h1, Bh1 = h1s.next()
                c_["h1"], c_["Bh1"] = h1, Bh1
                P.dma(None, h1[:], h1_s[tA:tA + 128, :], reads=[B_scr["h1"]], writes=[Bh1])
                onT, BonT = onTs.next()
                c_["onT"], c_["BonT"] = onT, BonT
                for half in range(2):
                    pT, BpT = pTs.next()
                    for c in range(8):
                        cc = half * 8 + c
                        P.op("pe", lambda e: e.transpose(
                            pT[:, c * 128:(c + 1) * 128], on[:, cc // 2, (cc % 2) * 128:(cc % 2 + 1) * 128], identb[:]),
                            reads=[Bon, Bidb], writes=[BpT])
                    P.op("act", lambda e: e.activation(
                        onT[:, half * 8:(half + 1) * 8, :], pT[:].rearrange("p (c t) -> p c t", c=8), AF.Copy),
                        reads=[BpT], writes=[BonT])

            def E4(j):
                c_ = ectx[j]
                onT, BonT, h1, Bh1 = c_["onT"], c_["BonT"], c_["h1"], c_["Bh1"]
                h2, Bh2 = h2s.next()
                c_["h2"], c_["Bh2"] = h2, Bh2
                for half in range(2):
                    po_, Bpo = pos.next()
                    for c in range(16):
                        P.op("pe", lambda e: e.matmul(
                            po_[:], onT[:, c, :], Wo2[:, c, half * 512:(half + 1) * 512], start=(c == 0), stop=(c == 15)),
                            reads=[BonT, BWo2], writes=[Bpo])
                    P.op("dve", lambda e: e.tensor_tensor(
                        h2[:, half * 512:(half + 1) * 512], po_[:], h1[:, half * 512:(half + 1) * 512], ALU.add),
                        reads=[Bpo, Bh1], writes=[Bh2])

            def E5(j):
                c_ = ectx.pop(j)
                h2, Bh2 = c_["h2"], c_["Bh2"]
                tA = j * 128
                yo, Byo = yos.next()
                s2, Bs2 = s2s.next()
                rmsnorm_rows(jk2, h2[:, :], 128, fwb, yo[:, :], s2[:, 0:1], s2[:, 1:2], [Bh2], Bs2, Bjk2, Byo, Bfwb)
                P.dma(None, y_out[tA:tA + 128, :], yo[:], reads=[Byo], writes=[B_x])

            nE = NT // 128
            for t in range(nE + 5):
                for fn_, off_ in ((E5, 5), (E4, 4), (E3, 3), (E2, 2), (E1, 1), (E0, 0)):
                    if 0 <= t - off_ < nE:
                        fn_(t - off_)
            P.flush()
    return nc


def make_consts():
    c = np.zeros((128, 8, 128), np.float32)
    p = np.arange(128)[:, None]
    f = np.arange(128)[None, :]
    c[:, 0, :] = (p == f)
    c[:, 1, :] = 1.0
    c[:, 2, :] = (p <= f)
    c[:, 3, :] = (p >= f)
    c[:, 4, :] = np.where(f >= p, 0.0, -BIG)
    c[:, 5, :] = np.where(f <= p, 0.0, -BIG)
    c[:, 6, :] = np.where(f < p, 0.0, -BIG)
    c[:, 7, :] = np.where(f > p, 0.0, -BIG)
    return c


def shared_inputs(ln_w, na_w_in, na_w_out, gdn_w_in, gdn_conv_w, gdn_a_log, gdn_dt_bias, gdn_norm_w, gdn_w_out,
                  final_norm_w):
    f = lambda a: np.ascontiguousarray(np.asarray(a, dtype=np.float32))
    return {
        "ln_w_bc": f(ln_w),
        "fin_w": f(final_norm_w).reshape(1, D),
        "na_w_in": f(na_w_in[0]),
        "na_w_out": f(na_w_out[0]),
        "g_w_in": f(gdn_w_in[0]),
        "g_cw": f(np.asarray(gdn_conv_w[0]).T.reshape(32, 128, 5).transpose(1, 0, 2)),
        "g_alog": f(gdn_a_log[0]).reshape(1, 16),
        "g_dtb": f(gdn_dt_bias[0]).reshape(1, 16),
        "g_nw": f(gdn_norm_w[0]).reshape(1, DV),
        "g_w_out": f(gdn_w_out[0]),
        "consts": make_consts(),
    }


_PROG_CACHE = {}


def kernel(x_prompt, x_sample, ln_w, na_w_in, na_rpb, na_w_out, gdn_w_in, gdn_conv_w, gdn_a_log, gdn_dt_bias,
           gdn_norm_w, gdn_w_out, final_norm_w):
    seg_t = 2048
    x_prompt = np.asarray(x_prompt, np.float32)
    x_sample = np.asarray(x_sample, np.float32)
    shared = shared_inputs(ln_w, na_w_in, na_w_out, gdn_w_in, gdn_conv_w, gdn_a_log, gdn_dt_bias, gdn_norm_w,
                           gdn_w_out, final_norm_w)
    rpb = np.asarray(na_rpb, np.float32)[0]
    tabs_j = build_na_bias_tables(rpb, seg_t, True).reshape(-1, 128, NH_A, 768)
    tabs_s = build_na_bias_tables(rpb, seg_t, False).reshape(-1, 128, NH_A, 768)
    in_maps = []
    for c in range(8):
        if c < 2:
            xc = np.concatenate([x_prompt[c], x_sample[c]], axis=0)
            m = dict(shared, x=np.ascontiguousarray(xc), na_bias=tabs_j, flag=np.ones((128, 1), np.float32))
        else:
            s0 = 2 + 5 * (c - 2)
            xc = x_sample[s0:s0 + 5].reshape(5 * seg_t, D)
            m = dict(shared, x=np.ascontiguousarray(xc), na_bias=tabs_s, flag=np.zeros((128, 1), np.float32))
        in_maps.append(m)
    if seg_t not in _PROG_CACHE:
        _PROG_CACHE[seg_t] = build_program(seg_t)
    nc = _PROG_CACHE[seg_t]
    res = run_bass_kernel_spmd(nc, in_maps, core_ids=list(range(8)))
    y_prompt = np.empty_like(x_prompt)
    y_sample = np.empty_like(x_sample)
    for c in range(8):
        y = res.results[c]["y"]
        if c < 2:
            y_prompt[c] = y[0:4 * seg_t]
            y_sample[c] = y[4 * seg_t:]
        else:
            s0 = 2 + 5 * (c - 2)
            y_sample[s0:s0 + 5] = y.reshape(5, seg_t, D)
    return (y_prompt, y_sample)
```

```python
import contextlib
import numpy as np
import concourse.bass as bass
import concourse.mybir as mybir
from concourse.bass_utils import run_bass_kernel_spmd

F32 = mybir.dt.float32
BF16 = mybir.dt.bfloat16
F32R = mybir.dt.float32r
ALU = mybir.AluOpType
AF = mybir.ActivationFunctionType

D = 1024
NH_A = 16
GW = 64
NH_G = 8
DK = 128
DV = 256
NSEG = 5
GRP = 4
BIG = 1.0e5
RMS_EPS = 1e-6
L2_EPS = 1e-6

NSLOT = 32
DMA_QS = ("sp", "pool")


class Buf:
    __slots__ = ("w", "r")

    def __init__(self):
        self.w = {}
        self.r = {}


class PBuf(Buf):
    __slots__ = ("bank",)

    def __init__(self):
        Buf.__init__(self)
        self.bank = Buf()


class _Rec:
    def __getattr__(self, name):
        def f(*a, **k):
            self.call = (name, a, k)
            return self
        return f


class Prog:
    def __init__(self, nc):
        self.nc = nc
        self.streams = {e: [] for e in ("pe", "act", "dve", "pool", "sp")}
        self.count = {e: 0 for e in self.streams}
        self.waited = {e: {} for e in self.streams}
        self.dma_n = {q: 0 for q in DMA_QS}
        self.semh = {}
        self.ninst = 0
        self.rr = 0
        self.only_sp = False

    def _deps(self, eng, reads, writes):
        deps = {}
        for b in reads:
            for k, v in b.w.items():
                if deps.get(k, 0) < v:
                    deps[k] = v
        for b in writes:
            for k, v in b.w.items():
                if deps.get(k, 0) < v:
                    deps[k] = v
            for k, v in b.r.items():
                if deps.get(k, 0) < v:
                    deps[k] = v
        for b in list(reads) + list(writes):
            if isinstance(b, PBuf):
                for k, v in b.bank.w.items():
                    if k != eng and deps.get(k, 0) < v:
                        deps[k] = v
        wt = self.waited[eng]
        out = []
        for k, v in deps.items():
            if k == "pe" and eng == "pe":
                continue
            if wt.get(k, 0) >= v:
                continue
            wt[k] = v
            out.append((k, v))
        return out

    def op(self, eng, fn, reads=(), writes=()):
        rec = _Rec()
        fn(rec)
        cname, cargs, ckw = rec.call

        def fn(e, cname=cname, cargs=cargs, ckw=ckw):
            return getattr(e, cname)(*cargs, **ckw)
        waits = self._deps(eng, reads, writes)
        self.count[eng] += 1
        n = self.count[eng]
        self.streams[eng].append((waits, fn, (eng, 1)))
        for b in reads:
            b.r[eng] = n
            if isinstance(b, PBuf):
                b.bank.w[eng] = n
        for b in writes:
            b.w[eng] = n
            b.r = {}
            if isinstance(b, PBuf):
                b.bank.w[eng] = n
        self.ninst += 1

    def dma(self, q, out, in_, reads=(), writes=()):
        if q is None:
            q = "sp"
            self.rr += 1
        waits = self._deps(q, reads, writes)
        n = self.dma_n[q]
        self.dma_n[q] += 1
        slot = n % NSLOT
        val = 16 * (n // NSLOT + 1)
        key = "d_%s_%d" % (q, slot)
        if n >= NSLOT and self.waited[q].get(key, 0) < val - 16:
            self.waited[q][key] = val - 16
            waits.append((key, val - 16))

        def fn(e, out=out, in_=in_):
            return e.dma_start(out=out, in_=in_)
        self.streams[q].append((waits, fn, (key, 16)))
        for b in reads:
            b.r[key] = val
        for b in writes:
            b.w[key] = val
            b.r = {}
        self.ninst += 1

    def alloc_sems(self, st):
        keys = ["pe", "act", "dve", "pool"]
        for q in DMA_QS:
            for s_ in range(NSLOT):
                keys.append("d_%s_%d" % (q, s_))
        for k in keys:
            self.semh[k] = st.enter_context(self.nc.semaphore("s_" + k))

    def _all_events(self):
        fin = []
        for e in ("pe", "act", "dve", "pool"):
            if self.count[e]:
                fin.append((e, self.count[e]))
        for q in DMA_QS:
            n = self.dma_n[q]
            for s_ in range(NSLOT):
                cnt = (n - s_ + NSLOT - 1) // NSLOT if n > s_ else 0
                if cnt:
                    fin.append(("d_%s_%d" % (q, s_), 16 * cnt))
        return fin

    def flush(self):
        nc = self.nc
        fin = self._all_events()
        semh = self.semh
        streams = self.streams
        self.streams = {e: [] for e in streams}
        with nc.Block() as block:
            def run(engobj, name):
                for waits, fn, (ik, iv) in streams[name]:
                    for k, v in waits:
                        engobj.wait_ge(semh[k], v)
                    ins = fn(engobj)
                    ins.then_inc(semh[ik], iv)
                wt = self.waited[name]
                for k, v in fin:
                    if wt.get(k, 0) >= v:
                        continue
                    wt[k] = v
                    engobj.wait_ge(semh[k], v)

            @block.sync
            def _(e):
                run(e, "sp")

            @block.tensor
            def _(e):
                run(e, "pe")

            @block.scalar
            def _(e):
                run(e, "act")

            @block.vector
            def _(e):
                run(e, "dve")

            @block.gpsimd
            def _(e):
                run(e, "pool")


class Rot:
    def __init__(self, items):
        self.items = items
        self.i = 0

    def next(self):
        it = self.items[self.i % len(self.items)]
        self.i += 1
        return it


def na_chunk_range(i, nb, nbs):
    p = i % nbs
    lo, hi = i - 2, i + 2
    if p == 0:
        hi = i + 3
    if p == nbs - 1:
        lo = i - 3
    return max(lo, 0), min(hi, nb - 1)


def na_special_pos(p, nbs):
    sp = sorted(set([0, 1, nbs - 2, nbs - 1]))
    if nbs <= 4:
        sp = list(range(nbs))
    return sp.index(p) if p in sp else None


def na_nspecial(nbs):
    return min(nbs, 4)


def na_table_id(seg, p, nbs):
    spi = na_special_pos(p, nbs)
    if spi is None:
        return 0
    nsp = na_nspecial(nbs)
    if seg == GRP:
        kind = 2
    else:
        top = p < nbs // 2 if nbs > 2 else (p == 0)
        if top:
            kind = 0 if seg == 0 else 1
        else:
            kind = 0 if seg == GRP - 1 else 1
    return 1 + kind * nsp + spi


def build_na_bias_tables(rpb, seg_t, joined):
    rs_rows = seg_t // GW
    nbs = rs_rows // 2
    nsp = na_nspecial(nbs)
    ntab = 1 + 3 * nsp
    tabs = np.full((ntab, 128, NH_A, 6, 128), -BIG, dtype=np.float32)
    done = set()

    def fill(tid, i, nb, seq_of_row):
        if tid in done:
            return
        done.add(tid)
        lo, hi = na_chunk_range(i, nb, nbs)
        a = np.arange(2)[:, None, None, None]
        kc = np.arange(GW)[None, :, None, None]
        b = np.arange(2)[None, None, :, None]
        qc = np.arange(GW)[None, None, None, :]
        for s, m in enumerate(range(lo, hi + 1)):
            kr = 2 * m + a
            qr = 2 * i + b
            s_lo_q, s_r_q = seq_of_row(qr)
            s_lo_k, _ = seq_of_row(kr)
            rs = np.clip(qr - s_lo_q - 4, 0, s_r_q - 8)
            okr = (s_lo_k == s_lo_q) & (kr - s_lo_q >= rs) & (kr - s_lo_q < rs + 8)
            ws = np.clip(qc - 8, 0, GW - 16)
            okc = (kc >= ws) & (kc < ws + 16)
            ok = np.broadcast_to(okr & okc, (2, GW, 2, GW))
            dr = np.broadcast_to(np.clip(kr - qr + 7, 0, 14), (2, GW, 2, GW))
            dc = np.broadcast_to(np.clip(kc - qc + 15, 0, 30), (2, GW, 2, GW))
            vals = rpb[:, dr, dc]
            vals = np.where(ok[None], vals, np.float32(-BIG)).astype(np.float32)
            tabs[tid, :, :, s, :] = vals.reshape(NH_A, 128, 128).transpose(1, 0, 2)

    nbg = GRP * nbs

    def seq_group(row):
        if joined:
            return np.zeros_like(row), np.full_like(row, GRP * rs_rows)
        return (row // rs_rows) * rs_rows, np.full_like(row, rs_rows)

    def seq_single(row):
        return np.zeros_like(row), np.full_like(row, rs_rows)

    for seg in range(GRP):
        for p in range(nbs):
            tid = na_table_id(seg, p, nbs)
            fill(tid, seg * nbs + p, nbg, seq_group)
    for p in range(nbs):
        fill(na_table_id(GRP, p, nbs), p, nbs, seq_single)
    return tabs


def build_program(seg_t, debug=()):
    NT = NSEG * seg_t
    NCH = NT // 128
    NCS = seg_t // 128
    NBS = NCS
    nsp = na_nspecial(NBS)
    NTAB = 1 + 3 * nsp
    assert seg_t % 512 == 0

    nc = bass.Bass("TRN2", target_bir_lowering=False)

    def din(name, shape, dt=F32):
        return nc.dram_tensor(name, list(shape), dt, kind="ExternalInput").ap()

    def dscr(name, shape, dt):
        kind = "ExternalOutput" if name in debug else "Internal"
        return nc.dram_tensor(name, list(shape), dt, kind=kind).ap()

    x_in = din("x", [NT, D])
    ln_w_bc = din("ln_w_bc", [2, D])
    fin_w = din("fin_w", [1, D])
    na_w_in = din("na_w_in", [D, 4 * D])
    na_bias = din("na_bias", [NTAB, 128, NH_A, 6 * 128])
    na_w_out = din("na_w_out", [D, D])
    g_w_in = din("g_w_in", [D, 6176])
    g_cw = din("g_cw", [128, 32, 5])
    g_alog = din("g_alog", [1, 16])
    g_dtb = din("g_dtb", [1, 16])
    g_nw = din("g_nw", [1, DV])
    g_w_out = din("g_w_out", [2 * D, D])
    consts = din("consts", [128, 8, 128])
    flag_in = din("flag", [128, 1])
    y_out = nc.dram_tensor("y", [NT, D], F32, kind="ExternalOutput").ap()

    qT_s = dscr("qT_s", [NCH, 128, 8, 128], BF16)
    kT_s = dscr("kT_s", [NCH, 128, 8, 128], BF16)
    v_s = dscr("v_s", [NT, D], BF16)
    sg_s = dscr("sg_s", [NT, D], BF16)
    eb_s = dscr("eb_s", [NTAB, 128, NH_A, 768], BF16)
    h1_s = dscr("h1_s", [NT, D], F32)
    gq_s = dscr("gq_s", [NCH, 128, 8, 128], BF16)
    gk_s = dscr("gk_s", [NCH, 128, 8, 128], BF16)
    gkt_s = dscr("gkt_s", [NT, 8, 128], BF16)
    gv_s = dscr("gv_s", [NT, 8, 256], BF16)
    z_s = dscr("z_s", [NT, 2 * D], BF16)
    bg_s = dscr("bg_s", [NT, 32], F32)
    gcr_s = dscr("gcr_s", [2, 8, NT], F32)
    o_s = [dscr("of_s", [NT, 8, 256], F32), dscr("ob_s", [NT, 8, 256], F32)]

    P = Prog(nc)
    B_x = Buf()
    B_scr = {n: Buf() for n in ("qT", "kT", "v", "sg", "eb", "h1", "gq", "gk", "gkt", "gv", "z", "bg", "gcr",
                                "of", "ob")}

    with contextlib.ExitStack() as st0:
        P.alloc_sems(st0)

        def mk(st, name, shape, dt, psum=False):
            if psum:
                esz = 2 if dt == BF16 else 4
                per_bank = 2048 // esz
                assert len(shape) == 2
                ncols = ((shape[1] + per_bank - 1) // per_bank) * per_bank
                t_ = st.enter_context(nc.psum_tensor(name, [shape[0], ncols], dt))
                return t_[:, 0:shape[1]], PBuf()
            return st.enter_context(nc.sbuf_tensor(name, list(shape), dt)), Buf()

        cst, Bc = mk(st0, "cst", [128, 8, 128], F32)
        identb, Bidb = mk(st0, "identb", [128, 128], BF16)
        flagt, Bflag = mk(st0, "flagt", [128, 1], F32)
        P.dma("sp", cst[:], consts[:, :, :], writes=[Bc])
        P.dma("sp", flagt[:], flag_in[:, :], writes=[Bflag])
        P.op("dve", lambda e: e.tensor_copy(identb[:], cst[:, 0, :]), reads=[Bc], writes=[Bidb])
        identf = cst[:, 0, :]
        onesf = cst[:, 1, :]
        trif = cst[:, 2, :]
        trib = cst[:, 3, :]
        maskD = [cst[:, 4, :], cst[:, 5, :]]
        maskS = [cst[:, 6, :], cst[:, 7, :]]

        def load_weight_bf16(st, name, src, kchunks, ncols, stage_cols=2048):
            wt, Bw = mk(st, name, [128, kchunks, ncols], BF16)
            engs = ["act", "dve", "pool"]
            k = 0
            with contextlib.ExitStack() as stl:
                stg = [mk(stl, "%s_stg%d" % (name, i), [128, stage_cols], F32) for i in range(2)]
                it = 0
                for c in range(kchunks):
                    for c0 in range(0, ncols, stage_cols):
                        w_ = min(stage_cols, ncols - c0)
                        s_t, s_b = stg[it % 2]
                        it += 1
                        P.dma(None, s_t[:, 0:w_], src[c * 128:(c + 1) * 128, c0:c0 + w_], writes=[s_b])
                        en = engs[k % 3]
                        k += 1
                        if en == "act":
                            P.op("act", lambda e, s_t=s_t, c=c, c0=c0, w_=w_: e.activation(
                                wt[:, c, c0:c0 + w_], s_t[:, 0:w_], AF.Copy), reads=[s_b], writes=[Bw])
                        else:
                            P.op(en, lambda e, s_t=s_t, c=c, c0=c0, w_=w_: e.tensor_copy(
                                wt[:, c, c0:c0 + w_], s_t[:, 0:w_]), reads=[s_b], writes=[Bw])
                P.flush()
            return wt, Bw

        def rmsnorm_rows(jk, xt_ap, np_, lnw_bc, xn_ap, ss, rs, reads, Bss, Bjk, Bxn, Blnw, flag=None):
            P.op("act", lambda e: e.activation(jk[0:np_, :], xt_ap, AF.Square, scale=1.0 / 32.0,
                                               accum_out=ss[0:np_, 0:1]), reads=reads, writes=[Bjk, Bss])
            P.op("act", lambda e: e.activation(rs[0:np_, 0:1], ss[0:np_, 0:1], AF.Sqrt, bias=RMS_EPS),
                 reads=[Bss], writes=[Bss])
            P.op("dve", lambda e: e.reciprocal(rs[0:np_, 0:1], rs[0:np_, 0:1]), reads=[Bss], writes=[Bss])
            if flag is not None:
                P.op("dve", lambda e: e.tensor_tensor(rs[0:np_, 0:1], rs[0:np_, 0:1], flag[0:np_, 0:1], ALU.mult),
                     reads=[Bss, Bflag], writes=[Bss])
            P.op("dve", lambda e: e.scalar_tensor_tensor(xn_ap, xt_ap, rs[0:np_, 0:1], lnw_bc[0:np_, :],
                                                         ALU.mult, ALU.mult),
                 reads=list(reads) + [Bss, Blnw], writes=[Bxn])

        P.flush()

        with contextlib.ExitStack() as st:
            stg = [mk(st, "eb_stg%d" % i, [128, 4, 768], F32) for i in range(2)]
            ebo = [mk(st, "eb_o%d" % i, [128, 4, 768], BF16) for i in range(2)]
            it = 0
            for t in range(NTAB):
                for hq in range(4):
                    s_t, s_b = stg[it % 2]
                    o_t, o_b = ebo[it % 2]
                    it += 1
                    P.dma(None, s_t[:], na_bias[t, :, hq * 4:(hq + 1) * 4, :], writes=[s_b])
                    P.op("act", lambda e, s_t=s_t, o_t=o_t: e.activation(o_t[:], s_t[:], AF.Exp),
                         reads=[s_b], writes=[o_b])
                    P.dma(None, eb_s[t, :, hq * 4:(hq + 1) * 4, :], o_t[:], reads=[o_b], writes=[B_scr["eb"]])
            P.flush()

        with contextlib.ExitStack() as st:
            Wa, BWa = load_weight_bf16(st, "Wa", na_w_in, 8, 4 * D)
            lnw, Blnw = mk(st, "lnwA", [128, D], F32)
            P.dma("sp", lnw[:], ln_w_bc[0:1, :].partition_broadcast(128), writes=[Blnw])
            xts = Rot([mk(st, "xtA%d" % i, [128, 4, D], F32) for i in range(2)])
            xn, Bxn = mk(st, "xnA", [128, 4, D], BF16)
            jk, Bjk = mk(st, "jkA", [128, D], BF16)
            sss = Rot([mk(st, "ssA%d" % i, [128, 2], F32) for i in range(4)])
            hnTs = Rot([mk(st, "hnTA%d" % i, [128, 8, 512], BF16) for i in range(2)])
            pTs = Rot([mk(st, "pTA%d" % i, [128, 512], BF16, psum=True) for i in range(2)])
            pqs = Rot([mk(st, "pqA%d" % i, [128, 512], F32, psum=True) for i in range(4)])
            qos = Rot([mk(st, "qoA%d" % i, [128, 512], BF16) for i in range(4)])
            for g in range(NT // 512):
                t0 = g * 512
                xt, Bxt = xts.next()
                P.dma(None, xt[:], x_in[t0:t0 + 512, :].rearrange("(j p) d -> p j d", p=128), reads=[B_x],
                      writes=[Bxt])
                for j in range(4):
                    ss, Bss = sss.next()
                    rmsnorm_rows(jk, xt[:, j, :], 128, lnw, xn[:, j, :], ss[:, 0:1], ss[:, 1:2], [Bxt], Bss,
                                 Bjk, Bxn, Blnw)
                hnT, BhnT = hnTs.next()
                for c in range(8):
                    pT, BpT = pTs.next()
                    for j in range(4):
                        P.op("pe", lambda e, pT=pT, j=j, c=c: e.transpose(
                            pT[:, j * 128:(j + 1) * 128], xn[:, j, c * 128:(c + 1) * 128], identb[:]),
                            reads=[Bxn, Bidb], writes=[BpT])
                    en = "dve" if c % 2 == 0 else "pool"
                    if en == "dve":
                        P.op("dve", lambda e, pT=pT, hnT=hnT, c=c: e.tensor_copy(hnT[:, c, :], pT[:]),
                             reads=[BpT], writes=[BhnT])
                    else:
                        P.op("act", lambda e, pT=pT, hnT=hnT, c=c: e.activation(hnT[:, c, :], pT[:], AF.Copy),
                             reads=[BpT], writes=[BhnT])
                for fo in range(16):
                    pq, Bpq = pqs.next()
                    for c in range(8):
                        P.op("pe", lambda e, pq=pq, hnT=hnT, c=c, fo=fo: e.matmul(
                            pq[:], Wa[:, c, fo * 128:(fo + 1) * 128], hnT[:, c, :], start=(c == 0), stop=(c == 7)),
                            reads=[BWa, BhnT], writes=[Bpq])
                    qo, Bqo = qos.next()
                    sc = 0.125 if fo < 8 else 1.0
                    P.op("act", lambda e, pq=pq, qo=qo, sc=sc: e.activation(qo[:], pq[:], AF.Copy, scale=sc),
                         reads=[Bpq], writes=[Bqo])
                    dst = qT_s if fo < 8 else kT_s
                    bd = B_scr["qT"] if fo < 8 else B_scr["kT"]
                    ch0 = t0 // 128
                    P.dma(None, dst[ch0:ch0 + 4, :, fo % 8, :].rearrange("j p t -> p j t"),
                          qo[:].rearrange("p (j t) -> p j t", j=4), reads=[Bqo], writes=[bd])
                for j in range(4):
                    for fg in range(4):
                        pq, Bpq = pqs.next()
                        for c in range(8):
                            P.op("pe", lambda e, pq=pq, hnT=hnT, c=c, fg=fg, j=j: e.matmul(
                                pq[:], hnT[:, c, j * 128:(j + 1) * 128],
                                Wa[:, c, 2048 + fg * 512:2048 + (fg + 1) * 512], start=(c == 0), stop=(c == 7)),
                                reads=[BWa, BhnT], writes=[Bpq])
                        qo, Bqo = qos.next()
                        if fg < 2:
                            P.op("dve", lambda e, pq=pq, qo=qo: e.tensor_copy(qo[:], pq[:]), reads=[Bpq], writes=[Bqo])
                            P.dma(None, v_s[t0 + j * 128:t0 + (j + 1) * 128, fg * 512:(fg + 1) * 512], qo[:],
                                  reads=[Bqo], writes=[B_scr["v"]])
                        else:
                            P.op("act", lambda e, pq=pq, qo=qo: e.activation(qo[:], pq[:], AF.Silu),
                                 reads=[Bpq], writes=[Bqo])
                            P.dma(None, sg_s[t0 + j * 128:t0 + (j + 1) * 128, (fg - 2) * 512:(fg - 1) * 512], qo[:],
                                  reads=[Bqo], writes=[B_scr["sg"]])
            P.flush()

        if "stopA" in debug:
            return nc

        with contextlib.ExitStack() as st:
            P.only_sp = "onlysp" in debug
            Wo, BWo = load_weight_bf16(st, "Wo", na_w_out, 8, D, stage_cols=1024)
            ebg, Bebg = mk(st, "ebg", [128, NH_A, 768], BF16)
            P.dma("sp", ebg[:], eb_s[0, :, :, :], reads=[B_scr["eb"]], writes=[Bebg])
            ebsp = Rot([mk(st, "ebsp%d" % i, [128, NH_A, 768], BF16) for i in range(2)])
            NR = 8
            kring = [mk(st, "kring%d" % i, [128, 8, 128], BF16) for i in range(NR)]
            vring = [mk(st, "vring%d" % i, [128, NH_A, 128], BF16) for i in range(NR)]
            for i in range(NR):
                P.op("pool", lambda e, i=i: e.memset(vring[i][0][:, :, 64:128], 1.0), writes=[vring[i][1]])
            qts = Rot([mk(st, "qtB%d" % i, [128, 8, 128], BF16) for i in range(2)])
            sgs = Rot([mk(st, "sgB%d" % i, [128, D], BF16) for i in range(2)])
            xbs = Rot([mk(st, "xB%d" % i, [128, D], F32) for i in range(3)])
            Es = Rot([mk(st, "EB%d" % i, [128, 768], BF16) for i in range(4)])
            Pms = Rot([mk(st, "PmB%d" % i, [128, 768], BF16) for i in range(4)])
            ogs = Rot([mk(st, "ogB%d" % i, [128, D], BF16) for i in range(2)])
            ogTs = Rot([mk(st, "ogTB%d" % i, [128, 8, 128], BF16) for i in range(2)])
            h1os = Rot([mk(st, "h1oB%d" % i, [128, D], F32) for i in range(2)])
            rcs = Rot([mk(st, "rcB%d" % i, [128, 1], F32) for i in range(4)])
            onrm = Rot([mk(st, "onB%d" % i, [128, 64], BF16) for i in range(4)])
            psts = Rot([mk(st, "pstB%d" % i, [128, 1024], F32, psum=True) for i in range(3)])
            pOs = Rot([mk(st, "pOB%d" % i, [128, 128], F32, psum=True) for i in range(1)])
            po_t, Bpo_t = mk(st, "poB", [128, 512], F32, psum=True)
            pT2, BpT2 = po_t.bitcast(BF16), Bpo_t
            pos = Rot([(po_t, Bpo_t)])

            bunits = []
            for (seg0, nsegs) in ((0, GRP), (GRP, 1)):
                nb = nsegs * NBS
                for i in range(nb):
                    bunits.append((seg0, nb, i))
            bctx = {}
            loaded = set()

            def B0(u):
                seg0, nb, i = bunits[u]
                c_ = {}
                bctx[u] = c_
                tokbase = seg0 * seg_t
                chbase = tokbase // 128
                seg = seg0 + i // NBS
                p = i % NBS
                lo, hi = na_chunk_range(i, nb, NBS)
                tid = na_table_id(seg, p, NBS)
                for m in range(lo, hi + 1):
                    if (chbase + m) in loaded:
                        continue
                    loaded.add(chbase + m)
                    kt, Bkt = kring[(chbase + m) % NR]
                    vt, Bvt = vring[(chbase + m) % NR]
                    P.dma(None, kt[:], kT_s[chbase + m, :, :, :], reads=[B_scr["kT"]], writes=[Bkt])
                    tk = tokbase + m * 128
                    P.dma(None, vt[:, :, 0:64], v_s[tk:tk + 128, :].rearrange("p (h d) -> p h d", h=NH_A),
                          reads=[B_scr["v"]], writes=[Bvt])
                tq = tokbase + i * 128
                qt, Bqt = qts.next()
                sg, Bsg = sgs.next()
                P.dma(None, qt[:], qT_s[chbase + i, :, :, :], reads=[B_scr["qT"]], writes=[Bqt])
                P.dma(None, sg[:], sg_s[tq:tq + 128, :], reads=[B_scr["sg"]], writes=[Bsg])
                if tid == 0:
                    ebt, Bebt = ebg, Bebg
                else:
                    ebt, Bebt = ebsp.next()
                    P.dma(None, ebt[:], eb_s[tid, :, :, :], reads=[B_scr["eb"]], writes=[Bebt])
                c_.update(qt=qt, Bqt=Bqt, sg=sg, Bsg=Bsg, ebt=ebt, Bebt=Bebt, lo=lo, ns=hi - lo + 1, chbase=chbase, tq=tq)

            def B1(u):
                c_ = bctx[u]
                qt, Bqt, sg, Bsg, ebt, Bebt = c_["qt"], c_["Bqt"], c_["sg"], c_["Bsg"], c_["ebt"], c_["Bebt"]
                lo, ns, chbase = c_["lo"], c_["ns"], c_["chbase"]
                og, Bog = ogs.next()
                c_["og"], c_["Bog"] = og, Bog

                def s_stage(h):
                    hp, po = h // 2, (h % 2) * 64
                    pst, Bpst = psts.next()
                    for s in range(ns):
                        kt, Bkt = kring[(chbase + lo + s) % NR]
                        P.op("pe", lambda e: e.matmul(
                            pst[:, s * 128:(s + 1) * 128], kt[po:po + 64, hp, :], qt[po:po + 64, hp, :],
                            start=True, stop=True), reads=[Bkt, Bqt], writes=[Bpst])
                    E, BE = Es.next()
                    P.op("act", lambda e: e.activation(E[:, 0:ns * 128], pst[:, 0:ns * 128], AF.Exp),
                         reads=[Bpst], writes=[BE])
                    Pm, BPm = Pms.next()
                    P.op("pool" if h % 2 == 0 else "dve", lambda e: e.tensor_tensor(
                        Pm[:, 0:ns * 128], E[:, 0:ns * 128], ebt[:, h, 0:ns * 128], ALU.mult),
                        reads=[BE, Bebt], writes=[BPm])
                    return Pm, BPm

                def pv_stage(h, Pm, BPm):
                    pO, BpO = pOs.next()
                    for s in range(ns):
                        vt, Bvt = vring[(chbase + lo + s) % NR]
                        P.op("pe", lambda e: e.matmul(
                            pO[:, 0:72], Pm[:, s * 128:(s + 1) * 128], vt[:, h, 0:72],
                            start=(s == 0), stop=(s == ns - 1)), reads=[BPm, Bvt], writes=[BpO])
                    rc, Brc = rcs.next()
                    P.op("dve", lambda e: e.reciprocal(rc[:], pO[:, 64:65]), reads=[BpO], writes=[Brc])
                    on_, Bon_ = onrm.next()
                    P.op("dve", lambda e: e.tensor_scalar(on_[:], pO[:, 0:64], rc[:, 0:1], None, ALU.mult),
                         reads=[BpO, Brc], writes=[Bon_])
                    P.op("dve", lambda e: e.tensor_tensor(
                        og[:, h * 64:(h + 1) * 64], on_[:], sg[:, h * 64:(h + 1) * 64], ALU.mult),
                        reads=[Bon_, Bsg], writes=[Bog])

                pend = []
                for h in range(NH_A):
                    pend.append((h, s_stage(h)))
                    if len(pend) > 2:
                        h_, (Pm_, BPm_) = pend.pop(0)
                        pv_stage(h_, Pm_, BPm_)
                for h_, (Pm_, BPm_) in pend:
                    pv_stage(h_, Pm_, BPm_)

            def B2(u):
                c_ = bctx[u]
                og, Bog = c_["og"], c_["Bog"]
                tq = c_["tq"]
                xb, Bxb = xbs.next()
                c_["xb"], c_["Bxb"] = xb, Bxb
                P.dma(None, xb[:], x_in[tq:tq + 128, :], reads=[B_x], writes=[Bxb])
                for c in range(8):
                    P.op("pe", lambda e: e.transpose(
                        pT2[:, c * 128:(c + 1) * 128], og[:, c * 128:(c + 1) * 128], identb[:]),
                        reads=[Bog, Bidb], writes=[BpT2])
                ogT, BogT = ogTs.next()
                c_["ogT"], c_["BogT"] = ogT, BogT
                P.op("act", lambda e: e.activation(ogT[:].rearrange("p c t -> p (c t)"), pT2[:], AF.Copy),
                     reads=[BpT2], writes=[BogT])

            def B3(u):
                c_ = bctx.pop(u)
                ogT, BogT, xb, Bxb, tq = c_["ogT"], c_["BogT"], c_["xb"], c_["Bxb"], c_["tq"]
                h1o, Bh1o = h1os.next()
                for half in range(2):
                    po_, Bpo = pos.next()
                    for c in range(8):
                        P.op("pe", lambda e: e.matmul(
                            po_[:], ogT[:, c, :], Wo[:, c, half * 512:(half + 1) * 512],
                            start=(c == 0), stop=(c == 7)), reads=[BogT, BWo], writes=[Bpo])
                    P.op("dve", lambda e: e.tensor_tensor(
                        h1o[:, half * 512:(half + 1) * 512], po_[:], xb[:, half * 512:(half + 1) * 512], ALU.add),
                        reads=[Bpo, Bxb], writes=[Bh1o])
                P.dma(None, h1_s[tq:tq + 128, :], h1o[:], reads=[Bh1o], writes=[B_scr["h1"]])

            nBU = len(bunits)
            for t in range(nBU + 3):
                for fn_, off_ in ((B3, 3), (B2, 2), (B1, 1), (B0, 0)):
                    if 0 <= t - off_ < nBU:
                        fn_(t - off_)
            P.flush()

        if "stopB" in debug:
            return nc

        with contextlib.ExitStack() as st:
            Wg, BWg = load_weight_bf16(st, "Wg", g_w_in, 8, 6176, stage_cols=1544)
            lnw, Blnw = mk(st, "lnwC", [128, D], F32)
            P.dma("sp", lnw[:], ln_w_bc[1:2, :].partition_broadcast(128), writes=[Blnw])
            cw, Bcw = mk(st, "cwC", [128, 32, 5], F32)
            P.dma("sp", cw[:], g_cw[:, :, :], writes=[Bcw])
            dtb, Bdtb = mk(st, "dtbC", [128, 16], F32)
            nA, BnA = mk(st, "nAC", [128, 16], F32)
            P.dma("sp", dtb[:], g_dtb[0:1, :].partition_broadcast(128), writes=[Bdtb])
            P.dma("sp", nA[:], g_alog[0:1, :].partition_broadcast(128), writes=[BnA])
            P.op("act", lambda e: e.activation(nA[:], nA[:], AF.Exp), reads=[BnA], writes=[BnA])
            P.op("dve", lambda e: e.tensor_scalar(nA[:], nA[:], -1.0, None, ALU.mult), reads=[BnA], writes=[BnA])

            W_ = seg_t + 4
            hnT, BhnT = mk(st, "hnTC", [128, 8, W_], BF16)
            xts = Rot([mk(st, "xtC%d" % i, [128, D], F32) for i in range(2)])
            xns = Rot([mk(st, "xnC%d" % i, [128, D], BF16) for i in range(2)])
            sss = Rot([mk(st, "ssC%d" % i, [128, 2], F32) for i in range(4)])
            pTs = Rot([mk(st, "pTC%d" % i, [128, 1024], BF16, psum=True) for i in range(2)])
            pps = Rot([mk(st, "ppC%d" % i, [128, 512], F32, psum=True) for i in range(4)])
            psm, Bpsm = mk(st, "psmC", [128, 512], F32, psum=True)
            HT = seg_t // 2
            NW = min(512, HT)
            NN = HT // NW
            NJ = HT // 128
            projs = Rot([mk(st, "projC%d" % i, [128, HT + 4], F32) for i in range(3)])
            accs = Rot([mk(st, "accC%d" % i, [128, HT], F32) for i in range(3)])
            ys = Rot([mk(st, "yC%d" % i, [128, HT], F32) for i in range(3)])
            ynb = Rot([mk(st, "ynC%d" % i, [128, HT], BF16) for i in range(5)])
            tks = Rot([mk(st, "tkC%d" % i, [128, 8, 128], BF16) for i in range(2)])
            zos = Rot([mk(st, "zoC%d" % i, [128, 512], BF16) for i in range(2)])
            bgt = Rot([mk(st, "bgC%d" % i, [128, 32], F32) for i in range(2)])
            gts = Rot([mk(st, "gtC%d" % i, [128, 16], F32) for i in range(2)])
            bgall = mk(st, "bgallC", [128, NCS, 32], F32)
            gtall = mk(st, "gtallC", [128, 2, NCS, 8], F32)
            grall = mk(st, "grallC", [NCS * 8, 2, 128], F32)

            for seg in range(NSEG):
                T0 = seg * seg_t
                has_prev = (seg not in (0, GRP))
                has_next = (seg not in (GRP - 1, GRP))
                s1ctx = {}

                def S1a(j):
                    xt, Bxt = xts.next()
                    xn, Bxn = xns.next()
                    ss, Bss = sss.next()
                    s1ctx[j] = (xn, Bxn)
                    if j < NCS:
                        P.dma(None, xt[:], h1_s[T0 + j * 128:T0 + (j + 1) * 128, :], reads=[B_scr["h1"]], writes=[Bxt])
                        rmsnorm_rows(xn, xt[:, :], 128, lnw, xn[:, :], ss[:, 0:1], ss[:, 1:2], [Bxt], Bss, Bxn, Bxn, Blnw)
                    else:
                        P.op("pool", lambda e: e.memset(xt[0:32, :], 0.0), writes=[Bxt])
                        if has_prev:
                            P.dma(None, xt[0:2, :], h1_s[T0 - 2:T0, :], reads=[B_scr["h1"]], writes=[Bxt])
                        if has_next:
                            P.dma(None, xt[2:4, :], h1_s[T0 + seg_t:T0 + seg_t + 2, :], reads=[B_scr["h1"]], writes=[Bxt])
                        rmsnorm_rows(xn, xt[0:4, :], 4, lnw, xn[0:4, :], ss[:, 0:1], ss[:, 1:2], [Bxt], Bss, Bxn, Bxn,
                                     Blnw, flag=flagt)

                def S1b(j):
                    xn, Bxn = s1ctx.pop(j)
                    np_ = 128 if j < NCS else 4
                    pT, BpT = pTs.next()
                    for c in range(8):
                        P.op("pe", lambda e: e.transpose(
                            pT[:, c * 128:c * 128 + np_], xn[0:np_, c * 128:(c + 1) * 128], identb[0:np_, 0:np_]),
                            reads=[Bxn, Bidb], writes=[BpT])
                    if j < NCS:
                        P.op("act", lambda e: e.activation(
                            hnT[:, :, 2 + j * 128:2 + (j + 1) * 128], pT[:].rearrange("p (c t) -> p c t", c=8), AF.Copy),
                            reads=[BpT], writes=[BhnT])
                    else:
                        pv_ = pT[:].rearrange("p (c t) -> p c t", c=8)
                        P.op("dve", lambda e: e.tensor_copy(hnT[:, :, 0:2], pv_[:, :, 0:2]), reads=[BpT], writes=[BhnT])
                        P.op("dve", lambda e: e.tensor_copy(hnT[:, :, seg_t + 2:seg_t + 4], pv_[:, :, 2:4]),
                             reads=[BpT], writes=[BhnT])

                for t in range(NCS + 2):
                    if 0 <= t - 1 <= NCS:
                        S1b(t - 1)
                    if t <= NCS:
                        S1a(t)
                cunits = [(hf, cc) for hf in range(2) for cc in range(32)]
                cctx = {}

                def PJ(u):
                    hf, cc = cunits[u]
                    c_ = {}
                    cctx[u] = c_
                    col0 = hf * HT
                    proj, Bproj = projs.next()
                    c_["proj"], c_["Bproj"] = proj, Bproj
                    for (c0, d0) in ((col0, 0), (col0 + HT + 2, 2)):
                        for c in range(8):
                            P.op("pe", lambda e: e.matmul(
                                psm[:, d0:d0 + 2], Wg[:, c, cc * 128:(cc + 1) * 128], hnT[:, c, c0:c0 + 2],
                                start=(c == 0), stop=(c == 7)), reads=[BWg, BhnT], writes=[Bpsm])
                    P.op("act", lambda e: e.activation(proj[:, 0:2], psm[:, 0:2], AF.Copy), reads=[Bpsm], writes=[Bproj])
                    P.op("act", lambda e: e.activation(proj[:, HT + 2:HT + 4], psm[:, 2:4], AF.Copy),
                         reads=[Bpsm], writes=[Bproj])
                    for n in range(NN):
                        pp, Bpp = pps.next()
                        for c in range(8):
                            P.op("pe", lambda e: e.matmul(
                                pp[:, 0:NW], Wg[:, c, cc * 128:(cc + 1) * 128],
                                hnT[:, c, col0 + 2 + n * NW:col0 + 2 + (n + 1) * NW],
                                start=(c == 0), stop=(c == 7)), reads=[BWg, BhnT], writes=[Bpp])
                        P.op("act", lambda e: e.activation(proj[:, 2 + n * NW:2 + (n + 1) * NW], pp[:, 0:NW], AF.Copy),
                             reads=[Bpp], writes=[Bproj])

                def CV(u):
                    hf, cc = cunits[u]
                    c_ = cctx[u]
                    proj, Bproj = c_["proj"], c_["Bproj"]
                    acc, Bacc = accs.next()
                    c_["acc"], c_["Bacc"] = acc, Bacc
                    P.op("dve", lambda e: e.tensor_scalar(acc[:], proj[:, 0:HT], cw[:, cc, 0:1], None, ALU.mult),
                         reads=[Bproj, Bcw], writes=[Bacc])
                    for k in range(1, 5):
                        P.op("dve", lambda e: e.scalar_tensor_tensor(
                            acc[:], proj[:, k:k + HT], cw[:, cc, k:k + 1], acc[:], ALU.mult, ALU.add),
                            reads=[Bproj, Bcw, Bacc], writes=[Bacc])
                    if cc < 16:
                        y, By = ys.next()
                        c_["y"], c_["By"] = y, By
                        P.op("act", lambda e: e.activation(y[:], acc[:], AF.Silu), reads=[Bacc], writes=[By])
                    else:
                        yn, Byn = ynb.next()
                        c_["yn"], c_["Byn"] = yn, Byn
                        P.op("act", lambda e: e.activation(yn[:], acc[:], AF.Silu), reads=[Bacc], writes=[Byn])

                def NM1(u):
                    hf, cc = cunits[u]
                    if cc >= 16:
                        return
                    c_ = cctx[u]
                    acc, Bacc, y, By = c_["acc"], c_["Bacc"], c_["y"], c_["By"]
                    P.op("pool", lambda e: e.tensor_tensor(acc[:], y[:], y[:], ALU.mult), reads=[By], writes=[Bacc])
                    for n in range(NN):
                        pp, Bpp = pps.next()
                        P.op("pe", lambda e: e.matmul(pp[:, 0:NW], onesf, acc[:, n * NW:(n + 1) * NW], start=True, stop=True),
                             reads=[Bc, Bacc], writes=[Bpp])
                        P.op("act", lambda e: e.activation(acc[:, n * NW:(n + 1) * NW], pp[:, 0:NW], AF.Ln, bias=L2_EPS),
                             reads=[Bpp], writes=[Bacc])

                def NM2(u):
                    hf, cc = cunits[u]
                    if cc >= 16:
                        return
                    c_ = cctx[u]
                    acc, Bacc, y, By = c_["acc"], c_["Bacc"], c_["y"], c_["By"]
                    P.op("act", lambda e: e.activation(acc[:], acc[:], AF.Exp, scale=-0.5), reads=[Bacc], writes=[Bacc])
                    yn, Byn = ynb.next()
                    c_["yn"], c_["Byn"] = yn, Byn
                    sc = float(DK ** -0.5) if cc < 8 else 1.0
                    P.op("dve", lambda e: e.scalar_tensor_tensor(yn[:], y[:], sc, acc[:], ALU.mult, ALU.mult),
                         reads=[By, Bacc], writes=[Byn])
                    dst, bd = (gq_s, B_scr["gq"]) if cc < 8 else (gk_s, B_scr["gk"])
                    ch0 = (T0 + hf * HT) // 128
                    P.dma(None, dst[ch0:ch0 + NJ, :, cc % 8, :].rearrange("j p t -> p j t"),
                          yn[:].rearrange("p (j t) -> p j t", j=NJ), reads=[Byn], writes=[bd])

                def TR(u):
                    hf, cc = cunits[u]
                    c_ = cctx.pop(u)
                    if cc < 8:
                        return
                    yn, Byn = c_["yn"], c_["Byn"]
                    pT, BpT = pTs.next()
                    for jj in range(NJ):
                        P.op("pe", lambda e: e.transpose(pT[:, jj * 128:(jj + 1) * 128], yn[:, jj * 128:(jj + 1) * 128], identb[:]),
                             reads=[Byn, Bidb], writes=[BpT])
                    tk, Btk = tks.next()
                    P.op("act", lambda e: e.activation(
                        tk[:, 0:NJ, :], pT[:, 0:NJ * 128].rearrange("p (j t) -> p j t", j=NJ), AF.Copy),
                        reads=[BpT], writes=[Btk])
                    tA = T0 + hf * HT
                    if cc < 16:
                        dd = gkt_s[tA:tA + HT, cc - 8, :].rearrange("(j p) t -> p j t", p=128)
                        bd = B_scr["gkt"]
                    else:
                        hh_, half_ = (cc - 16) // 2, (cc - 16) % 2
                        dd = gv_s[tA:tA + HT, hh_, half_ * 128:(half_ + 1) * 128].rearrange("(j p) t -> p j t", p=128)
                        bd = B_scr["gv"]
                    P.dma(None, dd, tk[:, 0:NJ, :], reads=[Btk], writes=[bd])

                nCU = len(cunits)
                stages_c = ((TR, 4), (NM2, 3), (PJ, 0), (NM1, 2), (CV, 1))
                for t in range(nCU + 4):
                    for fn_, off_ in stages_c:
                        if 0 <= t - off_ < nCU:
                            fn_(t - off_)
                psm_v = psm[:, 0:NCS * 32].rearrange("p (j c) -> p j c", j=NCS)
                for j in range(NCS):
                    tA = T0 + j * 128
                    for n in range(4):
                        pp, Bpp = pps.next()
                        for c in range(8):
                            P.op("pe", lambda e: e.matmul(
                                pp[:], hnT[:, c, 2 + j * 128:2 + (j + 1) * 128],
                                Wg[:, c, 4096 + n * 512:4096 + (n + 1) * 512], start=(c == 0), stop=(c == 7)),
                                reads=[BWg, BhnT], writes=[Bpp])
                        zo, Bzo = zos.next()
                        P.op("act", lambda e: e.activation(zo[:], pp[:], AF.Silu), reads=[Bpp], writes=[Bzo])
                        P.dma(None, z_s[tA:tA + 128, n * 512:(n + 1) * 512], zo[:], reads=[Bzo], writes=[B_scr["z"]])
                    for c in range(8):
                        P.op("pe", lambda e: e.matmul(
                            psm[:, j * 32:(j + 1) * 32], hnT[:, c, 2 + j * 128:2 + (j + 1) * 128], Wg[:, c, 6144:6176],
                            start=(c == 0), stop=(c == 7)), reads=[BWg, BhnT], writes=[Bpsm])
                bga, Bbga = bgall
                gta, Bgta = gtall
                P.op("act", lambda e: e.activation(bga[:, :, 0:16], psm_v[:, :, 0:16], AF.Sigmoid), reads=[Bpsm], writes=[Bbga])
                P.op("dve", lambda e: e.tensor_tensor(
                    gta[:].rearrange("p d j h -> p j d h"), psm_v[:, :, 16:32].rearrange("p j (d h) -> p j d h", d=2),
                    dtb[:].rearrange("p (d h) -> p d h", d=2).unsqueeze(1).to_broadcast([128, NCS, 2, 8]), ALU.add),
                     reads=[Bpsm, Bdtb], writes=[Bgta])
                P.op("act", lambda e: e.activation(gta[:], gta[:], AF.Exp), reads=[Bgta], writes=[Bgta])
                P.op("act", lambda e: e.activation(gta[:], gta[:], AF.Ln, bias=1.0), reads=[Bgta], writes=[Bgta])
                P.op("dve", lambda e: e.tensor_tensor(
                    gta[:], gta[:], nA[:].rearrange("p (d h) -> p d h", d=2).unsqueeze(2).to_broadcast([128, 2, NCS, 8]), ALU.mult),
                     reads=[Bgta, BnA], writes=[Bgta])
                pp, Bpp = pps.next()
                P.op("pe", lambda e: e.matmul(pp[:, 0:NCS * 8], trif, gta[:, 0, :, :].rearrange("p j h -> p (j h)"), start=True, stop=True),
                     reads=[Bc, Bgta], writes=[Bpp])
                P.op("pe", lambda e: e.matmul(pp[:, NCS * 8:2 * NCS * 8], trib, gta[:, 1, :, :].rearrange("p j h -> p (j h)"), start=True, stop=True),
                     reads=[Bc, Bgta], writes=[Bpp])
                P.op("dve", lambda e: e.tensor_copy(
                    bga[:, :, 16:32].rearrange("p j (d h) -> p d j h", d=2),
                    pp[:, 0:2 * NCS * 8].rearrange("p (d j h) -> p d j h", d=2, j=NCS)), reads=[Bpp], writes=[Bbga])
                P.dma(None, bg_s[T0:T0 + seg_t, :].rearrange("(j p) c -> p j c", p=128), bga[:], reads=[Bbga],
                      writes=[B_scr["bg"]])
                pr_, Bpr_ = pps.next()
                P.op("pe", lambda e: e.matmul(pr_[0:NCS * 8, 0:128], gta[:, 0, :, :].rearrange("p j h -> p (j h)"), trif, start=True, stop=True),
                     reads=[Bc, Bgta], writes=[Bpr_])
                P.op("pe", lambda e: e.matmul(pr_[0:NCS * 8, 128:256], gta[:, 1, :, :].rearrange("p j h -> p (j h)"), trib, start=True, stop=True),
                     reads=[Bc, Bgta], writes=[Bpr_])
                gra, Bgra = grall
                P.op("dve", lambda e: e.tensor_copy(
                    gra[:], pr_[0:NCS * 8, 0:256].rearrange("p (d t) -> p d t", d=2)), reads=[Bpr_], writes=[Bgra])
                for j in range(NCS):
                    tA = T0 + j * 128
                    P.dma(None, gcr_s[:, :, tA:tA + 128].rearrange("d h t -> h d t"), gra[j * 8:(j + 1) * 8, :, :],
                          reads=[Bgra], writes=[B_scr["gcr"]])
            P.flush()

        if "stopC" in debug:
            return nc

        with contextlib.ExitStack() as st:
            HB = 4
            S32 = [mk(st, "S32_%d" % d, [128, 8, 256], F32) for d in range(2)]
            Sbf = [mk(st, "Sbf_%d" % d, [128, 8, 256], BF16) for d in range(2)]
            S32B = [[Buf() for _ in range(8)] for d in range(2)]
            SbfB = [[Buf() for _ in range(8)] for d in range(2)]
            kTs = Rot([mk(st, "kTD%d" % i, [128, 8, 128], BF16) for i in range(2)])
            qTs = Rot([mk(st, "qTD%d" % i, [128, 8, 128], BF16) for i in range(2)])
            kts = Rot([mk(st, "ktD%d" % i, [128, 8, 128], BF16) for i in range(2)])
            vts = Rot([mk(st, "vtD%d" % i, [128, 8, 256], BF16) for i in range(2)])
            bgs = Rot([mk(st, "bgD%d" % i, [128, 32], F32) for i in range(3)])
            GRs = Rot([mk(st, "GRD%d" % i, [128, 8, 128], F32) for i in range(2)])
            smalls = Rot([mk(st, "smD%d" % i, [128, 4, 8], F32) for i in range(2)])
            tmps = Rot([mk(st, "tmpD%d" % i, [128, 8, 128], F32) for i in range(2)])
            DTs = Rot([mk(st, "DTD%d" % i, [128, 8, 128], F32) for i in range(1)])
            DSs = Rot([mk(st, "DSD%d" % i, [128, 8, 128], F32) for i in range(1)])
            ERs = Rot([mk(st, "ERD%d" % i, [128, 8, 128], F32) for i in range(2)])
            Aqs = Rot([mk(st, "AqD%d" % i, [128, 8, 128], BF16) for i in range(2)])
            qgs = Rot([mk(st, "qgD%d" % i, [128, 8, 128], BF16) for i in range(2)])
            rws = Rot([mk(st, "rwD%d" % i, [128, 8, 128], F32) for i in range(2)])
            kds = Rot([mk(st, "kdD%d" % i, [128, 8, 128], BF16) for i in range(2)])
            vbs = Rot([mk(st, "vbD%d" % i, [128, 8, 256], F32) for i in range(2)])
            Ab = Rot([mk(st, "AbD%d" % i, [128, HB, 128], F32) for i in range(4)])
            Nb = Rot([mk(st, "NbD%d" % i, [128, HB, 128], F32) for i in range(4)])
            ZP0 = Rot([mk(st, "ZP0D%d" % i, [128, HB, 2, 128], F32) for i in range(4)])
            ZPm = Rot([mk(st, "ZPmD%d" % i, [128, HB, 2, 128], F32) for i in range(4)])
            Yb = Rot([mk(st, "YbD%d" % i, [128, HB, 128], F32) for i in range(4)])
            P32 = Rot([mk(st, "P32D%d" % i, [128, HB, 128], F32) for i in range(4)])
            nWs = Rot([mk(st, "nWD%d" % i, [128, HB, 128], BF16) for i in range(2)])
            vns = Rot([mk(st, "vnD%d" % i, [128, 256], BF16) for i in range(4)])
            oos = Rot([mk(st, "ooD%d" % i, [128, 8, 256], F32) for i in range(2)])
            pM = [mk(st, "pMD%d" % i, [128, HB * 128], F32, psum=True) for i in range(2)]
            pZs = [mk(st, "pZD%d" % i, [128, 512], F32, psum=True) for i in range(2)]
            pYs = [mk(st, "pYD%d" % i, [128, 512], F32, psum=True) for i in range(2)]
            pPs = [mk(st, "pPD%d" % i, [128, HB * 128], F32, psum=True) for i in range(2)]

            units = []
            for (seg0, nsegs) in ((0, GRP), (GRP, 1)):
                nchg = nsegs * NCS
                chb = seg0 * NCS
                for step in range(nchg):
                    for d in range(2):
                        cl = step if d == 0 else nchg - 1 - step
                        first = (step == 0)
                        carry = (not first) and ((cl % NCS == 0) if d == 0 else (cl % NCS == NCS - 1))
                        units.append((chb + cl, d, first, carry))
            ctxs = {}
            h4 = lambda ap: ap.rearrange("p (h t) -> p h t", h=HB)

            def L_stage(u):
                ch, d, first, carry = units[u]
                c = {}
                ctxs[u] = c
                tA = ch * 128
                c["kT"], c["BkT"] = kTs.next()
                c["qT"], c["BqT"] = qTs.next()
                c["kt"], c["Bkt"] = kts.next()
                c["vt"], c["Bvt"] = vts.next()
                c["bg"], c["Bbg"] = bgs.next()
                c["GR"], c["BGR"] = GRs.next()
                P.dma(None, c["kT"][:], gk_s[ch, :, :, :], reads=[B_scr["gk"]], writes=[c["BkT"]])
                P.dma(None, c["qT"][:], gq_s[ch, :, :, :], reads=[B_scr["gq"]], writes=[c["BqT"]])
                P.dma(None, c["kt"][:], gkt_s[tA:tA + 128, :, :], reads=[B_scr["gkt"]], writes=[c["Bkt"]])
                P.dma(None, c["vt"][:], gv_s[tA:tA + 128, :, :], reads=[B_scr["gv"]], writes=[c["Bvt"]])
                P.dma(None, c["bg"][:], bg_s[tA:tA + 128, :], reads=[B_scr["bg"]], writes=[c["Bbg"]])
                for h in range(8):
                    P.dma(None, c["GR"][:, h, :], gcr_s[d, h:h + 1, tA:tA + 128].partition_broadcast(128),
                          reads=[B_scr["gcr"]], writes=[c["BGR"]])

            def Q_stage(u):
                ch, d, first, carry = units[u]
                c = ctxs[u]
                last = 127 if d == 0 else 0
                kT, BkT, qT, BqT, kt, Bkt, vt, Bvt = c["kT"], c["BkT"], c["qT"], c["BqT"], c["kt"], c["Bkt"], c["vt"], c["Bvt"]
                bg, Bbg, GR, BGR = c["bg"], c["Bbg"], c["GR"], c["BGR"]
                beta = bg[:, d * 8:(d + 1) * 8]
                gc = bg[:, 16 + d * 8:16 + (d + 1) * 8]
                sm, Bsm = smalls.next()
                P.op("act", lambda e: e.activation(sm[:, 0, :], gc, AF.Exp), reads=[Bbg], writes=[Bsm])
                P.op("dve", lambda e: e.tensor_tensor(sm[:, 0, :], sm[:, 0, :], beta, ALU.mult), reads=[Bsm, Bbg], writes=[Bsm])
                P.op("dve", lambda e: e.tensor_tensor(sm[:, 1, :], GR[:, :, last], gc, ALU.subtract),
                     reads=[BGR, Bbg], writes=[Bsm])
                P.op("act", lambda e: e.activation(sm[:, 1, :], sm[:, 1, :], AF.Exp), reads=[Bsm], writes=[Bsm])
                yield
                tmp, Btmp = tmps.next()
                DT, BDT = DTs.next()
                DS, BDS = DSs.next()
                ER, BER = ERs.next()
                c["ER"], c["BER"] = ER, BER
                for h in range(8):
                    P.op("dve", lambda e, h=h: e.scalar_tensor_tensor(
                        tmp[:, h, :], GR[:, h, :], gc[:, h:h + 1], maskD[d], ALU.subtract, ALU.add),
                        reads=[BGR, Bbg, Bc], writes=[Btmp])
                    if h % 2 == 1:
                        yield
                    if h % 4 == 3:
                        P.op("act", lambda e: e.activation(DT[:, h - 3:h + 1, :], tmp[:, h - 3:h + 1, :], AF.Exp),
                             reads=[Btmp], writes=[BDT])
                tmp2, Btmp2 = tmps.next()
                for h in range(8):
                    P.op("dve", lambda e, h=h: e.scalar_tensor_tensor(
                        tmp2[:, h, :], GR[:, h, :], gc[:, h:h + 1], maskS[d], ALU.subtract, ALU.subtract),
                        reads=[BGR, Bbg, Bc], writes=[Btmp2])
                    if h % 2 == 1:
                        yield
                    if h % 4 == 3:
                        P.op("act", lambda e: e.activation(DS[:, h - 3:h + 1, :], tmp2[:, h - 3:h + 1, :], AF.Exp, scale=-1.0),
                             reads=[Btmp2], writes=[BDS])
                for hq in range(2):
                    P.op("act", lambda e: e.activation(ER[:, hq * 4:hq * 4 + 4, :], GR[:, hq * 4:hq * 4 + 4, :], AF.Exp),
                         reads=[BGR], writes=[BER])
                    yield
                qg, Bqg = qgs.next()
                rw, Brw = rws.next()
                kd, Bkd = kds.next()
                vb, Bvb = vbs.next()
                c.update(qg=qg, Bqg=Bqg, rw=rw, Brw=Brw, kd=kd, Bkd=Bkd, vb=vb, Bvb=Bvb)
                P.op("pool", lambda e: e.tensor_tensor(qg[:], qT[:], ER[:], ALU.mult), reads=[BqT, BER], writes=[Bqg])
                P.op("pool", lambda e: e.tensor_tensor(rw[:], kt[:], sm[:, 0, :].unsqueeze(2).to_broadcast([128, 8, 128]),
                                                       ALU.mult), reads=[Bkt, Bsm], writes=[Brw])
                P.op("pool", lambda e: e.tensor_tensor(kd[:], kt[:], sm[:, 1, :].unsqueeze(2).to_broadcast([128, 8, 128]),
                                                       ALU.mult), reads=[Bkt, Bsm], writes=[Bkd])
                P.op("pool", lambda e: e.tensor_tensor(vb[:].bitcast(F32R), vt[:], beta.unsqueeze(2).to_broadcast([128, 8, 256]),
                                                       ALU.mult), reads=[Bvt, Bbg], writes=[Bvb])
                yield
                Aq, BAq = Aqs.next()
                c["Aq"], c["BAq"] = Aq, BAq
                c["hb"] = []
                for hb in range(2):
                    h0 = hb * HB
                    pk, Bpk = pM[0]
                    for hh in range(HB):
                        P.op("pe", lambda e, hh=hh: e.matmul(
                            pk[:, hh * 128:(hh + 1) * 128], kT[:, h0 + hh, :], kT[:, h0 + hh, :], start=True, stop=True),
                            reads=[BkT], writes=[Bpk])
                    A_, BA = Ab.next()
                    yield
                    for hh in range(HB):
                        P.op("dve", lambda e, hh=hh: e.scalar_tensor_tensor(
                            A_[:, hh, :], pk[:, hh * 128:(hh + 1) * 128], beta[:, h0 + hh:h0 + hh + 1], DS[:, h0 + hh, :],
                            ALU.mult, ALU.mult), reads=[Bpk, Bbg, BDS], writes=[BA])
                        if hh % 2 == 1:
                            yield
                    pq, Bpq = pM[1]
                    for hh in range(HB):
                        P.op("pe", lambda e, hh=hh: e.matmul(
                            pq[:, hh * 128:(hh + 1) * 128], kT[:, h0 + hh, :], qT[:, h0 + hh, :], start=True, stop=True),
                            reads=[BkT, BqT], writes=[Bpq])
                    yield
                    P.op("dve", lambda e: e.tensor_tensor(
                        Aq[:, h0:h0 + HB, :], h4(pq[:]), DT[:, h0:h0 + HB, :], ALU.mult), reads=[Bpq, BDT], writes=[BAq])
                    yield
                    for hh in range(HB):
                        P.op("pe", lambda e, hh=hh: e.transpose(pk[:, hh * 128:(hh + 1) * 128], A_[:, hh, :], identf),
                             reads=[BA, Bc], writes=[Bpk])
                    yield
                    N_, BN = Nb.next()
                    P.op("act", lambda e: e.activation(N_[:], h4(pk[:]), AF.Copy), reads=[Bpk], writes=[BN])
                    zp0, Bzp0 = ZP0.next()
                    P.op("dve", lambda e: e.tensor_tensor(
                        zp0[:, :, 1, :].bitcast(F32R), identf.unsqueeze(1).to_broadcast([128, HB, 128]), h4(pk[:]), ALU.subtract),
                        reads=[Bc, Bpk], writes=[Bzp0])
                    c["hb"].append(dict(A=A_, BA=BA, N=N_, BN=BN, zp=zp0, Bzp=Bzp0))
                    yield

            def M_stage(u):
                ch, d, first, carry = units[u]
                c = ctxs[u]
                tA = ch * 128
                last = 127 if d == 0 else 0
                S32t = S32[d][0]
                Sbft = Sbf[d][0]
                BS32h = S32B[d]
                BSbfh = SbfB[d]
                ER, BER = c["ER"], c["BER"]
                qg, Bqg, rw, Brw, kd, Bkd, vb, Bvb, Aq, BAq = (c["qg"], c["Bqg"], c["rw"], c["Brw"], c["kd"], c["Bkd"],
                                                              c["vb"], c["Bvb"], c["Aq"], c["BAq"])
                stt = []
                for hb in range(2):
                    x = c["hb"][hb]
                    stt.append(dict(N=x["N"], BN=x["BN"], Yc=x["A"], BYc=x["BA"], zp=x["zp"], Bzp=x["Bzp"]))
                r32 = lambda ap: ap.bitcast(F32R)
                for lvl in range(0, 7):
                    zpns = [None, None]
                    for hb in range(2):
                        s_ = stt[hb]
                        pRa, BpRa = pZs[hb]
                        pRb, BpRb = pYs[hb]
                        Yc, BYc, zp, Bzp = s_["Yc"], s_["BYc"], s_["zp"], s_["Bzp"]
                        for hh in range(HB):
                            pr, Bpr = (pRa, BpRa) if hh < 2 else (pRb, BpRb)
                            o0 = (hh % 2) * 256
                            if lvl == 0:
                                N_, BN = s_["N"], s_["BN"]
                                P.op("pe", lambda e: e.matmul(pr[:, o0:o0 + 128], Yc[:, hh, :], N_[:, hh, :],
                                                              start=True, stop=True), reads=[BYc, BN], writes=[Bpr])
                            elif lvl < 6:
                                P.op("pe", lambda e: e.matmul(
                                    pr[:, o0:o0 + 256], r32(Yc[:, hh, :]),
                                    r32(zp[:, hh, :, :].rearrange("p z t -> p (z t)")), start=True, stop=True),
                                    reads=[BYc, Bzp], writes=[Bpr])
                            else:
                                P.op("pe", lambda e: e.matmul(pr[:, o0:o0 + 128], r32(Yc[:, hh, :]), r32(zp[:, hh, 1, :]),
                                                              start=True, stop=True), reads=[BYc, Bzp], writes=[Bpr])
                    yield
                    for hb in range(2):
                        s_ = stt[hb]
                        pRa, BpRa = pZs[hb]
                        pRb, BpRb = pYs[hb]
                        zp, Bzp = s_["zp"], s_["Bzp"]
                        if lvl == 0:
                            zpn, Bzpn = zp, Bzp
                        elif lvl < 6:
                            zpn, Bzpn = ZPm.next()
                        else:
                            zpn, Bzpn = P32.next()
                        zpns[hb] = (zpn, Bzpn)
                        for half, (pr, Bpr) in enumerate(((pRa, BpRa), (pRb, BpRb))):
                            prv = pr[:].rearrange("p (h z t) -> p h z t", h=2, z=2)
                            hs = slice(2 * half, 2 * half + 2)
                            if lvl < 6:
                                P.op("act", lambda e: e.activation(zpn[:, hs, 0, :].bitcast(F32R), prv[:, :, 0, :], AF.Copy),
                                     reads=[Bpr], writes=[Bzpn])
                            if 1 <= lvl < 6:
                                P.op("dve", lambda e: e.tensor_tensor(
                                    zpn[:, hs, 1, :].bitcast(F32R), prv[:, :, 1, :], zp[:, hs, 1, :], ALU.add),
                                    reads=[Bpr, Bzp], writes=[Bzpn])
                            if lvl == 6:
                                P.op("dve", lambda e: e.tensor_tensor(
                                    zpn[:, hs, :].bitcast(F32R), prv[:, :, 0, :], zp[:, hs, 1, :], ALU.add),
                                    reads=[Bpr, Bzp], writes=[Bzpn])
                        if lvl == 6:
                            s_["Pf"], s_["BPf"] = zpn, Bzpn
                    yield
                    if lvl == 6:
                        continue
                    for hb in range(2):
                        pTr, BpTr = pPs[hb]
                        zpn, Bzpn = zpns[hb]
                        for hh in range(HB):
                            P.op("pe", lambda e: e.transpose(pTr[:, hh * 128:(hh + 1) * 128], zpn[:, hh, 0, :], identf),
                                 reads=[Bzpn, Bc], writes=[BpTr])
                    yield
                    for hb in range(2):
                        s_ = stt[hb]
                        pTr, BpTr = pPs[hb]
                        Yn, BYn = Yb.next()
                        if hb == 0:
                            P.op("dve", lambda e: e.tensor_copy(Yn[:].bitcast(F32R), h4(pTr[:])), reads=[BpTr], writes=[BYn])
                        else:
                            P.op("act", lambda e: e.activation(Yn[:].bitcast(F32R), h4(pTr[:]), AF.Copy),
                                 reads=[BpTr], writes=[BYn])
                        s_["Yc"], s_["BYc"] = Yn, BYn
                        s_["zp"], s_["Bzp"] = zpns[hb]
                    yield
                nWl = []
                for hb in range(2):
                    h0 = hb * HB
                    pZ, BpZ = pZs[hb]
                    Pf, BPf = stt[hb]["Pf"], stt[hb]["BPf"]
                    for hh in range(HB):
                        P.op("pe", lambda e, hh=hh: e.matmul(
                            pZ[:, hh * 128:(hh + 1) * 128], rw[:, h0 + hh, :], Pf[:, hh, :], start=True, stop=True),
                            reads=[Brw, BPf], writes=[BpZ])
                    nW, BnW = nWs.next()
                    P.op("act", lambda e: e.activation(nW[:], h4(pZ[:]), AF.Copy, scale=-1.0), reads=[BpZ], writes=[BnW])
                    nWl.append((nW, BnW))
                    yield
                if first:
                    P.op("pool", lambda e: e.memset(S32t[:], 0.0), writes=BS32h)
                    P.op("pool", lambda e: e.memset(Sbft[:], 0.0), writes=BSbfh)
                elif carry:
                    P.op("dve", lambda e: e.tensor_scalar(S32t[:], S32t[:], flagt[:, 0:1], None, ALU.mult),
                         reads=BS32h + [Bflag], writes=BS32h)
                    P.op("pool", lambda e: e.tensor_scalar(Sbft[:], Sbft[:], flagt[:, 0:1], None, ALU.mult),
                         reads=BSbfh + [Bflag], writes=BSbfh)
                oo, Boo = oos.next()
                for hh in range(HB):
                    for hb in range(2):
                        h = hb * HB + hh
                        Pf, BPf = stt[hb]["Pf"], stt[hb]["BPf"]
                        nW, BnW = nWl[hb]
                        pv, Bpv = pZs[hb][0][:, 0:256], pZs[hb][1]
                        po_, Bpo = pYs[hb][0][:, 0:256], pYs[hb][1]
                        ps_, Bps = pPs[hb][0][:, 0:256], pPs[hb][1]
                        P.op("pe", lambda e: e.matmul(pv, Pf[:, hh, :].bitcast(F32R), vb[:, h, :].bitcast(F32R),
                                                      start=True, stop=False), reads=[BPf, Bvb], writes=[Bpv])
                        P.op("pe", lambda e: e.matmul(pv, nW[:, hh, :], Sbft[:, h, :], start=False, stop=True),
                             reads=[BnW, BSbfh[h]], writes=[Bpv])
                        vn, Bvn = vns.next()
                        P.op("act", lambda e: e.activation(vn[:], pv, AF.Copy), reads=[Bpv], writes=[Bvn])
                        P.op("pe", lambda e: e.matmul(po_, qg[:, h, :], Sbft[:, h, :], start=True, stop=False),
                             reads=[Bqg, BSbfh[h]], writes=[Bpo])
                        P.op("pe", lambda e: e.matmul(po_, Aq[:, h, :], vn[:], start=False, stop=True),
                             reads=[BAq, Bvn], writes=[Bpo])
                        P.op("pe", lambda e: e.matmul(ps_, kd[:, h, :], vn[:], start=True, stop=True),
                             reads=[Bkd, Bvn], writes=[Bps])
                        P.op("act", lambda e: e.activation(oo[:, h, :], po_, AF.Copy), reads=[Bpo], writes=[Boo])
                        P.op("dve", lambda e: e.scalar_tensor_tensor(
                            S32t[:, h, :], S32t[:, h, :], ER[:, h, last:last + 1], ps_, ALU.mult, ALU.add),
                            reads=[BS32h[h], BER, Bps], writes=[BS32h[h]])
                        P.op("pool", lambda e: e.tensor_copy(Sbft[:, h, :], S32t[:, h, :]), reads=[BS32h[h]], writes=[BSbfh[h]])
                        yield
                P.dma(None, o_s[d][tA:tA + 128, :, :], oo[:], reads=[Boo], writes=[B_scr["of" if d == 0 else "ob"]])
                del ctxs[u]

            nU = len(units)
            for t in range(-2, nU):
                if 0 <= t + 2 < nU:
                    L_stage(t + 2)
                gens = []
                if 0 <= t < nU:
                    gens.append(M_stage(t))
                if 0 <= t + 1 < nU:
                    gens.append(Q_stage(t + 1))
                while gens:
                    for g_ in list(gens):
                        try:
                            next(g_)
                        except StopIteration:
                            gens.remove(g_)
            P.flush()

        if "stopD" in debug:
            return nc

        with contextlib.ExitStack() as st:
            Wo2, BWo2 = load_weight_bf16(st, "Wo2", g_w_out, 16, D, stage_cols=1024)
            nwb, Bnwb = mk(st, "nwE", [128, DV], F32)
            P.dma("sp", nwb[:], g_nw[0:1, :].partition_broadcast(128), writes=[Bnwb])
            fwb, Bfwb = mk(st, "fwE", [128, D], F32)
            P.dma("sp", fwb[:], fin_w[0:1, :].partition_broadcast(128), writes=[Bfwb])
            ofs = Rot([mk(st, "ofE%d" % i, [128, 8, 256], F32) for i in range(4)])
            obs = Rot([mk(st, "obE%d" % i, [128, 8, 256], F32) for i in range(3)])
            zts = Rot([mk(st, "zE%d" % i, [128, 8, 256], BF16) for i in range(4)])
            h1s = Rot([mk(st, "h1E%d" % i, [128, D], F32) for i in range(3)])
            jk, Bjk = mk(st, "jkE", [128, 256], BF16)
            jk2, Bjk2 = mk(st, "jk2E", [128, D], BF16)
            sss = Rot([mk(st, "ssE%d" % i, [128, 16], F32) for i in range(4)])
            ons = Rot([mk(st, "onE%d" % i, [128, 8, 256], BF16) for i in range(3)])
            onTs = Rot([mk(st, "onTE%d" % i, [128, 16, 128], BF16) for i in range(3)])
            h2s = Rot([mk(st, "h2E%d" % i, [128, D], F32) for i in range(3)])
            yos = Rot([mk(st, "yoE%d" % i, [128, D], F32) for i in range(2)])
            s2s = Rot([mk(st, "s2E%d" % i, [128, 2], F32) for i in range(2)])
            pTs = Rot([mk(st, "pTE%d" % i, [128, 1024], BF16, psum=True) for i in range(4)])
            pos = Rot([mk(st, "poE%d" % i, [128, 512], F32, psum=True) for i in range(4)])
            ectx = {}

            def E0(j):
                tA = j * 128
                c_ = {}
                ectx[j] = c_
                of, Bof = ofs.next()
                ob, Bob = obs.next()
                zt, Bzt = zts.next()
                c_.update(of=of, Bof=Bof, ob=ob, Bob=Bob, zt=zt, Bzt=Bzt)
                P.dma(None, of[:], o_s[0][tA:tA + 128, :, :], reads=[B_scr["of"]], writes=[Bof])
                P.dma(None, ob[:], o_s[1][tA:tA + 128, :, :], reads=[B_scr["ob"]], writes=[Bob])
                P.dma(None, zt[:], z_s[tA:tA + 128, :].rearrange("p (h d) -> p h d", h=8), reads=[B_scr["z"]], writes=[Bzt])

            def E1(j):
                c_ = ectx[j]
                of, Bof, ob, Bob = c_["of"], c_["Bof"], c_["ob"], c_["Bob"]
                P.op("pool", lambda e: e.tensor_tensor(of[:], of[:], ob[:], ALU.add), reads=[Bof, Bob], writes=[Bof])
                ss, Bss = sss.next()
                c_["ss"], c_["Bss"] = ss, Bss
                for h in range(8):
                    P.op("act", lambda e: e.activation(
                        jk[:], of[:, h, :], AF.Square, scale=1.0 / 16.0, accum_out=ss[:, h:h + 1]),
                        reads=[Bof], writes=[Bjk, Bss])
                P.op("act", lambda e: e.activation(ss[:, 8:16], ss[:, 0:8], AF.Sqrt, bias=RMS_EPS), reads=[Bss], writes=[Bss])
                P.op("dve", lambda e: e.reciprocal(ss[:, 8:16], ss[:, 8:16]), reads=[Bss], writes=[Bss])

            def E2(j):
                c_ = ectx[j]
                of, Bof, zt, Bzt, ss, Bss = c_["of"], c_["Bof"], c_["zt"], c_["Bzt"], c_["ss"], c_["Bss"]
                on, Bon = ons.next()
                c_["on"], c_["Bon"] = on, Bon
                for h in range(8):
                    P.op("dve", lambda e: e.scalar_tensor_tensor(
                        of[:, h, :], of[:, h, :], ss[:, 8 + h:9 + h], nwb[:], ALU.mult, ALU.mult),
                        reads=[Bof, Bss, Bnwb], writes=[Bof])
                P.op("pool", lambda e: e.tensor_tensor(on[:], of[:], zt[:], ALU.mult), reads=[Bof, Bzt], writes=[Bon])

            def E3(j):
                c_ = ectx[j]
                on, Bon = c_["on"], c_["Bon"]
                tA = j * 128
                h1, Bh1 = h1s.next()
                c_["h1"], c_["Bh1"] = h1, Bh1
                P.dma(None, h1[:], h1_s[tA:tA + 128, :], reads=[B_scr["h1"]], writes=[Bh1])
                onT, BonT = onTs.next()
                c_["onT"], c_["BonT"] = onT, BonT
                for half in range(2):
                    pT, BpT = pTs.next()
                    for c in range(8):
                        cc = half * 8 + c
                        P.op("pe", lambda e: e.transpose(
                            pT[:, c * 128:(c + 1) * 128], on[:, cc // 2, (cc % 2) * 128:(cc % 2 + 1) * 128], identb[:]),
                            reads=[Bon, Bidb], writes=[BpT])
                    P.op("act", lambda e: e.activation(
                        onT[:, half * 8:(half + 1) * 8, :], pT[:].rearrange("p (c t) -> p c t", c=8), AF.Copy),
                        reads=[BpT], writes=[BonT])

            def E4(j):
                c_ = ectx[j]
                onT, BonT, h1, Bh1 = c_["onT"], c_["BonT"], c_["h1"], c_["Bh1"]
                h2, Bh2 = h2s.next()
                c_["h2"], c_["Bh2"] = h2, Bh2
                for half in range(2):
                    po_, Bpo = pos.next()
                    for c in range(16):
                        P.op("pe", lambda e: e.matmul(
                            po_[:], onT[:, c, :], Wo2[:, c, half * 512:(half + 1) * 512], start=(c == 0), stop=(c == 15)),
                            reads=[BonT, BWo2], writes=[Bpo])
                    P.op("dve", lambda e: e.tensor_tensor(
                        h2[:, half * 512:(half + 1) * 512], po_[:], h1[:, half * 512:(half + 1) * 512], ALU.add),
                        reads=[Bpo, Bh1], writes=[Bh2])

            def E5(j):
                c_ = ectx.pop(j)
                h2, Bh2 = c_["h2"], c_["Bh2"]
                tA = j * 128
                yo, Byo = yos.next()
                s2, Bs2 = s2s.next()
                rmsnorm_rows(jk2, h2[:, :], 128, fwb, yo[:, :], s2[:, 0:1], s2[:, 1:2], [Bh2], Bs2, Bjk2, Byo, Bfwb)
                P.dma(None, y_out[tA:tA + 128, :], yo[:], reads=[Byo], writes=[B_x])

            nE = NT // 128
            for t in range(nE + 5):
                for fn_, off_ in ((E5, 5), (E4, 4), (E3, 3), (E2, 2), (E1, 1), (E0, 0)):
                    if 0 <= t - off_ < nE:
                        fn_(t - off_)
            P.flush()
    return nc


def make_consts():
    c = np.zeros((128, 8, 128), np.float32)
    p = np.arange(128)[:, None]
    f = np.arange(128)[None, :]
    c[:, 0, :] = (p == f)
    c[:, 1, :] = 1.0
    c[:, 2, :] = (p <= f)
    c[:, 3, :] = (p >= f)
    c[:, 4, :] = np.where(f >= p, 0.0, -BIG)
    c[:, 5, :] = np.where(f <= p, 0.0, -BIG)
    c[:, 6, :] = np.where(f < p, 0.0, -BIG)
    c[:, 7, :] = np.where(f > p, 0.0, -BIG)
    return c


def shared_inputs(ln_w, na_w_in, na_w_out, gdn_w_in, gdn_conv_w, gdn_a_log, gdn_dt_bias, gdn_norm_w, gdn_w_out,
                  final_norm_w):
    f = lambda a: np.ascontiguousarray(np.asarray(a, dtype=np.float32))
    return {
        "ln_w_bc": f(ln_w),
        "fin_w": f(final_norm_w).reshape(1, D),
        "na_w_in": f(na_w_in[0]),
        "na_w_out": f(na_w_out[0]),
        "g_w_in": f(gdn_w_in[0]),
        "g_cw": f(np.asarray(gdn_conv_w[0]).T.reshape(32, 128, 5).transpose(1, 0, 2)),
        "g_alog": f(gdn_a_log[0]).reshape(1, 16),
        "g_dtb": f(gdn_dt_bias[0]).reshape(1, 16),
        "g_nw": f(gdn_norm_w[0]).reshape(1, DV),
        "g_w_out": f(gdn_w_out[0]),
        "consts": make_consts(),
    }


_PROG_CACHE = {}


def kernel(x_prompt, x_sample, ln_w, na_w_in, na_rpb, na_w_out, gdn_w_in, gdn_conv_w, gdn_a_log, gdn_dt_bias,
           gdn_norm_w, gdn_w_out, final_norm_w):
    seg_t = 2048
    x_prompt = np.asarray(x_prompt, np.float32)
    x_sample = np.asarray(x_sample, np.float32)
    shared = shared_inputs(ln_w, na_w_in, na_w_out, gdn_w_in, gdn_conv_w, gdn_a_log, gdn_dt_bias, gdn_norm_w,
                           gdn_w_out, final_norm_w)
    rpb = np.asarray(na_rpb, np.float32)[0]
    tabs_j = build_na_bias_tables(rpb, seg_t, True).reshape(-1, 128, NH_A, 768)
    tabs_s = build_na_bias_tables(rpb, seg_t, False).reshape(-1, 128, NH_A, 768)
    in_maps = []
    for c in range(8):
        if c < 2:
            xc = np.concatenate([x_prompt[c], x_sample[c]], axis=0)
            m = dict(shared, x=np.ascontiguousarray(xc), na_bias=tabs_j, flag=np.ones((128, 1), np.float32))
        else:
            s0 = 2 + 5 * (c - 2)
            xc = x_sample[s0:s0 + 5].reshape(5 * seg_t, D)
            m = dict(shared, x=np.ascontiguousarray(xc), na_bias=tabs_s, flag=np.zeros((128, 1), np.float32))
        in_maps.append(m)
    if seg_t not in _PROG_CACHE:
        _PROG_CACHE[seg_t] = build_program(seg_t)
    nc = _PROG_CACHE[seg_t]
    res = run_bass_kernel_spmd(nc, in_maps, core_ids=list(range(8)))
    y_prompt = np.empty_like(x_prompt)
    y_sample = np.empty_like(x_sample)
    for c in range(8):
        y = res.results[c]["y"]
        if c < 2:
            y_prompt[c] = y[0:4 * seg_t]
            y_sample[c] = y[4 * seg_t:]
        else:
            s0 = 2 + 5 * (c - 2)
            y_sample[s0:s0 + 5] = y.reshape(5, seg_t, D)
    return (y_prompt, y_sample)
```

```python
import contextlib
import numpy as np
import concourse.bass as bass
import concourse.mybir as mybir
from concourse.bass_utils import run_bass_kernel_spmd

F32 = mybir.dt.float32
BF16 = mybir.dt.bfloat16
F32R = mybir.dt.float32r
ALU = mybir.AluOpType
AF = mybir.ActivationFunctionType

D = 1024
NH_A = 16
GW = 64
NH_G = 8
DK = 128
DV = 256
NSEG = 5
GRP = 4
BIG = 1.0e5
RMS_EPS = 1e-6
L2_EPS = 1e-6

NSLOT = 32
DMA_QS = ("sp", "pool")


class Buf:
    __slots__ = ("w", "r")

    def __init__(self):
        self.w = {}
        self.r = {}


class PBuf(Buf):
    __slots__ = ("bank",)

    def __init__(self):
        Buf.__init__(self)
        self.bank = Buf()


class _Rec:
    def __getattr__(self, name):
        def f(*a, **k):
            self.call = (name, a, k)
            return self
        return f


class Prog:
    def __init__(self, nc):
        self.nc = nc
        self.streams = {e: [] for e in ("pe", "act", "dve", "pool", "sp")}
        self.count = {e: 0 for e in self.streams}
        self.waited = {e: {} for e in self.streams}
        self.dma_n = {q: 0 for q in DMA_QS}
        self.semh = {}
        self.ninst = 0
        self.rr = 0
        self.only_sp = False

    def _deps(self, eng, reads, writes):
        deps = {}
        for b in reads:
            for k, v in b.w.items():
                if deps.get(k, 0) < v:
                    deps[k] = v
        for b in writes:
            for k, v in b.w.items():
                if deps.get(k, 0) < v:
                    deps[k] = v
            for k, v in b.r.items():
                if deps.get(k, 0) < v:
                    deps[k] = v
        for b in list(reads) + list(writes):
            if isinstance(b, PBuf):
                for k, v in b.bank.w.items():
                    if k != eng and deps.get(k, 0) < v:
                        deps[k] = v
        wt = self.waited[eng]
        out = []
        for k, v in deps.items():
            if k == "pe" and eng == "pe":
                continue
            if wt.get(k, 0) >= v:
                continue
            wt[k] = v
            out.append((k, v))
        return out

    def op(self, eng, fn, reads=(), writes=()):
        rec = _Rec()
        fn(rec)
        cname, cargs, ckw = rec.call

        def fn(e, cname=cname, cargs=cargs, ckw=ckw):
            return getattr(e, cname)(*cargs, **ckw)
        waits = self._deps(eng, reads, writes)
        self.count[eng] += 1
        n = self.count[eng]
        self.streams[eng].append((waits, fn, (eng, 1)))
        for b in reads:
            b.r[eng] = n
            if isinstance(b, PBuf):
                b.bank.w[eng] = n
        for b in writes:
            b.w[eng] = n
            b.r = {}
            if isinstance(b, PBuf):
                b.bank.w[eng] = n
        self.ninst += 1

    def dma(self, q, out, in_, reads=(), writes=()):
        if q is None:
            q = "sp"
            self.rr += 1
        waits = self._deps(q, reads, writes)
        n = self.dma_n[q]
        self.dma_n[q] += 1
        slot = n % NSLOT
        val = 16 * (n // NSLOT + 1)
        key = "d_%s_%d" % (q, slot)
        if n >= NSLOT and self.waited[q].get(key, 0) < val - 16:
            self.waited[q][key] = val - 16
            waits.append((key, val - 16))

        def fn(e, out=out, in_=in_):
            return e.dma_start(out=out, in_=in_)
        self.streams[q].append((waits, fn, (key, 16)))
        for b in reads:
            b.r[key] = val
        for b in writes:
            b.w[key] = val
            b.r = {}
        self.ninst += 1

    def alloc_sems(self, st):
        keys = ["pe", "act", "dve", "pool"]
        for q in DMA_QS:
            for s_ in range(NSLOT):
                keys.append("d_%s_%d" % (q, s_))
        for k in keys:
            self.semh[k] = st.enter_context(self.nc.semaphore("s_" + k))

    def _all_events(self):
        fin = []
        for e in ("pe", "act", "dve", "pool"):
            if self.count[e]:
                fin.append((e, self.count[e]))
        for q in DMA_QS:
            n = self.dma_n[q]
            for s_ in range(NSLOT):
                cnt = (n - s_ + NSLOT - 1) // NSLOT if n > s_ else 0
                if cnt:
                    fin.append(("d_%s_%d" % (q, s_), 16 * cnt))
        return fin

    def flush(self):
        nc = self.nc
        fin = self._all_events()
        semh = self.semh
        streams = self.streams
        self.streams = {e: [] for e in streams}
        with nc.Block() as block:
            def run(engobj, name):
                for waits, fn, (ik, iv) in streams[name]:
                    for k, v in waits:
                        engobj.wait_ge(semh[k], v)
                    ins = fn(engobj)
                    ins.then_inc(semh[ik], iv)
                wt = self.waited[name]
                for k, v in fin:
                    if wt.get(k, 0) >= v:
                        continue
                    wt[k] = v
                    engobj.wait_ge(semh[k], v)

            @block.sync
            def _(e):
                run(e, "sp")

            @block.tensor
            def _(e):
                run(e, "pe")

            @block.scalar
            def _(e):
                run(e, "act")

            @block.vector
            def _(e):
                run(e, "dve")

            @block.gpsimd
            def _(e):
                run(e, "pool")


class Rot:
    def __init__(self, items):
        self.items = items
        self.i = 0

    def next(self):
        it = self.items[self.i % len(self.items)]
        self.i += 1
        return it


def na_chunk_range(i, nb, nbs):
    p = i % nbs
    lo, hi = i - 2, i + 2
    if p == 0:
        hi = i + 3
    if p == nbs - 1:
        lo = i - 3
    return max(lo, 0), min(hi, nb - 1)


def na_special_pos(p, nbs):
    sp = sorted(set([0, 1, nbs - 2, nbs - 1]))
    if nbs <= 4:
        sp = list(range(nbs))
    return sp.index(p) if p in sp else None


def na_nspecial(nbs):
    return min(nbs, 4)


def na_table_id(seg, p, nbs):
    spi = na_special_pos(p, nbs)
    if spi is None:
        return 0
    nsp = na_nspecial(nbs)
    if seg == GRP:
        kind = 2
    else:
        top = p < nbs // 2 if nbs > 2 else (p == 0)
        if top:
            kind = 0 if seg == 0 else 1
        else:
            kind = 0 if seg == GRP - 1 else 1
    return 1 + kind * nsp + spi


def build_na_bias_tables(rpb, seg_t, joined):
    rs_rows = seg_t // GW
    nbs = rs_rows // 2
    nsp = na_nspecial(nbs)
    ntab = 1 + 3 * nsp
    tabs = np.full((ntab, 128, NH_A, 6, 128), -BIG, dtype=np.float32)
    done = set()

    def fill(tid, i, nb, seq_of_row):
        if tid in done:
            return
        done.add(tid)
        lo, hi = na_chunk_range(i, nb, nbs)
        a = np.arange(2)[:, None, None, None]
        kc = np.arange(GW)[None, :, None, None]
        b = np.arange(2)[None, None, :, None]
        qc = np.arange(GW)[None, None, None, :]
        for s, m in enumerate(range(lo, hi + 1)):
            kr = 2 * m + a
            qr = 2 * i + b
            s_lo_q, s_r_q = seq_of_row(qr)
            s_lo_k, _ = seq_of_row(kr)
            rs = np.clip(qr - s_lo_q - 4, 0, s_r_q - 8)
            okr = (s_lo_k == s_lo_q) & (kr - s_lo_q >= rs) & (kr - s_lo_q < rs + 8)
            ws = np.clip(qc - 8, 0, GW - 16)
            okc = (kc >= ws) & (kc < ws + 16)
            ok = np.broadcast_to(okr & okc, (2, GW, 2, GW))
            dr = np.broadcast_to(np.clip(kr - qr + 7, 0, 14), (2, GW, 2, GW))
            dc = np.broadcast_to(np.clip(kc - qc + 15, 0, 30), (2, GW, 2, GW))
            vals = rpb[:, dr, dc]
            vals = np.where(ok[None], vals, np.float32(-BIG)).astype(np.float32)
            tabs[tid, :, :, s, :] = vals.reshape(NH_A, 128, 128).transpose(1, 0, 2)

    nbg = GRP * nbs

    def seq_group(row):
        if joined:
            return np.zeros_like(row), np.full_like(row, GRP * rs_rows)
        return (row // rs_rows) * rs_rows, np.full_like(row, rs_rows)

    def seq_single(row):
        return np.zeros_like(row), np.full_like(row, rs_rows)

    for seg in range(GRP):
        for p in range(nbs):
            tid = na_table_id(seg, p, nbs)
            fill(tid, seg * nbs + p, nbg, seq_group)
    for p in range(nbs):
        fill(na_table_id(GRP, p, nbs), p, nbs, seq_single)
    return tabs


def build_program(seg_t, debug=()):
    NT = NSEG * seg_t
    NCH = NT // 128
    NCS = seg_t // 128
    NBS = NCS
    nsp = na_nspecial(NBS)
    NTAB = 1 + 3 * nsp
    assert seg_t % 512 == 0

    nc = bass.Bass("TRN2", target_bir_lowering=False)

    def din(name, shape, dt=F32):
        return nc.dram_tensor(name, list(shape), dt, kind="ExternalInput").ap()

    def dscr(name, shape, dt):
        kind = "ExternalOutput" if name in debug else "Internal"
        return nc.dram_tensor(name, list(shape), dt, kind=kind).ap()

    x_in = din("x", [NT, D])
    ln_w_bc = din("ln_w_bc", [2, D])
    fin_w = din("fin_w", [1, D])
    na_w_in = din("na_w_in", [D, 4 * D])
    na_bias = din("na_bias", [NTAB, 128, NH_A, 6 * 128])
    na_w_out = din("na_w_out", [D, D])
    g_w_in = din("g_w_in", [D, 6176])
    g_cw = din("g_cw", [128, 32, 5])
    g_alog = din("g_alog", [1, 16])
    g_dtb = din("g_dtb", [1, 16])
    g_nw = din("g_nw", [1, DV])
    g_w_out = din("g_w_out", [2 * D, D])
    consts = din("consts", [128, 8, 128])
    flag_in = din("flag", [128, 1])
    y_out = nc.dram_tensor("y", [NT, D], F32, kind="ExternalOutput").ap()

    qT_s = dscr("qT_s", [NCH, 128, 8, 128], BF16)
    kT_s = dscr("kT_s", [NCH, 128, 8, 128], BF16)
    v_s = dscr("v_s", [NT, D], BF16)
    sg_s = dscr("sg_s", [NT, D], BF16)
    eb_s = dscr("eb_s", [NTAB, 128, NH_A, 768], BF16)
    h1_s = dscr("h1_s", [NT, D], F32)
    gq_s = dscr("gq_s", [NCH, 128, 8, 128], BF16)
    gk_s = dscr("gk_s", [NCH, 128, 8, 128], BF16)
    gkt_s = dscr("gkt_s", [NT, 8, 128], BF16)
    gv_s = dscr("gv_s", [NT, 8, 256], BF16)
    z_s = dscr("z_s", [NT, 2 * D], BF16)
    bg_s = dscr("bg_s", [NT, 32], F32)
    gcr_s = dscr("gcr_s", [2, 8, NT], F32)
    o_s = [dscr("of_s", [NT, 8, 256], F32), dscr("ob_s", [NT, 8, 256], F32)]

    P = Prog(nc)
    B_x = Buf()
    B_scr = {n: Buf() for n in ("qT", "kT", "v", "sg", "eb", "h1", "gq", "gk", "gkt", "gv", "z", "bg", "gcr",
                                "of", "ob")}

    with contextlib.ExitStack() as st0:
        P.alloc_sems(st0)

        def mk(st, name, shape, dt, psum=False):
            if psum:
                esz = 2 if dt == BF16 else 4
                per_bank = 2048 // esz
                assert len(shape) == 2
                ncols = ((shape[1] + per_bank - 1) // per_bank) * per_bank
                t_ = st.enter_context(nc.psum_tensor(name, [shape[0], ncols], dt))
                return t_[:, 0:shape[1]], PBuf()
            return st.enter_context(nc.sbuf_tensor(name, list(shape), dt)), Buf()

        cst, Bc = mk(st0, "cst", [128, 8, 128], F32)
        identb, Bidb = mk(st0, "identb", [128, 128], BF16)
        flagt, Bflag = mk(st0, "flagt", [128, 1], F32)
        P.dma("sp", cst[:], consts[:, :, :], writes=[Bc])
        P.dma("sp", flagt[:], flag_in[:, :], writes=[Bflag])
        P.op("dve", lambda e: e.tensor_copy(identb[:], cst[:, 0, :]), reads=[Bc], writes=[Bidb])
        identf = cst[:, 0, :]
        onesf = cst[:, 1, :]
        trif = cst[:, 2, :]
        trib = cst[:, 3, :]
        maskD = [cst[:, 4, :], cst[:, 5, :]]
        maskS = [cst[:, 6, :], cst[:, 7, :]]

        def load_weight_bf16(st, name, src, kchunks, ncols, stage_cols=2048):
            wt, Bw = mk(st, name, [128, kchunks, ncols], BF16)
            engs = ["act", "dve", "pool"]
            k = 0
            with contextlib.ExitStack() as stl:
                stg = [mk(stl, "%s_stg%d" % (name, i), [128, stage_cols], F32) for i in range(2)]
                it = 0
                for c in range(kchunks):
                    for c0 in range(0, ncols, stage_cols):
                        w_ = min(stage_cols, ncols - c0)
                        s_t, s_b = stg[it % 2]
                        it += 1
                        P.dma(None, s_t[:, 0:w_], src[c * 128:(c + 1) * 128, c0:c0 + w_], writes=[s_b])
                        en = engs[k % 3]
                        k += 1
                        if en == "act":
                            P.op("act", lambda e, s_t=s_t, c=c, c0=c0, w_=w_: e.activation(
                                wt[:, c, c0:c0 + w_], s_t[:, 0:w_], AF.Copy), reads=[s_b], writes=[Bw])
                        else:
                            P.op(en, lambda e, s_t=s_t, c=c, c0=c0, w_=w_: e.tensor_copy(
                                wt[:, c, c0:c0 + w_], s_t[:, 0:w_]), reads=[s_b], writes=[Bw])
                P.flush()
            return wt, Bw

        def rmsnorm_rows(jk, xt_ap, np_, lnw_bc, xn_ap, ss, rs, reads, Bss, Bjk, Bxn, Blnw, flag=None):
            P.op("act", lambda e: e.activation(jk[0:np_, :], xt_ap, AF.Square, scale=1.0 / 32.0,
                                               accum_out=ss[0:np_, 0:1]), reads=reads, writes=[Bjk, Bss])
            P.op("act", lambda e: e.activation(rs[0:np_, 0:1], ss[0:np_, 0:1], AF.Sqrt, bias=RMS_EPS),
                 reads=[Bss], writes=[Bss])
            P.op("dve", lambda e: e.reciprocal(rs[0:np_, 0:1], rs[0:np_, 0:1]), reads=[Bss], writes=[Bss])
            if flag is not None:
                P.op("dve", lambda e: e.tensor_tensor(rs[0:np_, 0:1], rs[0:np_, 0:1], flag[0:np_, 0:1], ALU.mult),
                     reads=[Bss, Bflag], writes=[Bss])
            P.op("dve", lambda e: e.scalar_tensor_tensor(xn_ap, xt_ap, rs[0:np_, 0:1], lnw_bc[0:np_, :],
                                                         ALU.mult, ALU.mult),
                 reads=list(reads) + [Bss, Blnw], writes=[Bxn])

        P.flush()

        with contextlib.ExitStack() as st:
            stg = [mk(st, "eb_stg%d" % i, [128, 4, 768], F32) for i in range(2)]
            ebo = [mk(st, "eb_o%d" % i, [128, 4, 768], BF16) for i in range(2)]
            it = 0
            for t in range(NTAB):
                for hq in range(4):
                    s_t, s_b = stg[it % 2]
                    o_t, o_b = ebo[it % 2]
                    it += 1
                    P.dma(None, s_t[:], na_bias[t, :, hq * 4:(hq + 1) * 4, :], writes=[s_b])
                    P.op("act", lambda e, s_t=s_t, o_t=o_t: e.activation(o_t[:], s_t[:], AF.Exp),
                         reads=[s_b], writes=[o_b])
                    P.dma(None, eb_s[t, :, hq * 4:(hq + 1) * 4, :], o_t[:], reads=[o_b], writes=[B_scr["eb"]])
            P.flush()

        with contextlib.ExitStack() as st:
            Wa, BWa = load_weight_bf16(st, "Wa", na_w_in, 8, 4 * D)
            lnw, Blnw = mk(st, "lnwA", [128, D], F32)
            P.dma("sp", lnw[:], ln_w_bc[0:1, :].partition_broadcast(128), writes=[Blnw])
            xts = Rot([mk(st, "xtA%d" % i, [128, 4, D], F32) for i in range(2)])
            xn, Bxn = mk(st, "xnA", [128, 4, D], BF16)
            jk, Bjk = mk(st, "jkA", [128, D], BF16)
            sss = Rot([mk(st, "ssA%d" % i, [128, 2], F32) for i in range(4)])
            hnTs = Rot([mk(st, "hnTA%d" % i, [128, 8, 512], BF16) for i in range(2)])
            pTs = Rot([mk(st, "pTA%d" % i, [128, 512], BF16, psum=True) for i in range(2)])
            pqs = Rot([mk(st, "pqA%d" % i, [128, 512], F32, psum=True) for i in range(4)])
            qos = Rot([mk(st, "qoA%d" % i, [128, 512], BF16) for i in range(4)])
            for g in range(NT // 512):
                t0 = g * 512
                xt, Bxt = xts.next()
                P.dma(None, xt[:], x_in[t0:t0 + 512, :].rearrange("(j p) d -> p j d", p=128), reads=[B_x],
                      writes=[Bxt])
                for j in range(4):
                    ss, Bss = sss.next()
                    rmsnorm_rows(jk, xt[:, j, :], 128, lnw, xn[:, j, :], ss[:, 0:1], ss[:, 1:2], [Bxt], Bss,
                                 Bjk, Bxn, Blnw)
                hnT, BhnT = hnTs.next()
                for c in range(8):
                    pT, BpT = pTs.next()
                    for j in range(4):
                        P.op("pe", lambda e, pT=pT, j=j, c=c: e.transpose(
                            pT[:, j * 128:(j + 1) * 128], xn[:, j, c * 128:(c + 1) * 128], identb[:]),
                            reads=[Bxn, Bidb], writes=[BpT])
                    en = "dve" if c % 2 == 0 else "pool"
                    if en == "dve":
                        P.op("dve", lambda e, pT=pT, hnT=hnT, c=c: e.tensor_copy(hnT[:, c, :], pT[:]),
                             reads=[BpT], writes=[BhnT])
                    else:
                        P.op("act", lambda e, pT=pT, hnT=hnT, c=c: e.activation(hnT[:, c, :], pT[:], AF.Copy),
                             reads=[BpT], writes=[BhnT])
                for fo in range(16):
                    pq, Bpq = pqs.next()
                    for c in range(8):
                        P.op("pe", lambda e, pq=pq, hnT=hnT, c=c, fo=fo: e.matmul(
                            pq[:], Wa[:, c, fo * 128:(fo + 1) * 128], hnT[:, c, :], start=(c == 0), stop=(c == 7)),
                            reads=[BWa, BhnT], writes=[Bpq])
                    qo, Bqo = qos.next()
                    sc = 0.125 if fo < 8 else 1.0
                    P.op("act", lambda e, pq=pq, qo=qo, sc=sc: e.activation(qo[:], pq[:], AF.Copy, scale=sc),
                         reads=[Bpq], writes=[Bqo])
                    dst = qT_s if fo < 8 else kT_s
                    bd = B_scr["qT"] if fo < 8 else B_scr["kT"]
                    ch0 = t0 // 128
                    P.dma(None, dst[ch0:ch0 + 4, :, fo % 8, :].rearrange("j p t -> p j t"),
                          qo[:].rearrange("p (j t) -> p j t", j=4), reads=[Bqo], writes=[bd])
                for j in range(4):
                    for fg in range(4):
                        pq, Bpq = pqs.next()
                        for c in range(8):
                            P.op("pe", lambda e, pq=pq, hnT=hnT, c=c, fg=fg, j=j: e.matmul(
                                pq[:], hnT[:, c, j * 128:(j + 1) * 128],
                                Wa[:, c, 2048 + fg * 512:2048 + (fg + 1) * 512], start=(c == 0), stop=(c == 7)),
                                reads=[BWa, BhnT], writes=[Bpq])
                        qo, Bqo = qos.next()
                        if fg < 2:
                            P.op("dve", lambda e, pq=pq, qo=qo: e.tensor_copy(qo[:], pq[:]), reads=[Bpq], writes=[Bqo])
                            P.dma(None, v_s[t0 + j * 128:t0 + (j + 1) * 128, fg * 512:(fg + 1) * 512], qo[:],
                                  reads=[Bqo], writes=[B_scr["v"]])
                        else:
                            P.op("act", lambda e, pq=pq, qo=qo: e.activation(qo[:], pq[:], AF.Silu),
                                 reads=[Bpq], writes=[Bqo])
                            P.dma(None, sg_s[t0 + j * 128:t0 + (j + 1) * 128, (fg - 2) * 512:(fg - 1) * 512], qo[:],
                                  reads=[Bqo], writes=[B_scr["sg"]])
            P.flush()

        if "stopA" in debug:
            return nc

        with contextlib.ExitStack() as st:
            P.only_sp = "onlysp" in debug
            Wo, BWo = load_weight_bf16(st, "Wo", na_w_out, 8, D, stage_cols=1024)
            ebg, Bebg = mk(st, "ebg", [128, NH_A, 768], BF16)
            P.dma("sp", ebg[:], eb_s[0, :, :, :], reads=[B_scr["eb"]], writes=[Bebg])
            ebsp = Rot([mk(st, "ebsp%d" % i, [128, NH_A, 768], BF16) for i in range(2)])
            NR = 8
            kring = [mk(st, "kring%d" % i, [128, 8, 128], BF16) for i in range(NR)]
            vring = [mk(st, "vring%d" % i, [128, NH_A, 128], BF16) for i in range(NR)]
            for i in range(NR):
                P.op("pool", lambda e, i=i: e.memset(vring[i][0][:, :, 64:128], 1.0), writes=[vring[i][1]])
            qts = Rot([mk(st, "qtB%d" % i, [128, 8, 128], BF16) for i in range(2)])
            sgs = Rot([mk(st, "sgB%d" % i, [128, D], BF16) for i in range(2)])
            xbs = Rot([mk(st, "xB%d" % i, [128, D], F32) for i in range(3)])
            Es = Rot([mk(st, "EB%d" % i, [128, 768], BF16) for i in range(4)])
            Pms = Rot([mk(st, "PmB%d" % i, [128, 768], BF16) for i in range(4)])
            ogs = Rot([mk(st, "ogB%d" % i, [128, D], BF16) for i in range(2)])
            ogTs = Rot([mk(st, "ogTB%d" % i, [128, 8, 128], BF16) for i in range(2)])
            h1os = Rot([mk(st, "h1oB%d" % i, [128, D], F32) for i in range(2)])
            rcs = Rot([mk(st, "rcB%d" % i, [128, 1], F32) for i in range(4)])
            onrm = Rot([mk(st, "onB%d" % i, [128, 64], BF16) for i in range(4)])
            psts = Rot([mk(st, "pstB%d" % i, [128, 1024], F32, psum=True) for i in range(3)])
            pOs = Rot([mk(st, "pOB%d" % i, [128, 128], F32, psum=True) for i in range(1)])
            po_t, Bpo_t = mk(st, "poB", [128, 512], F32, psum=True)
            pT2, BpT2 = po_t.bitcast(BF16), Bpo_t
            pos = Rot([(po_t, Bpo_t)])

            bunits = []
            for (seg0, nsegs) in ((0, GRP), (GRP, 1)):
                nb = nsegs * NBS
                for i in range(nb):
                    bunits.append((seg0, nb, i))
            bctx = {}
            loaded = set()

            def B0(u):
                seg0, nb, i = bunits[u]
                c_ = {}
                bctx[u] = c_
                tokbase = seg0 * seg_t
                chbase = tokbase // 128
                seg = seg0 + i // NBS
                p = i % NBS
                lo, hi = na_chunk_range(i, nb, NBS)
                tid = na_table_id(seg, p, NBS)
                for m in range(lo, hi + 1):
                    if (chbase + m) in loaded:
                        continue
                    loaded.add(chbase + m)
                    kt, Bkt = kring[(chbase + m) % NR]
                    vt, Bvt = vring[(chbase + m) % NR]
                    P.dma(None, kt[:], kT_s[chbase + m, :, :, :], reads=[B_scr["kT"]], writes=[Bkt])
                    tk = tokbase + m * 128
                    P.dma(None, vt[:, :, 0:64], v_s[tk:tk + 128, :].rearrange("p (h d) -> p h d", h=NH_A),
                          reads=[B_scr["v"]], writes=[Bvt])
                tq = tokbase + i * 128
                qt, Bqt = qts.next()
                sg, Bsg = sgs.next()
                P.dma(None, qt[:], qT_s[chbase + i, :, :, :], reads=[B_scr["qT"]], writes=[Bqt])
                P.dma(None, sg[:], sg_s[tq:tq + 128, :], reads=[B_scr["sg"]], writes=[Bsg])
                if tid == 0:
                    ebt, Bebt = ebg, Bebg
                else:
                    ebt, Bebt = ebsp.next()
                    P.dma(None, ebt[:], eb_s[tid, :, :, :], reads=[B_scr["eb"]], writes=[Bebt])
                c_.update(qt=qt, Bqt=Bqt, sg=sg, Bsg=Bsg, ebt=ebt, Bebt=Bebt, lo=lo, ns=hi - lo + 1, chbase=chbase, tq=tq)

            def B1(u):
                c_ = bctx[u]
                qt, Bqt, sg, Bsg, ebt, Bebt = c_["qt"], c_["Bqt"], c_["sg"], c_["Bsg"], c_["ebt"], c_["Bebt"]
                lo, ns, chbase = c_["lo"], c_["ns"], c_["chbase"]
                og, Bog = ogs.next()
                c_["og"], c_["Bog"] = og, Bog

                def s_stage(h):
                    hp, po = h // 2, (h % 2) * 64
                    pst, Bpst = psts.next()
                    for s in range(ns):
                        kt, Bkt = kring[(chbase + lo + s) % NR]
                        P.op("pe", lambda e: e.matmul(
                            pst[:, s * 128:(s + 1) * 128], kt[po:po + 64, hp, :], qt[po:po + 64, hp, :],
                            start=True, stop=True), reads=[Bkt, Bqt], writes=[Bpst])
                    E, BE = Es.next()
                    P.op("act", lambda e: e.activation(E[:, 0:ns * 128], pst[:, 0:ns * 128], AF.Exp),
                         reads=[Bpst], writes=[BE])
                    Pm, BPm = Pms.next()
                    P.op("pool" if h % 2 == 0 else "dve", lambda e: e.tensor_tensor(
                        Pm[:, 0:ns * 128], E[:, 0:ns * 128], ebt[:, h, 0:ns * 128], ALU.mult),
                        reads=[BE, Bebt], writes=[BPm])
                    return Pm, BPm

                def pv_stage(h, Pm, BPm):
                    pO, BpO = pOs.next()
                    for s in range(ns):
                        vt, Bvt = vring[(chbase + lo + s) % NR]
                        P.op("pe", lambda e: e.matmul(
                            pO[:, 0:72], Pm[:, s * 128:(s + 1) * 128], vt[:, h, 0:72],
                            start=(s == 0), stop=(s == ns - 1)), reads=[BPm, Bvt], writes=[BpO])
                    rc, Brc = rcs.next()
                    P.op("dve", lambda e: e.reciprocal(rc[:], pO[:, 64:65]), reads=[BpO], writes=[Brc])
                    on_, Bon_ = onrm.next()
                    P.op("dve", lambda e: e.tensor_scalar(on_[:], pO[:, 0:64], rc[:, 0:1], None, ALU.mult),
                         reads=[BpO, Brc], writes=[Bon_])
                    P.op("dve", lambda e: e.tensor_tensor(
                        og[:, h * 64:(h + 1) * 64], on_[:], sg[:, h * 64:(h + 1) * 64], ALU.mult),
                        reads=[Bon_, Bsg], writes=[Bog])

                pend = []
                for h in range(NH_A):
                    pend.append((h, s_stage(h)))
                    if len(pend) > 2:
                        h_, (Pm_, BPm_) = pend.pop(0)
                        pv_stage(h_, Pm_, BPm_)
                for h_, (Pm_, BPm_) in pend:
                    pv_stage(h_, Pm_, BPm_)

            def B2(u):
                c_ = bctx[u]
                og, Bog = c_["og"], c_["Bog"]
                tq = c_["tq"]
                xb, Bxb = xbs.next()
                c_["xb"], c_["Bxb"] = xb, Bxb
                P.dma(None, xb[:], x_in[tq:tq + 128, :], reads=[B_x], writes=[Bxb])
                for c in range(8):
                    P.op("pe", lambda e: e.transpose(
                        pT2[:, c * 128:(c + 1) * 128], og[:, c * 128:(c + 1) * 128], identb[:]),
                        reads=[Bog, Bidb], writes=[BpT2])
                ogT, BogT = ogTs.next()
                c_["ogT"], c_["BogT"] = ogT, BogT
                P.op("act", lambda e: e.activation(ogT[:].rearrange("p c t -> p (c t)"), pT2[:], AF.Copy),
                     reads=[BpT2], writes=[BogT])

            def B3(u):
                c_ = bctx.pop(u)
                ogT, BogT, xb, Bxb, tq = c_["ogT"], c_["BogT"], c_["xb"], c_["Bxb"], c_["tq"]
                h1o, Bh1o = h1os.next()
                for half in range(2):
                    po_, Bpo = pos.next()
                    for c in range(8):
                        P.op("pe", lambda e: e.matmul(
                            po_[:], ogT[:, c, :], Wo[:, c, half * 512:(half + 1) * 512],
                            start=(c == 0), stop=(c == 7)), reads=[BogT, BWo], writes=[Bpo])
                    P.op("dve", lambda e: e.tensor_tensor(
                        h1o[:, half * 512:(half + 1) * 512], po_[:], xb[:, half * 512:(half + 1) * 512], ALU.add),
                        reads=[Bpo, Bxb], writes=[Bh1o])
                P.dma(None, h1_s[tq:tq + 128, :], h1o[:], reads=[Bh1o], writes=[B_scr["h1"]])

            nBU = len(bunits)
            for t in range(nBU + 3):
                for fn_, off_ in ((B3, 3), (B2, 2), (B1, 1), (B0, 0)):
                    if 0 <= t - off_ < nBU:
                        fn_(t - off_)
            P.flush()

        if "stopB" in debug:
            return nc

        with contextlib.ExitStack() as st:
            Wg, BWg = load_weight_bf16(st, "Wg", g_w_in, 8, 6176, stage_cols=1544)
            lnw, Blnw = mk(st, "lnwC", [128, D], F32)
            P.dma("sp", lnw[:], ln_w_bc[1:2, :].partition_broadcast(128), writes=[Blnw])
            cw, Bcw = mk(st, "cwC", [128, 32, 5], F32)
            P.dma("sp", cw[:], g_cw[:, :, :], writes=[Bcw])
            dtb, Bdtb = mk(st, "dtbC", [128, 16], F32)
            nA, BnA = mk(st, "nAC", [128, 16], F32)
            P.dma("sp", dtb[:], g_dtb[0:1, :].partition_broadcast(128), writes=[Bdtb])
            P.dma("sp", nA[:], g_alog[0:1, :].partition_broadcast(128), writes=[BnA])
            P.op("act", lambda e: e.activation(nA[:], nA[:], AF.Exp), reads=[BnA], writes=[BnA])
            P.op("dve", lambda e: e.tensor_scalar(nA[:], nA[:], -1.0, None, ALU.mult), reads=[BnA], writes=[BnA])

            W_ = seg_t + 4
            hnT, BhnT = mk(st, "hnTC", [128, 8, W_], BF16)
            xts = Rot([mk(st, "xtC%d" % i, [128, D], F32) for i in range(2)])
            xns = Rot([mk(st, "xnC%d" % i, [128, D], BF16) for i in range(2)])
            sss = Rot([mk(st, "ssC%d" % i, [128, 2], F32) for i in range(4)])
            pTs = Rot([mk(st, "pTC%d" % i, [128, 1024], BF16, psum=True) for i in range(2)])
            pps = Rot([mk(st, "ppC%d" % i, [128, 512], F32, psum=True) for i in range(4)])
            psm, Bpsm = mk(st, "psmC", [128, 512], F32, psum=True)
            HT = seg_t // 2
            NW = min(512, HT)
            NN = HT // NW
            NJ = HT // 128
            projs = Rot([mk(st, "projC%d" % i, [128, HT + 4], F32) for i in range(3)])
            accs = Rot([mk(st, "accC%d" % i, [128, HT], F32) for i in range(3)])
            ys = Rot([mk(st, "yC%d" % i, [128, HT], F32) for i in range(3)])
            ynb = Rot([mk(st, "ynC%d" % i, [128, HT], BF16) for i in range(5)])
            tks = Rot([mk(st, "tkC%d" % i, [128, 8, 128], BF16) for i in range(2)])
            zos = Rot([mk(st, "zoC%d" % i, [128, 512], BF16) for i in range(2)])
            bgt = Rot([mk(st, "bgC%d" % i, [128, 32], F32) for i in range(2)])
            gts = Rot([mk(st, "gtC%d" % i, [128, 16], F32) for i in range(2)])
            bgall = mk(st, "bgallC", [128, NCS, 32], F32)
            gtall = mk(st, "gtallC", [128, 2, NCS, 8], F32)
            grall = mk(st, "grallC", [NCS * 8, 2, 128], F32)

            for seg in range(NSEG):
                T0 = seg * seg_t
                has_prev = (seg not in (0, GRP))
                has_next = (seg not in (GRP - 1, GRP))
                s1ctx = {}

                def S1a(j):
                    xt, Bxt = xts.next()
                    xn, Bxn = xns.next()
                    ss, Bss = sss.next()
                    s1ctx[j] = (xn, Bxn)
                    if j < NCS:
                        P.dma(None, xt[:], h1_s[T0 + j * 128:T0 + (j + 1) * 128, :], reads=[B_scr["h1"]], writes=[Bxt])
                        rmsnorm_rows(xn, xt[:, :], 128, lnw, xn[:, :], ss[:, 0:1], ss[:, 1:2], [Bxt], Bss, Bxn, Bxn, Blnw)
                    else:
                        P.op("pool", lambda e: e.memset(xt[0:32, :], 0.0), writes=[Bxt])
                        if has_prev:
                            P.dma(None, xt[0:2, :], h1_s[T0 - 2:T0, :], reads=[B_scr["h1"]], writes=[Bxt])
                        if has_next:
                            P.dma(None, xt[2:4, :], h1_s[T0 + seg_t:T0 + seg_t + 2, :], reads=[B_scr["h1"]], writes=[Bxt])
                        rmsnorm_rows(xn, xt[0:4, :], 4, lnw, xn[0:4, :], ss[:, 0:1], ss[:, 1:2], [Bxt], Bss, Bxn, Bxn,
                                     Blnw, flag=flagt)

                def S1b(j):
                    xn, Bxn = s1ctx.pop(j)
                    np_ = 128 if j < NCS else 4
                    pT, BpT = pTs.next()
                    for c in range(8):
                        P.op("pe", lambda e: e.transpose(
                            pT[:, c * 128:c * 128 + np_], xn[0:np_, c * 128:(c + 1) * 128], identb[0:np_, 0:np_]),
                            reads=[Bxn, Bidb], writes=[BpT])
                    if j < NCS:
                        P.op("act", lambda e: e.activation(
                            hnT[:, :, 2 + j * 128:2 + (j + 1) * 128], pT[:].rearrange("p (c t) -> p c t", c=8), AF.Copy),
                            reads=[BpT], writes=[BhnT])
                    else:
                        pv_ = pT[:].rearrange("p (c t) -> p c t", c=8)
                        P.op("dve", lambda e: e.tensor_copy(hnT[:, :, 0:2], pv_[:, :, 0:2]), reads=[BpT], writes=[BhnT])
                        P.op("dve", lambda e: e.tensor_copy(hnT[:, :, seg_t + 2:seg_t + 4], pv_[:, :, 2:4]),
                             reads=[BpT], writes=[BhnT])

                for t in range(NCS + 2):
                    if 0 <= t - 1 <= NCS:
                        S1b(t - 1)
                    if t <= NCS:
                        S1a(t)
                cunits = [(hf, cc) for hf in range(2) for cc in range(32)]
                cctx = {}

                def PJ(u):
                    hf, cc = cunits[u]
                    c_ = {}
                    cctx[u] = c_
                    col0 = hf * HT
                    proj, Bproj = projs.next()
                    c_["proj"], c_["Bproj"] = proj, Bproj
                    for (c0, d0) in ((col0, 0), (col0 + HT + 2, 2)):
                        for c in range(8):
                            P.op("pe", lambda e: e.matmul(
                                psm[:, d0:d0 + 2], Wg[:, c, cc * 128:(cc + 1) * 128], hnT[:, c, c0:c0 + 2],
                                start=(c == 0), stop=(c == 7)), reads=[BWg, BhnT], writes=[Bpsm])
                    P.op("act", lambda e: e.activation(proj[:, 0:2], psm[:, 0:2], AF.Copy), reads=[Bpsm], writes=[Bproj])
                    P.op("act", lambda e: e.activation(proj[:, HT + 2:HT + 4], psm[:, 2:4], AF.Copy),
                         reads=[Bpsm], writes=[Bproj])
                    for n in range(NN):
                        pp, Bpp = pps.next()
                        for c in range(8):
                            P.op("pe", lambda e: e.matmul(
                                pp[:, 0:NW], Wg[:, c, cc * 128:(cc + 1) * 128],
                                hnT[:, c, col0 + 2 + n * NW:col0 + 2 + (n + 1) * NW],
                                start=(c == 0), stop=(c == 7)), reads=[BWg, BhnT], writes=[Bpp])
                        P.op("act", lambda e: e.activation(proj[:, 2 + n * NW:2 + (n + 1) * NW], pp[:, 0:NW], AF.Copy),
                             reads=[Bpp], writes=[Bproj])

                def CV(u):
                    hf, cc = cunits[u]
                    c_ = cctx[u]
                    proj, Bproj = c_["proj"], c_["Bproj"]
                    acc, Bacc = accs.next()
                    c_["acc"], c_["Bacc"] = acc, Bacc
                    P.op("dve", lambda e: e.tensor_scalar(acc[:], proj[:, 0:HT], cw[:, cc, 0:1], None, ALU.mult),
                         reads=[Bproj, Bcw], writes=[Bacc])
                    for k in range(1, 5):
                        P.op("dve", lambda e: e.scalar_tensor_tensor(
                            acc[:], proj[:, k:k + HT], cw[:, cc, k:k + 1], acc[:], ALU.mult, ALU.add),
                            reads=[Bproj, Bcw, Bacc], writes=[Bacc])
                    if cc < 16:
                        y, By = ys.next()
                        c_["y"], c_["By"] = y, By
                        P.op("act", lambda e: e.activation(y[:], acc[:], AF.Silu), reads=[Bacc], writes=[By])
                    else:
                        yn, Byn = ynb.next()
                        c_["yn"], c_["Byn"] = yn, Byn
                        P.op("act", lambda e: e.activation(yn[:], acc[:], AF.Silu), reads=[Bacc], writes=[Byn])

                def NM1(u):
                    hf, cc = cunits[u]
                    if cc >= 16:
                        return
                    c_ = cctx[u]
                    acc, Bacc, y, By = c_["acc"], c_["Bacc"], c_["y"], c_["By"]
                    P.op("pool", lambda e: e.tensor_tensor(acc[:], y[:], y[:], ALU.mult), reads=[By], writes=[Bacc])
                    for n in range(NN):
                        pp, Bpp = pps.next()
                        P.op("pe", lambda e: e.matmul(pp[:, 0:NW], onesf, acc[:, n * NW:(n + 1) * NW], start=True, stop=True),
                             reads=[Bc, Bacc], writes=[Bpp])
                        P.op("act", lambda e: e.activation(acc[:, n * NW:(n + 1) * NW], pp[:, 0:NW], AF.Ln, bias=L2_EPS),
                             reads=[Bpp], writes=[Bacc])
                    P.op("act", lambda e: e.activation(acc[:], acc[:], AF.Exp, scale=-0.5), reads=[Bacc], writes=[Bacc])

                def NM2(u):
                    hf, cc = cunits[u]
                    if cc >= 16:
                        return
                    c_ = cctx[u]
                    acc, Bacc, y, By = c_["acc"], c_["Bacc"], c_["y"], c_["By"]
                    yn, Byn = ynb.next()
                    c_["yn"], c_["Byn"] = yn, Byn
                    sc = float(DK ** -0.5) if cc < 8 else 1.0
                    P.op("dve", lambda e: e.scalar_tensor_tensor(yn[:], y[:], sc, acc[:], ALU.mult, ALU.mult),
                         reads=[By, Bacc], writes=[Byn])
                    dst, bd = (gq_s, B_scr["gq"]) if cc < 8 else (gk_s, B_scr["gk"])
                    ch0 = (T0 + hf * HT) // 128
                    P.dma(None, dst[ch0:ch0 + NJ, :, cc % 8, :].rearrange("j p t -> p j t"),
                          yn[:].rearrange("p (j t) -> p j t", j=NJ), reads=[Byn], writes=[bd])

                def TR(u):
                    hf, cc = cunits[u]
                    c_ = cctx.pop(u)
                    if cc < 8:
                        return
                    yn, Byn = c_["yn"], c_["Byn"]
                    pT, BpT = pTs.next()
                    for jj in range(NJ):
                        P.op("pe", lambda e: e.transpose(pT[:, jj * 128:(jj + 1) * 128], yn[:, jj * 128:(jj + 1) * 128], identb[:]),
                             reads=[Byn, Bidb], writes=[BpT])
                    tk, Btk = tks.next()
                    P.op("act", lambda e: e.activation(
                        tk[:, 0:NJ, :], pT[:, 0:NJ * 128].rearrange("p (j t) -> p j t", j=NJ), AF.Copy),
                        reads=[BpT], writes=[Btk])
                    tA = T0 + hf * HT
                    if cc < 16:
                        dd = gkt_s[tA:tA + HT, cc - 8, :].rearrange("(j p) t -> p j t", p=128)
                        bd = B_scr["gkt"]
                    else:
                        hh_, half_ = (cc - 16) // 2, (cc - 16) % 2
                        dd = gv_s[tA:tA + HT, hh_, half_ * 128:(half_ + 1) * 128].rearrange("(j p) t -> p j t", p=128)
                        bd = B_scr["gv"]
                    P.dma(None, dd, tk[:, 0:NJ, :], reads=[Btk], writes=[bd])

                nCU = len(cunits)
                stages_c = ((TR, 4), (NM2, 3), (PJ, 0), (NM1, 2), (CV, 1))
                for t in range(nCU + 4):
                    for fn_, off_ in stages_c:
                        if 0 <= t - off_ < nCU:
                            fn_(t - off_)
                psm_v = psm[:, 0:NCS * 32].rearrange("p (j c) -> p j c", j=NCS)
                for j in range(NCS):
                    tA = T0 + j * 128
                    for n in range(4):
                        pp, Bpp = pps.next()
                        for c in range(8):
                            P.op("pe", lambda e: e.matmul(
                                pp[:], hnT[:, c, 2 + j * 128:2 + (j + 1) * 128],
                                Wg[:, c, 4096 + n * 512:4096 + (n + 1) * 512], start=(c == 0), stop=(c == 7)),
                                reads=[BWg, BhnT], writes=[Bpp])
                        zo, Bzo = zos.next()
                        P.op("act", lambda e: e.activation(zo[:], pp[:], AF.Silu), reads=[Bpp], writes=[Bzo])
                        P.dma(None, z_s[tA:tA + 128, n * 512:(n + 1) * 512], zo[:], reads=[Bzo], writes=[B_scr["z"]])
                    for c in range(8):
                        P.op("pe", lambda e: e.matmul(
                            psm[:, j * 32:(j + 1) * 32], hnT[:, c, 2 + j * 128:2 + (j + 1) * 128], Wg[:, c, 6144:6176],
                            start=(c == 0), stop=(c == 7)), reads=[BWg, BhnT], writes=[Bpsm])
                bga, Bbga = bgall
                gta, Bgta = gtall
                P.op("act", lambda e: e.activation(bga[:, :, 0:16], psm_v[:, :, 0:16], AF.Sigmoid), reads=[Bpsm], writes=[Bbga])
                P.op("dve", lambda e: e.tensor_tensor(
                    gta[:].rearrange("p d j h -> p j d h"), psm_v[:, :, 16:32].rearrange("p j (d h) -> p j d h", d=2),
                    dtb[:].rearrange("p (d h) -> p d h", d=2).unsqueeze(1).to_broadcast([128, NCS, 2, 8]), ALU.add),
                     reads=[Bpsm, Bdtb], writes=[Bgta])
                P.op("act", lambda e: e.activation(gta[:], gta[:], AF.Exp), reads=[Bgta], writes=[Bgta])
                P.op("act", lambda e: e.activation(gta[:], gta[:], AF.Ln, bias=1.0), reads=[Bgta], writes=[Bgta])
                P.op("dve", lambda e: e.tensor_tensor(
                    gta[:], gta[:], nA[:].rearrange("p (d h) -> p d h", d=2).unsqueeze(2).to_broadcast([128, 2, NCS, 8]), ALU.mult),
                     reads=[Bgta, BnA], writes=[Bgta])
                pp, Bpp = pps.next()
                P.op("pe", lambda e: e.matmul(pp[:, 0:NCS * 8], trif, gta[:, 0, :, :].rearrange("p j h -> p (j h)"), start=True, stop=True),
                     reads=[Bc, Bgta], writes=[Bpp])
                P.op("pe", lambda e: e.matmul(pp[:, NCS * 8:2 * NCS * 8], trib, gta[:, 1, :, :].rearrange("p j h -> p (j h)"), start=True, stop=True),
                     reads=[Bc, Bgta], writes=[Bpp])
                P.op("dve", lambda e: e.tensor_copy(
                    bga[:, :, 16:32].rearrange("p j (d h) -> p d j h", d=2),
                    pp[:, 0:2 * NCS * 8].rearrange("p (d j h) -> p d j h", d=2, j=NCS)), reads=[Bpp], writes=[Bbga])
                P.dma(None, bg_s[T0:T0 + seg_t, :].rearrange("(j p) c -> p j c", p=128), bga[:], reads=[Bbga],
                      writes=[B_scr["bg"]])
                pr_, Bpr_ = pps.next()
                P.op("pe", lambda e: e.matmul(pr_[0:NCS * 8, 0:128], gta[:, 0, :, :].rearrange("p j h -> p (j h)"), trif, start=True, stop=True),
                     reads=[Bc, Bgta], writes=[Bpr_])
                P.op("pe", lambda e: e.matmul(pr_[0:NCS * 8, 128:256], gta[:, 1, :, :].rearrange("p j h -> p (j h)"), trib, start=True, stop=True),
                     reads=[Bc, Bgta], writes=[Bpr_])
                gra, Bgra = grall
                P.op("dve", lambda e: e.tensor_copy(
                    gra[:], pr_[0:NCS * 8, 0:256].rearrange("p (d t) -> p d t", d=2)), reads=[Bpr_], writes=[Bgra])
                for j in range(NCS):
                    tA = T0 + j * 128
                    P.dma(None, gcr_s[:, :, tA:tA + 128].rearrange("d h t -> h d t"), gra[j * 8:(j + 1) * 8, :, :],
                          reads=[Bgra], writes=[B_scr["gcr"]])
            P.flush()

        if "stopC" in debug:
            return nc

        with contextlib.ExitStack() as st:
            HB = 4
            S32 = [mk(st, "S32_%d" % d, [128, 8, 256], F32) for d in range(2)]
            Sbf = [mk(st, "Sbf_%d" % d, [128, 8, 256], BF16) for d in range(2)]
            S32B = [[Buf() for _ in range(8)] for d in range(2)]
            SbfB = [[Buf() for _ in range(8)] for d in range(2)]
            kTs = Rot([mk(st, "kTD%d" % i, [128, 8, 128], BF16) for i in range(2)])
            qTs = Rot([mk(st, "qTD%d" % i, [128, 8, 128], BF16) for i in range(2)])
            kts = Rot([mk(st, "ktD%d" % i, [128, 8, 128], BF16) for i in range(2)])
            vts = Rot([mk(st, "vtD%d" % i, [128, 8, 256], BF16) for i in range(2)])
            bgs = Rot([mk(st, "bgD%d" % i, [128, 32], F32) for i in range(3)])
            GRs = Rot([mk(st, "GRD%d" % i, [128, 8, 128], F32) for i in range(2)])
            smalls = Rot([mk(st, "smD%d" % i, [128, 4, 8], F32) for i in range(2)])
            tmps = Rot([mk(st, "tmpD%d" % i, [128, 8, 128], F32) for i in range(2)])
            DTs = Rot([mk(st, "DTD%d" % i, [128, 8, 128], F32) for i in range(1)])
            DSs = Rot([mk(st, "DSD%d" % i, [128, 8, 128], F32) for i in range(1)])
            ERs = Rot([mk(st, "ERD%d" % i, [128, 8, 128], F32) for i in range(2)])
            Aqs = Rot([mk(st, "AqD%d" % i, [128, 8, 128], BF16) for i in range(2)])
            qgs = Rot([mk(st, "qgD%d" % i, [128, 8, 128], BF16) for i in range(2)])
            rws = Rot([mk(st, "rwD%d" % i, [128, 8, 128], F32) for i in range(2)])
            kds = Rot([mk(st, "kdD%d" % i, [128, 8, 128], BF16) for i in range(2)])
            vbs = Rot([mk(st, "vbD%d" % i, [128, 8, 256], F32) for i in range(2)])
            Ab = Rot([mk(st, "AbD%d" % i, [128, HB, 128], F32) for i in range(4)])
            Nb = Rot([mk(st, "NbD%d" % i, [128, HB, 128], F32) for i in range(4)])
            ZP0 = Rot([mk(st, "ZP0D%d" % i, [128, HB, 2, 128], F32) for i in range(4)])
            ZPm = Rot([mk(st, "ZPmD%d" % i, [128, HB, 2, 128], F32) for i in range(4)])
            Yb = Rot([mk(st, "YbD%d" % i, [128, HB, 128], F32) for i in range(4)])
            P32 = Rot([mk(st, "P32D%d" % i, [128, HB, 128], F32) for i in range(4)])
            nWs = Rot([mk(st, "nWD%d" % i, [128, HB, 128], BF16) for i in range(2)])
            vns = Rot([mk(st, "vnD%d" % i, [128, 256], BF16) for i in range(4)])
            oos = Rot([mk(st, "ooD%d" % i, [128, 8, 256], F32) for i in range(2)])
            pM = [mk(st, "pMD%d" % i, [128, HB * 128], F32, psum=True) for i in range(2)]
            pZs = [mk(st, "pZD%d" % i, [128, 512], F32, psum=True) for i in range(2)]
            pYs = [mk(st, "pYD%d" % i, [128, 512], F32, psum=True) for i in range(2)]
            pPs = [mk(st, "pPD%d" % i, [128, HB * 128], F32, psum=True) for i in range(2)]

            units = []
            for (seg0, nsegs) in ((0, GRP), (GRP, 1)):
                nchg = nsegs * NCS
                chb = seg0 * NCS
                for step in range(nchg):
                    for d in range(2):
                        cl = step if d == 0 else nchg - 1 - step
                        first = (step == 0)
                        carry = (not first) and ((cl % NCS == 0) if d == 0 else (cl % NCS == NCS - 1))
                        units.append((chb + cl, d, first, carry))
            ctxs = {}
            h4 = lambda ap: ap.rearrange("p (h t) -> p h t", h=HB)

            def L_stage(u):
                ch, d, first, carry = units[u]
                c = {}
                ctxs[u] = c
                tA = ch * 128
                c["kT"], c["BkT"] = kTs.next()
                c["qT"], c["BqT"] = qTs.next()
                c["kt"], c["Bkt"] = kts.next()
                c["vt"], c["Bvt"] = vts.next()
                c["bg"], c["Bbg"] = bgs.next()
                c["GR"], c["BGR"] = GRs.next()
                P.dma(None, c["kT"][:], gk_s[ch, :, :, :], reads=[B_scr["gk"]], writes=[c["BkT"]])
                P.dma(None, c["qT"][:], gq_s[ch, :, :, :], reads=[B_scr["gq"]], writes=[c["BqT"]])
                P.dma(None, c["kt"][:], gkt_s[tA:tA + 128, :, :], reads=[B_scr["gkt"]], writes=[c["Bkt"]])
                P.dma(None, c["vt"][:], gv_s[tA:tA + 128, :, :], reads=[B_scr["gv"]], writes=[c["Bvt"]])
                P.dma(None, c["bg"][:], bg_s[tA:tA + 128, :], reads=[B_scr["bg"]], writes=[c["Bbg"]])
                for h in range(8):
                    P.dma(None, c["GR"][:, h, :], gcr_s[d, h:h + 1, tA:tA + 128].partition_broadcast(128),
                          reads=[B_scr["gcr"]], writes=[c["BGR"]])

            def Q_stage(u):
                ch, d, first, carry = units[u]
                c = ctxs[u]
                last = 127 if d == 0 else 0
                kT, BkT, qT, BqT, kt, Bkt, vt, Bvt = c["kT"], c["BkT"], c["qT"], c["BqT"], c["kt"], c["Bkt"], c["vt"], c["Bvt"]
                bg, Bbg, GR, BGR = c["bg"], c["Bbg"], c["GR"], c["BGR"]
                beta = bg[:, d * 8:(d + 1) * 8]
                gc = bg[:, 16 + d * 8:16 + (d + 1) * 8]
                sm, Bsm = smalls.next()
                P.op("act", lambda e: e.activation(sm[:, 0, :], gc, AF.Exp), reads=[Bbg], writes=[Bsm])
                P.op("dve", lambda e: e.tensor_tensor(sm[:, 0, :], sm[:, 0, :], beta, ALU.mult), reads=[Bsm, Bbg], writes=[Bsm])
                P.op("dve", lambda e: e.tensor_tensor(sm[:, 1, :], GR[:, :, last], gc, ALU.subtract),
                     reads=[BGR, Bbg], writes=[Bsm])
                P.op("act", lambda e: e.activation(sm[:, 1, :], sm[:, 1, :], AF.Exp), reads=[Bsm], writes=[Bsm])
                yield
                tmp, Btmp = tmps.next()
                DT, BDT = DTs.next()
                DS, BDS = DSs.next()
                ER, BER = ERs.next()
                c["ER"], c["BER"] = ER, BER
                for h in range(8):
                    P.op("dve", lambda e, h=h: e.scalar_tensor_tensor(
                        tmp[:, h, :], GR[:, h, :], gc[:, h:h + 1], maskD[d], ALU.subtract, ALU.add),
                        reads=[BGR, Bbg, Bc], writes=[Btmp])
                    if h % 2 == 1:
                        yield
                    if h % 4 == 3:
                        P.op("act", lambda e: e.activation(DT[:, h - 3:h + 1, :], tmp[:, h - 3:h + 1, :], AF.Exp),
                             reads=[Btmp], writes=[BDT])
                tmp2, Btmp2 = tmps.next()
                for h in range(8):
                    P.op("dve", lambda e, h=h: e.scalar_tensor_tensor(
                        tmp2[:, h, :], GR[:, h, :], gc[:, h:h + 1], maskS[d], ALU.subtract, ALU.subtract),
                        reads=[BGR, Bbg, Bc], writes=[Btmp2])
                    if h % 2 == 1:
                        yield
                    if h % 4 == 3:
                        P.op("act", lambda e: e.activation(DS[:, h - 3:h + 1, :], tmp2[:, h - 3:h + 1, :], AF.Exp, scale=-1.0),
                             reads=[Btmp2], writes=[BDS])
                for hq in range(2):
                    P.op("act", lambda e: e.activation(ER[:, hq * 4:hq * 4 + 4, :], GR[:, hq * 4:hq * 4 + 4, :], AF.Exp),
                         reads=[BGR], writes=[BER])
                    yield
                qg, Bqg = qgs.next()
                rw, Brw = rws.next()
                kd, Bkd = kds.next()
                vb, Bvb = vbs.next()
                c.update(qg=qg, Bqg=Bqg, rw=rw, Brw=Brw, kd=kd, Bkd=Bkd, vb=vb, Bvb=Bvb)
                P.op("pool", lambda e: e.tensor_tensor(qg[:], qT[:], ER[:], ALU.mult), reads=[BqT, BER], writes=[Bqg])
                P.op("pool", lambda e: e.tensor_tensor(rw[:], kt[:], sm[:, 0, :].unsqueeze(2).to_broadcast([128, 8, 128]),
                                                       ALU.mult), reads=[Bkt, Bsm], writes=[Brw])
                P.op("pool", lambda e: e.tensor_tensor(kd[:], kt[:], sm[:, 1, :].unsqueeze(2).to_broadcast([128, 8, 128]),
                                                       ALU.mult), reads=[Bkt, Bsm], writes=[Bkd])
                P.op("pool", lambda e: e.tensor_tensor(vb[:].bitcast(F32R), vt[:], beta.unsqueeze(2).to_broadcast([128, 8, 256]),
                                                       ALU.mult), reads=[Bvt, Bbg], writes=[Bvb])
                yield
                Aq, BAq = Aqs.next()
                c["Aq"], c["BAq"] = Aq, BAq
                c["hb"] = []
                for hb in range(2):
                    h0 = hb * HB
                    pk, Bpk = pM[0]
                    for hh in range(HB):
                        P.op("pe", lambda e, hh=hh: e.matmul(
                            pk[:, hh * 128:(hh + 1) * 128], kT[:, h0 + hh, :], kT[:, h0 + hh, :], start=True, stop=True),
                            reads=[BkT], writes=[Bpk])
                    A_, BA = Ab.next()
                    yield
                    for hh in range(HB):
                        P.op("dve", lambda e, hh=hh: e.scalar_tensor_tensor(
                            A_[:, hh, :], pk[:, hh * 128:(hh + 1) * 128], beta[:, h0 + hh:h0 + hh + 1], DS[:, h0 + hh, :],
                            ALU.mult, ALU.mult), reads=[Bpk, Bbg, BDS], writes=[BA])
                        if hh % 2 == 1:
                            yield
                    pq, Bpq = pM[1]
                    for hh in range(HB):
                        P.op("pe", lambda e, hh=hh: e.matmul(
                            pq[:, hh * 128:(hh + 1) * 128], kT[:, h0 + hh, :], qT[:, h0 + hh, :], start=True, stop=True),
                            reads=[BkT, BqT], writes=[Bpq])
                    yield
                    P.op("dve", lambda e: e.tensor_tensor(
                        Aq[:, h0:h0 + HB, :], h4(pq[:]), DT[:, h0:h0 + HB, :], ALU.mult), reads=[Bpq, BDT], writes=[BAq])
                    yield
                    for hh in range(HB):
                        P.op("pe", lambda e, hh=hh: e.transpose(pk[:, hh * 128:(hh + 1) * 128], A_[:, hh, :], identf),
                             reads=[BA, Bc], writes=[Bpk])
                    yield
                    N_, BN = Nb.next()
                    P.op("act", lambda e: e.activation(N_[:], h4(pk[:]), AF.Copy), reads=[Bpk], writes=[BN])
                    zp0, Bzp0 = ZP0.next()
                    P.op("dve", lambda e: e.tensor_tensor(
                        zp0[:, :, 1, :].bitcast(F32R), identf.unsqueeze(1).to_broadcast([128, HB, 128]), h4(pk[:]), ALU.subtract),
                        reads=[Bc, Bpk], writes=[Bzp0])
                    c["hb"].append(dict(A=A_, BA=BA, N=N_, BN=BN, zp=zp0, Bzp=Bzp0))
                    yield

            def M_stage(u):
                ch, d, first, carry = units[u]
                c = ctxs[u]
                tA = ch * 128
                last = 127 if d == 0 else 0
                S32t = S32[d][0]
                Sbft = Sbf[d][0]
                BS32h = S32B[d]
                BSbfh = SbfB[d]
                ER, BER = c["ER"], c["BER"]
                qg, Bqg, rw, Brw, kd, Bkd, vb, Bvb, Aq, BAq = (c["qg"], c["Bqg"], c["rw"], c["Brw"], c["kd"], c["Bkd"],
                                                              c["vb"], c["Bvb"], c["Aq"], c["BAq"])
                stt = []
                for hb in range(2):
                    x = c["hb"][hb]
                    stt.append(dict(N=x["N"], BN=x["BN"], Yc=x["A"], BYc=x["BA"], zp=x["zp"], Bzp=x["Bzp"]))
                r32 = lambda ap: ap.bitcast(F32R)
                for lvl in range(0, 7):
                    zpns = [None, None]
                    for hb in range(2):
                        s_ = stt[hb]
                        pRa, BpRa = pZs[hb]
                        pRb, BpRb = pYs[hb]
                        Yc, BYc, zp, Bzp = s_["Yc"], s_["BYc"], s_["zp"], s_["Bzp"]
                        for hh in range(HB):
                            pr, Bpr = (pRa, BpRa) if hh < 2 else (pRb, BpRb)
                            o0 = (hh % 2) * 256
                            if lvl == 0:
                                N_, BN = s_["N"], s_["BN"]
                                P.op("pe", lambda e: e.matmul(pr[:, o0:o0 + 128], Yc[:, hh, :], N_[:, hh, :],
                                                              start=True, stop=True), reads=[BYc, BN], writes=[Bpr])
                            elif lvl < 6:
                                P.op("pe", lambda e: e.matmul(
                                    pr[:, o0:o0 + 256], r32(Yc[:, hh, :]),
                                    r32(zp[:, hh, :, :].rearrange("p z t -> p (z t)")), start=True, stop=True),
                                    reads=[BYc, Bzp], writes=[Bpr])
                            else:
                                P.op("pe", lambda e: e.matmul(pr[:, o0:o0 + 128], r32(Yc[:, hh, :]), r32(zp[:, hh, 1, :]),
                                                              start=True, stop=True), reads=[BYc, Bzp], writes=[Bpr])
                    yield
                    for hb in range(2):
                        s_ = stt[hb]
                        pRa, BpRa = pZs[hb]
                        pRb, BpRb = pYs[hb]
                        zp, Bzp = s_["zp"], s_["Bzp"]
                        if lvl == 0:
                            zpn, Bzpn = zp, Bzp
                        elif lvl < 6:
                            zpn, Bzpn = ZPm.next()
                        else:
                            zpn, Bzpn = P32.next()
                        zpns[hb] = (zpn, Bzpn)
                        for half, (pr, Bpr) in enumerate(((pRa, BpRa), (pRb, BpRb))):
                            prv = pr[:].rearrange("p (h z t) -> p h z t", h=2, z=2)
                            hs = slice(2 * half, 2 * half + 2)
                            if lvl < 6:
                                P.op("act", lambda e: e.activation(zpn[:, hs, 0, :].bitcast(F32R), prv[:, :, 0, :], AF.Copy),
                                     reads=[Bpr], writes=[Bzpn])
                            if 1 <= lvl < 6:
                                P.op("dve", lambda e: e.tensor_tensor(
                                    zpn[:, hs, 1, :].bitcast(F32R), prv[:, :, 1, :], zp[:, hs, 1, :], ALU.add),
                                    reads=[Bpr, Bzp], writes=[Bzpn])
                            if lvl == 6:
                                P.op("dve", lambda e: e.tensor_tensor(
                                    zpn[:, hs, :].bitcast(F32R), prv[:, :, 0, :], zp[:, hs, 1, :], ALU.add),
                                    reads=[Bpr, Bzp], writes=[Bzpn])
                        if lvl == 6:
                            s_["Pf"], s_["BPf"] = zpn, Bzpn
                    yield
                    if lvl == 6:
                        continue
                    for hb in range(2):
                        pTr, BpTr = pPs[hb]
                        zpn, Bzpn = zpns[hb]
                        for hh in range(HB):
                            P.op("pe", lambda e: e.transpose(pTr[:, hh * 128:(hh + 1) * 128], zpn[:, hh, 0, :], identf),
                                 reads=[Bzpn, Bc], writes=[BpTr])
                    yield
                    for hb in range(2):
                        s_ = stt[hb]
                        pTr, BpTr = pPs[hb]
                        Yn, BYn = Yb.next()
                        if hb == 0:
                            P.op("dve", lambda e: e.tensor_copy(Yn[:].bitcast(F32R), h4(pTr[:])), reads=[BpTr], writes=[BYn])
                        else:
                            P.op("act", lambda e: e.activation(Yn[:].bitcast(F32R), h4(pTr[:]), AF.Copy),
                                 reads=[BpTr], writes=[BYn])
                        s_["Yc"], s_["BYc"] = Yn, BYn
                        s_["zp"], s_["Bzp"] = zpns[hb]
                    yield
                nWl = []
                for hb in range(2):
                    h0 = hb * HB
                    pZ, BpZ = pZs[hb]
                    Pf, BPf = stt[hb]["Pf"], stt[hb]["BPf"]
                    for hh in range(HB):
                        P.op("pe", lambda e, hh=hh: e.matmul(
                            pZ[:, hh * 128:(hh + 1) * 128], rw[:, h0 + hh, :], Pf[:, hh, :], start=True, stop=True),
                            reads=[Brw, BPf], writes=[BpZ])
                    nW, BnW = nWs.next()
                    P.op("act", lambda e: e.activation(nW[:], h4(pZ[:]), AF.Copy, scale=-1.0), reads=[BpZ], writes=[BnW])
                    nWl.append((nW, BnW))
                    yield
                if first:
                    P.op("pool", lambda e: e.memset(S32t[:], 0.0), writes=BS32h)
                    P.op("pool", lambda e: e.memset(Sbft[:], 0.0), writes=BSbfh)
                elif carry:
                    P.op("dve", lambda e: e.tensor_scalar(S32t[:], S32t[:], flagt[:, 0:1], None, ALU.mult),
                         reads=BS32h + [Bflag], writes=BS32h)
                    P.op("pool", lambda e: e.tensor_scalar(Sbft[:], Sbft[:], flagt[:, 0:1], None, ALU.mult),
                         reads=BSbfh + [Bflag], writes=BSbfh)
                oo, Boo = oos.next()
                for hh in range(HB):
                    for hb in range(2):
                        h = hb * HB + hh
                        Pf, BPf = stt[hb]["Pf"], stt[hb]["BPf"]
                        nW, BnW = nWl[hb]
                        pv, Bpv = pZs[hb][0][:, 0:256], pZs[hb][1]
                        po_, Bpo = pYs[hb][0][:, 0:256], pYs[hb][1]
                        ps_, Bps = pPs[hb][0][:, 0:256], pPs[hb][1]
                        P.op("pe", lambda e: e.matmul(pv, Pf[:, hh, :].bitcast(F32R), vb[:, h, :].bitcast(F32R),
                                                      start=True, stop=False), reads=[BPf, Bvb], writes=[Bpv])
                        P.op("pe", lambda e: e.matmul(pv, nW[:, hh, :], Sbft[:, h, :], start=False, stop=True),
                             reads=[BnW, BSbfh[h]], writes=[Bpv])
                        vn, Bvn = vns.next()
                        P.op("act", lambda e: e.activation(vn[:], pv, AF.Copy), reads=[Bpv], writes=[Bvn])
                        P.op("pe", lambda e: e.matmul(po_, qg[:, h, :], Sbft[:, h, :], start=True, stop=False),
                             reads=[Bqg, BSbfh[h]], writes=[Bpo])
                        P.op("pe", lambda e: e.matmul(po_, Aq[:, h, :], vn[:], start=False, stop=True),
                             reads=[BAq, Bvn], writes=[Bpo])
                        P.op("pe", lambda e: e.matmul(ps_, kd[:, h, :], vn[:], start=True, stop=True),
                             reads=[Bkd, Bvn], writes=[Bps])
                        P.op("act", lambda e: e.activation(oo[:, h, :], po_, AF.Copy), reads=[Bpo], writes=[Boo])
                        P.op("dve", lambda e: e.scalar_tensor_tensor(
                            S32t[:, h, :], S32t[:, h, :], ER[:, h, last:last + 1], ps_, ALU.mult, ALU.add),
                            reads=[BS32h[h], BER, Bps], writes=[BS32h[h]])
                        P.op("pool", lambda e: e.tensor_copy(Sbft[:, h, :], S32t[:, h, :]), reads=[BS32h[h]], writes=[BSbfh[h]])
                        yield
                P.dma(None, o_s[d][tA:tA + 128, :, :], oo[:], reads=[Boo], writes=[B_scr["of" if d == 0 else "ob"]])
                del ctxs[u]

            nU = len(units)
            for t in range(-2, nU):
                if 0 <= t + 2 < nU:
                    L_stage(t + 2)
                gens = []
                if 0 <= t < nU:
                    gens.append(M_stage(t))
                if 0 <= t + 1 < nU:
                    gens.append(Q_stage(t + 1))
                while gens:
                    for g_ in list(gens):
                        try:
                            next(g_)
                        except StopIteration:
                            gens.remove(g_)
            P.flush()

        if "stopD" in debug:
            return nc

        with contextlib.ExitStack() as st:
            Wo2, BWo2 = load_weight_bf16(st, "Wo2", g_w_out, 16, D, stage_cols=1024)
            nwb, Bnwb = mk(st, "nwE", [128, DV], F32)
            P.dma("sp", nwb[:], g_nw[0:1, :].partition_broadcast(128), writes=[Bnwb])
            fwb, Bfwb = mk(st, "fwE", [128, D], F32)
            P.dma("sp", fwb[:], fin_w[0:1, :].partition_broadcast(128), writes=[Bfwb])
            ofs = Rot([mk(st, "ofE%d" % i, [128, 8, 256], F32) for i in range(4)])
            obs = Rot([mk(st, "obE%d" % i, [128, 8, 256], F32) for i in range(3)])
            zts = Rot([mk(st, "zE%d" % i, [128, 8, 256], BF16) for i in range(4)])
            h1s = Rot([mk(st, "h1E%d" % i, [128, D], F32) for i in range(3)])
            jk, Bjk = mk(st, "jkE", [128, 256], BF16)
            jk2, Bjk2 = mk(st, "jk2E", [128, D], BF16)
            sss = Rot([mk(st, "ssE%d" % i, [128, 16], F32) for i in range(4)])
            ons = Rot([mk(st, "onE%d" % i, [128, 8, 256], BF16) for i in range(3)])
            onTs = Rot([mk(st, "onTE%d" % i, [128, 16, 128], BF16) for i in range(3)])
            h2s = Rot([mk(st, "h2E%d" % i, [128, D], F32) for i in range(3)])
            yos = Rot([mk(st, "yoE%d" % i, [128, D], F32) for i in range(2)])
            s2s = Rot([mk(st, "s2E%d" % i, [128, 2], F32) for i in range(2)])
            pTs = Rot([mk(st, "pTE%d" % i, [128, 1024], BF16, psum=True) for i in range(4)])
            pos = Rot([mk(st, "poE%d" % i, [128, 512], F32, psum=True) for i in range(4)])
            ectx = {}

            def E0(j):
                tA = j * 128
                c_ = {}
                ectx[j] = c_
                of, Bof = ofs.next()
                ob, Bob = obs.next()
                zt, Bzt = zts.next()
                c_.update(of=of, Bof=Bof, ob=ob, Bob=Bob, zt=zt, Bzt=Bzt)
                P.dma(None, of[:], o_s[0][tA:tA + 128, :, :], reads=[B_scr["of"]], writes=[Bof])
                P.dma(None, ob[:], o_s[1][tA:tA + 128, :, :], reads=[B_scr["ob"]], writes=[Bob])
                P.dma(None, zt[:], z_s[tA:tA + 128, :].rearrange("p (h d) -> p h d", h=8), reads=[B_scr["z"]], writes=[Bzt])

            def E1(j):
                c_ = ectx[j]
                of, Bof, ob, Bob = c_["of"], c_["Bof"], c_["ob"], c_["Bob"]
                P.op("pool", lambda e: e.tensor_tensor(of[:], of[:], ob[:], ALU.add), reads=[Bof, Bob], writes=[Bof])
                ss, Bss = sss.next()
                c_["ss"], c_["Bss"] = ss, Bss
                for h in range(8):
                    P.op("act", lambda e: e.activation(
                        jk[:], of[:, h, :], AF.Square, scale=1.0 / 16.0, accum_out=ss[:, h:h + 1]),
                        reads=[Bof], writes=[Bjk, Bss])
                P.op("act", lambda e: e.activation(ss[:, 8:16], ss[:, 0:8], AF.Sqrt, bias=RMS_EPS), reads=[Bss], writes=[Bss])
                P.op("dve", lambda e: e.reciprocal(ss[:, 8:16], ss[:, 8:16]), reads=[Bss], writes=[Bss])

            def E2(j):
                c_ = ectx[j]
                of, Bof, zt, Bzt, ss, Bss = c_["of"], c_["Bof"], c_["zt"], c_["Bzt"], c_["ss"], c_["Bss"]
                on, Bon = ons.next()
                c_["on"], c_["Bon"] = on, Bon
                for h in range(8):
                    P.op("dve", lambda e: e.scalar_tensor_tensor(
                        of[:, h, :], of[:, h, :], ss[:, 8 + h:9 + h], nwb[:], ALU.mult, ALU.mult),
                        reads=[Bof, Bss, Bnwb], writes=[Bof])
                P.op("pool", lambda e: e.tensor_tensor(on[:], of[:], zt[:], ALU.mult), reads=[Bof, Bzt], writes=[Bon])

            def E3(j):
                c_ = ectx[j]
                on, Bon = c_["on"], c_["Bon"]
                tA = j * 128
                h1, Bh1 = h1s.next()
                c_["h1"], c_["Bh1"] = h1, Bh1
                P.dma(None, h1[:], h1_s[tA:tA + 128, :], reads=[B_scr["h1"]], writes=[Bh1])
                onT, BonT = onTs.next()
                c_["onT"], c_["BonT"] = onT, BonT
                for half in range(2):
                    pT, BpT = pTs.next()
                    for c in range(8):
                        cc = half * 8 + c
                        P.op("pe", lambda e: e.transpose(
                            pT[:, c * 128:(c + 1) * 128], on[:, cc // 2, (cc % 2) * 128:(cc % 2 + 1) * 128], identb[:]),
                            reads=[Bon, Bidb], writes=[BpT])
                    P.op("act", lambda e: e.activation(
                        onT[:, half * 8:(half + 1) * 8, :], pT[:].rearrange("p (c t) -> p c t", c=8), AF.Copy),
                        reads=[BpT], writes=[BonT])

            def E4(j):
                c_ = ectx[j]
                onT, BonT, h1, Bh1 = c_["onT"], c_["BonT"], c_["h1"], c_["Bh1"]
                h2, Bh2 = h2s.next()
                c_["h2"], c_["Bh2"] = h2, Bh2
                for half in range(2):
                    po_, Bpo = pos.next()
                    for c in range(16):
                        P.op("pe", lambda e: e.matmul(
                            po_[:], onT[:, c, :], Wo2[:, c, half * 512:(half + 1) * 512], start=(c == 0), stop=(c == 15)),
                            reads=[BonT, BWo2], writes=[Bpo])
                    P.op("dve", lambda e: e.tensor_tensor(
                        h2[:, half * 512:(half + 1) * 512], po_[:], h1[:, half * 512:(half + 1) * 512], ALU.add),
                        reads=[Bpo, Bh1], writes=[Bh2])

            def E5(j):
                c_ = ectx.pop(j)
                h2, Bh2 = c_["h2"], c_["Bh2"]
                tA = j * 128
                yo, Byo = yos.next()
                s2, Bs2 = s2s.next()
                rmsnorm_rows(jk2, h2[:, :], 128, fwb, yo[:, :], s2[:, 0:1], s2[:, 1:2], [Bh2], Bs2, Bjk2, Byo, Bfwb)
                P.dma(None, y_out[tA:tA + 128, :], yo[:], reads=[Byo], writes=[B_x])

            nE = NT // 128
            for t in range(nE + 5):
                for fn_, off_ in ((E5, 5), (E4, 4), (E3, 3), (E2, 2), (E1, 1), (E0, 0)):
                    if 0 <= t - off_ < nE:
                        fn_(t - off_)
            P.flush()
    return nc


def make_consts():
    c = np.zeros((128, 8, 128), np.float32)
    p = np.arange(128)[:, None]
    f = np.arange(128)[None, :]
    c[:, 0, :] = (p == f)
    c[:, 1, :] = 1.0
    c[:, 2, :] = (p <= f)
    c[:, 3, :] = (p >= f)
    c[:, 4, :] = np.where(f >= p, 0.0, -BIG)
    c[:, 5, :] = np.where(f <= p, 0.0, -BIG)
    c[:, 6, :] = np.where(f < p, 0.0, -BIG)
    c[:, 7, :] = np.where(f > p, 0.0, -BIG)
    return c


def shared_inputs(ln_w, na_w_in, na_w_out, gdn_w_in, gdn_conv_w, gdn_a_log, gdn_dt_bias, gdn_norm_w, gdn_w_out,
                  final_norm_w):
    f = lambda a: np.ascontiguousarray(np.asarray(a, dtype=np.float32))
    return {
        "ln_w_bc": f(ln_w),
        "fin_w": f(final_norm_w).reshape(1, D),
        "na_w_in": f(na_w_in[0]),
        "na_w_out": f(na_w_out[0]),
        "g_w_in": f(gdn_w_in[0]),
        "g_cw": f(np.asarray(gdn_conv_w[0]).T.reshape(32, 128, 5).transpose(1, 0, 2)),
        "g_alog": f(gdn_a_log[0]).reshape(1, 16),
        "g_dtb": f(gdn_dt_bias[0]).reshape(1, 16),
        "g_nw": f(gdn_norm_w[0]).reshape(1, DV),
        "g_w_out": f(gdn_w_out[0]),
        "consts": make_consts(),
    }


_PROG_CACHE = {}


def kernel(x_prompt, x_sample, ln_w, na_w_in, na_rpb, na_w_out, gdn_w_in, gdn_conv_w, gdn_a_log, gdn_dt_bias,
           gdn_norm_w, gdn_w_out, final_norm_w):
    seg_t = 2048
    x_prompt = np.asarray(x_prompt, np.float32)
    x_sample = np.asarray(x_sample, np.float32)
    shared = shared_inputs(ln_w, na_w_in, na_w_out, gdn_w_in, gdn_conv_w, gdn_a_log, gdn_dt_bias, gdn_norm_w,
                           gdn_w_out, final_norm_w)
    rpb = np.asarray(na_rpb, np.float32)[0]
    tabs_j = build_na_bias_tables(rpb, seg_t, True).reshape(-1, 128, NH_A, 768)
    tabs_s = build_na_bias_tables(rpb, seg_t, False).reshape(-1, 128, NH_A, 768)
    in_maps = []
    for c in range(8):
        if c < 2:
            xc = np.concatenate([x_prompt[c], x_sample[c]], axis=0)
            m = dict(shared, x=np.ascontiguousarray(xc), na_bias=tabs_j, flag=np.ones((128, 1), np.float32))
        else:
            s0 = 2 + 5 * (c - 2)
            xc = x_sample[s0:s0 + 5].reshape(5 * seg_t, D)
            m = dict(shared, x=np.ascontiguousarray(xc), na_bias=tabs_s, flag=np.zeros((128, 1), np.float32))
        in_maps.append(m)
    if seg_t not in _PROG_CACHE:
        _PROG_CACHE[seg_t] = build_program(seg_t)
    nc = _PROG_CACHE[seg_t]
    res = run_bass_kernel_spmd(nc, in_maps, core_ids=list(range(8)))
    y_prompt = np.empty_like(x_prompt)
    y_sample = np.empty_like(x_sample)
    for c in range(8):
        y = res.results[c]["y"]
        if c < 2:
            y_prompt[c] = y[0:4 * seg_t]
            y_sample[c] = y[4 * seg_t:]
        else:
            s0 = 2 + 5 * (c - 2)
            y_sample[s0:s0 + 5] = y.reshape(5, seg_t, D)
    return (y_prompt, y_sample)
```
